# Optimizing a Trainium2 kernel written in Bass

```python
import math
import numpy as np
import jax
import jax.numpy as jnp
from jax import lax

D_MODEL = 1024
BATCH = 16
SEQ = 256
DEPTH = 2
DEC_BATCH = 8
DEC_SEQ = 1024
PAST_LEN = 512

GRID_W = 64
N_BRANCH = 4
H_A = 4
DK_A = 128
DV_A = 128
CONV_W = 5
CHUNK_A = 64
H_B = 4
DK_B = 64
DV_B = 128
H_C = 4
DK_C = 128
DV_C = 128
CHUNK_C = 16
H_D = 8
DH_D = 64
WIN_R = 8
WIN_C = 16
BRANCH_W = H_A * DV_A
GDN_CONV_CH = 2 * H_A * DK_A + H_A * DV_A
Q_BLOCK = 128
ROPE_BASE = 10000.0
D_FF = ((8 * D_MODEL + 3 * 256 - 1) // (3 * 256)) * 256
EPS = 1e-6
MASK_VALUE = -1e30
F_FLOOR = 1e-30
F32 = jnp.float32
IN_SIZES = (H_A * DK_A, H_A * DK_A, H_A * DV_A, H_A * DV_A, 2 * H_A, 2 * H_A,
            H_B * 2 * DK_B, H_B * 2 * DK_B, H_B * DV_B,
            H_C * DK_C, 2 * H_C * DK_C, H_C * DV_C, H_C * DV_C,
            H_D * DH_D, H_D * DH_D, H_D * DH_D,
            N_BRANCH * D_MODEL)
D_IN = sum(IN_SIZES)

kernel_name = 'hybrid_flow_gdn_diff_hgrn2_na_step'


def _rmsnorm(x, g):
    xf = x.astype(F32)
    y = xf * lax.rsqrt(jnp.mean(xf * xf, axis=-1, keepdims=True) + EPS)
    return (y * g.astype(F32)).astype(x.dtype)


def _l2norm(x):
    return x * lax.rsqrt(jnp.sum(x * x, axis=-1, keepdims=True) + EPS)


def _heads(x, n):
    b, t, _ = x.shape
    return x.reshape(b, t, n, -1).transpose(0, 2, 1, 3)


def _merge_heads(o):
    b, h, t, d = o.shape
    return o.transpose(0, 2, 1, 3).reshape(b, t, h * d)


def _flip(a):
    return jnp.flip(a, axis=2)


def _dwconv_centred(x, w):
    ch, kw = w.shape
    rhs = w.T[:, None, :].astype(x.dtype)
    return lax.conv_general_dilated(x, rhs, window_strides=(1,), padding=[(kw // 2, kw // 2)],
                                    dimension_numbers=('NWC', 'WIO', 'NWC'), feature_group_count=ch)


def _axial_rope(n_tok, dim):
    t = jnp.arange(n_tok)
    n_freq = dim // 4
    inv = ROPE_BASE ** (-jnp.arange(n_freq, dtype=F32) / n_freq)
    ang = jnp.concatenate([(t // GRID_W).astype(F32)[:, None] * inv,
                           (t % GRID_W).astype(F32)[:, None] * inv], axis=-1)
    return jnp.cos(ang), jnp.sin(ang)


def _apply_rope(x, cos, sin):
    xf = x.astype(F32)
    half = xf.shape[-1] // 2
    x1, x2 = xf[..., :half], xf[..., half:]
    return jnp.concatenate([x1 * cos - x2 * sin, x2 * cos + x1 * sin], axis=-1).astype(x.dtype)


def _block_map(fn, q):
    b, h, t = q.shape[:3]
    nb = t // Q_BLOCK
    qb = jnp.moveaxis(q.reshape((b, h, nb, Q_BLOCK) + q.shape[3:]), 2, 0)
    o = jnp.moveaxis(lax.map(fn, qb), 0, 2)
    return o.reshape((b, h, t) + o.shape[4:])


def _softmax_attend(q, k, v):
    scale = q.shape[-1] ** -0.5

    def blk(qi):
        p = jax.nn.softmax(jnp.einsum('bhqd,bhkd->bhqk', qi, k).astype(F32) * scale, axis=-1)
        return jnp.einsum('bhqk,bhkd->bhqd', p, v.astype(F32)).astype(q.dtype)

    return _block_map(blk, q)


def _diff_attend(q, k, v, lam):
    scale = q.shape[-1] ** -0.5

    def blk(qi):
        p = jax.nn.softmax(jnp.einsum('bhqmd,bhkmd->bhmqk', qi, k).astype(F32) * scale, axis=-1)
        pd = p[:, :, 0] - lam * p[:, :, 1]
        return jnp.einsum('bhqk,bhkd->bhqd', pd, v.astype(F32)).astype(q.dtype)

    return _block_map(blk, q)


def _gdn_chunked(q, k, v, g, beta, s0):
    b, h, t, dk = q.shape
    dv = v.shape[-1]
    n = t // CHUNK_A
    q = q.reshape(b, h, n, CHUNK_A, dk)
    k = k.reshape(b, h, n, CHUNK_A, dk)
    v = v.reshape(b, h, n, CHUNK_A, dv)
    beta = beta.reshape(b, h, n, CHUNK_A, 1)
    gc = jnp.cumsum(g.reshape(b, h, n, CHUNK_A), axis=-1)
    tril = jnp.tril(jnp.ones((CHUNK_A, CHUNK_A), bool))
    strict = jnp.tril(jnp.ones((CHUNK_A, CHUNK_A), bool), -1)
    diff = gc[..., :, None] - gc[..., None, :]
    decay = jnp.where(tril, jnp.exp(jnp.where(tril, diff, 0.0)), 0.0)
    kb = k * beta
    lmat = jnp.where(strict, jnp.einsum('bhncd,bhnsd->bhncs', kb, k) * decay, 0.0)
    eye = jnp.eye(CHUNK_A, dtype=F32)
    rhs = jnp.concatenate([v * beta, kb * jnp.exp(gc)[..., None]], axis=-1)
    sol = lax.linalg.triangular_solve(eye + lmat, rhs, left_side=True, lower=True)
    u, w = sol[..., :dv], sol[..., dv:]
    a_intra = jnp.where(tril, jnp.einsum('bhncd,bhnsd->bhncs', q, k) * decay, 0.0)
    q_dec = q * jnp.exp(gc)[..., None]
    k_dec = k * jnp.exp(gc[..., -1:] - gc)[..., None]
    g_last = jnp.exp(gc[..., -1])

    def step(s, xs):
        qd, kd, ui, wi, ai, gl = xs
        v_new = ui - jnp.einsum('bhcd,bhde->bhce', wi, s)
        o = jnp.einsum('bhcd,bhde->bhce', qd, s) + jnp.einsum('bhcs,bhse->bhce', ai, v_new)
        s = s * gl[..., None, None] + jnp.einsum('bhcd,bhce->bhde', kd, v_new)
        return s, o

    xs = tuple(jnp.moveaxis(a, 2, 0) for a in (q_dec, k_dec, u, w, a_intra, g_last))
    s_fin, o = lax.scan(step, s0, xs)
    return jnp.moveaxis(o, 0, 2).reshape(b, h, t, dv), s_fin


def _hgrn_chunked(q, k, v, logf, s0):
    b, h, t, dk = q.shape
    dv = v.shape[-1]
    n = t // CHUNK_C
    q = q.reshape(b, h, n, CHUNK_C, dk)
    k = k.reshape(b, h, n, CHUNK_C, dk)
    v = v.reshape(b, h, n, CHUNK_C, dv)
    bc = jnp.cumsum(logf.reshape(b, h, n, CHUNK_C, dk), axis=3)
    tril = jnp.tril(jnp.ones((CHUNK_C, CHUNK_C), bool))[:, :, None]
    diff = bc[:, :, :, :, None, :] - bc[:, :, :, None, :, :]
    dmat = jnp.where(tril, jnp.exp(jnp.where(tril, diff, 0.0)), 0.0)
    a_intra = jnp.einsum('bhntd,bhnsd,bhntsd->bhnts', q, k, dmat)
    o_intra = jnp.einsum('bhnts,bhnse->bhnte', a_intra, v)
    q_dec = q * jnp.exp(bc)
    k_dec = k * jnp.exp(bc[:, :, :, -1:, :] - bc)
    f_last = jnp.exp(bc[:, :, :, -1, :])

    def step(s, xs):
        qd, kd, vi, fl = xs
        o = jnp.einsum('bhcd,bhde->bhce', qd, s)
        s = s * fl[..., None] + jnp.einsum('bhcd,bhce->bhde', kd, vi)
        return s, o

    xs = tuple(jnp.moveaxis(a, 2, 0) for a in (q_dec, k_dec, v, f_last))
    s_fin, o_inter = lax.scan(step, s0, xs)
    o = o_intra + jnp.moveaxis(o_inter, 0, 2)
    return o.reshape(b, h, t, dv), s_fin


def _gdn_mixer(q, k, v, z, a, bgate, conv_w, a_log, dt_bias, norm_g, s0):
    bsz, t, _ = q.shape
    qkv = jax.nn.silu(_dwconv_centred(jnp.concatenate([q, k, v], axis=-1), conv_w)).astype(F32)
    q, k, v = jnp.split(qkv, [H_A * DK_A, 2 * H_A * DK_A], axis=-1)
    q = _l2norm(_heads(q, H_A)) * DK_A ** -0.5
    k = _l2norm(_heads(k, H_A))
    v = _heads(v, H_A)
    g = -jnp.exp(a_log.astype(F32)) * jax.nn.softplus(a.astype(F32).reshape(bsz, t, 2, H_A) + dt_bias.astype(F32))
    beta = jax.nn.sigmoid(bgate.astype(F32).reshape(bsz, t, 2, H_A))
    g = g.transpose(2, 0, 3, 1)
    beta = beta.transpose(2, 0, 3, 1)
    s0 = s0.astype(F32)
    o_f, s_f = _gdn_chunked(q, k, v, g[0], beta[0], s0[:, 0])
    o_b, s_b = _gdn_chunked(_flip(q), _flip(k), _flip(v), _flip(g[1]), _flip(beta[1]), s0[:, 1])
    o = (o_f + _flip(o_b)).transpose(0, 2, 1, 3)
    o = _rmsnorm(o, norm_g) * jax.nn.silu(z.astype(F32).reshape(bsz, t, H_A, DV_A))
    return o.reshape(bsz, t, H_A * DV_A), jnp.stack([s_f, s_b], axis=1)


def _hgrn_mixer(q, f, i, gate, lb, norm_g, s0):
    bsz, t, _ = q.shape
    q = _heads(jax.nn.silu(q.astype(F32)), H_C)
    v = _heads(i.astype(F32), H_C)
    lb = lb.astype(F32)
    fgate = lb + (1.0 - lb) * jax.nn.sigmoid(f.astype(F32).reshape(bsz, t, 2, H_C * DK_C))
    logf = jnp.log(jnp.maximum(fgate, F_FLOOR))
    logf = logf.reshape(bsz, t, 2, H_C, DK_C).transpose(2, 0, 3, 1, 4)
    k = -jnp.expm1(logf)
    s0 = s0.astype(F32)
    o_f, s_f = _hgrn_chunked(q, k[0], v, logf[0], s0[:, 0])
    o_b, s_b = _hgrn_chunked(_flip(q), _flip(k[1]), _flip(v), _flip(logf[1]), s0[:, 1])
    o = (o_f + _flip(o_b)).transpose(0, 2, 1, 3)
    o = _rmsnorm(o, norm_g) * jax.nn.silu(gate.astype(F32).reshape(bsz, t, H_C, DV_C))
    return o.reshape(bsz, t, H_C * DV_C), jnp.stack([s_f, s_b], axis=1)


def _na_latent(q, k, v, ck, cv, rpb):
    b, h, t, d = q.shape
    rows = t // GRID_W
    wr = min(WIN_R, rows)
    n_ctx = ck.shape[2]
    scale = d ** -0.5
    col = jnp.arange(GRID_W)
    c0 = jnp.clip(col - WIN_C // 2, 0, GRID_W - WIN_C)
    col_ok = (col[None, :] >= c0[:, None]) & (col[None, :] < c0[:, None] + WIN_C)
    dc = jnp.clip(col[None, :] - col[:, None] + WIN_C - 1, 0, 2 * WIN_C - 2)
    mask = jnp.broadcast_to(col_ok[:, None, :], (GRID_W, wr, GRID_W)).reshape(GRID_W, wr * GRID_W)
    rpb = rpb.astype(F32)
    ckf, cvf = ck.astype(F32), cv.astype(F32)

    def row_block(r):
        r0 = jnp.clip(r - wr // 2, 0, rows - wr)
        qi = lax.dynamic_slice_in_dim(q, r * GRID_W, GRID_W, axis=2)
        ki = lax.dynamic_slice_in_dim(k, r0 * GRID_W, wr * GRID_W, axis=2)
        vi = lax.dynamic_slice_in_dim(v, r0 * GRID_W, wr * GRID_W, axis=2)
        dr = r0 + jnp.arange(wr) - r + WIN_R - 1
        bias = rpb[:, dr[None, :, None], dc[:, None, :]].reshape(h, GRID_W, wr * GRID_W)
        s_lat = jnp.einsum('bhqd,bhkd->bhqk', qi, ki).astype(F32) * scale + bias
        s_lat = jnp.where(mask, s_lat, MASK_VALUE)
        s_ctx = jnp.einsum('bhqd,bhkd->bhqk', qi.astype(F32), ckf) * scale
        p = jax.nn.softmax(jnp.concatenate([s_ctx, s_lat], axis=-1), axis=-1)
        o = (jnp.einsum('bhqk,bhkd->bhqd', p[..., :n_ctx], cvf)
             + jnp.einsum('bhqk,bhkd->bhqd', p[..., n_ctx:], vi.astype(F32)))
        return o.astype(q.dtype)

    o = lax.map(row_block, jnp.arange(rows))
    return jnp.moveaxis(o, 0, 2).reshape(b, h, t, d)


def _token_mixers(h, lp, ctx):
    bsz, t, _ = h.shape
    offs = np.cumsum(IN_SIZES)[:-1].tolist()
    (aq, ak, av, az, aa, ab, bq, bk, bv, cq, cf, ci, cg, dq, dk, dv, mg) = jnp.split(h @ lp['w_in'], offs, axis=-1)
    latent = ctx is not None
    if latent:
        st_gdn0, ck_b, cv_b, st_hgrn0, ck_d, cv_d = ctx
    else:
        st_gdn0 = jnp.zeros((bsz, 2, H_A, DK_A, DV_A), F32)
        st_hgrn0 = jnp.zeros((bsz, 2, H_C, DK_C, DV_C), F32)

    o_a, st_gdn = _gdn_mixer(aq, ak, av, az, aa, ab, lp['gdn_conv_w'], lp['gdn_A_log'], lp['gdn_dt_bias'],
                             lp['gdn_norm_g'], st_gdn0)

    q_b = bq.reshape(bsz, t, H_B, 2, DK_B).transpose(0, 2, 1, 3, 4)
    k_b = bk.reshape(bsz, t, H_B, 2, DK_B).transpose(0, 2, 1, 3, 4)
    v_b = _heads(bv, H_B)
    lam_p = lp['diff_lambda'].astype(F32)
    lam = jnp.exp(jnp.sum(lam_p[0] * lam_p[1])) - jnp.exp(jnp.sum(lam_p[2] * lam_p[3])) + lp['lam_init']
    if latent:
        cos, sin = _axial_rope(t, DK_B)
        q_b = _apply_rope(q_b, cos[:, None], sin[:, None])
        keys = jnp.concatenate([ck_b.reshape(bsz, H_B, -1, 2, DK_B).astype(k_b.dtype),
                                _apply_rope(k_b, cos[:, None], sin[:, None])], axis=2)
        vals = jnp.concatenate([cv_b.astype(v_b.dtype), v_b], axis=2)
    else:
        keys, vals = k_b, v_b
    o_b = _diff_attend(q_b, keys, vals, lam)
    o_b = _merge_heads(_rmsnorm(o_b, lp['diff_norm_g']) * (1.0 - lp['lam_init']))

    o_c, st_hgrn = _hgrn_mixer(cq, cf, ci, cg, lp['hgrn_lb'], lp['hgrn_norm_g'], st_hgrn0)

    q_d, k_d, v_d = _heads(dq, H_D), _heads(dk, H_D), _heads(dv, H_D)
    if latent:
        o_d = _na_latent(q_d, k_d, v_d, ck_d, cv_d, lp['na_rpb'])
    else:
        o_d = _softmax_attend(q_d, k_d, v_d)
    o_d = _merge_heads(o_d)

    branches = jnp.stack([o_a.astype(h.dtype), o_b.astype(h.dtype), o_c.astype(h.dtype), o_d.astype(h.dtype)], axis=2)
    proj = jnp.einsum('btnw,nwd->btnd', branches, lp['w_branch'])
    gates = jax.nn.sigmoid(mg.reshape(bsz, t, N_BRANCH, D_MODEL))
    out = jnp.einsum('btnd,btnd->btd', gates, proj) @ lp['w_out']
    new_ctx = None if latent else (st_gdn, k_b.reshape(bsz, H_B, t, 2 * DK_B), v_b, st_hgrn, k_d, v_d)
    return out, new_ctx


def _swiglu(h, wg, wu, wd):
    return (jax.nn.silu(h @ wg) * (h @ wu)) @ wd


def _layer(x, cond, lp, ctx):
    mod = jax.nn.silu(cond) @ lp['w_ada'] + lp['b_ada']
    sh1, sc1, g1, sh2, sc2, g2 = jnp.split(mod[:, None, :], 6, axis=-1)
    h = _rmsnorm(x, lp['norm1_g']) * (1.0 + sc1) + sh1
    mix, new_ctx = _token_mixers(h, lp, ctx)
    x = x + g1 * mix
    h = _rmsnorm(x, lp['norm2_g']) * (1.0 + sc2) + sh2
    x = x + g2 * _swiglu(h, lp['w_ffn_gate'], lp['w_ffn_up'], lp['w_ffn_down'])
    return x, new_ctx


def setup_inputs(seed: int = 0) -> dict:
    key = jax.random.key(seed)
    ks = iter(jax.random.split(key, 40))

    def nrm(shape, s):
        return jax.random.normal(next(ks), shape, jnp.float32) * s

    L = DEPTH
    a_log = jnp.log(jax.random.uniform(next(ks), (L, 2, H_A), jnp.float32, 1.0, 16.0))
    dt = jnp.exp(jax.random.uniform(next(ks), (L, 2, H_A), jnp.float32, math.log(1e-3), math.log(1e-1)))
    return {
        'x_prompt': nrm((BATCH, SEQ, D_MODEL), 1.0),
        'x_sample': nrm((DEC_BATCH, DEC_SEQ, D_MODEL), 1.0),
        'c': nrm((DEC_BATCH, D_MODEL), 1.0),
        'state_gdn': nrm((DEC_BATCH, L, 2, H_A, DK_A, DV_A), 0.1),
        'cache_diff_k': nrm((DEC_BATCH, L, H_B, PAST_LEN, 2 * DK_B), 1.0),
        'cache_diff_v': nrm((DEC_BATCH, L, H_B, PAST_LEN, DV_B), 1.0),
        'state_hgrn': nrm((DEC_BATCH, L, 2, H_C, DK_C, DV_C), 0.1),
        'cache_na_k': nrm((DEC_BATCH, L, H_D, PAST_LEN, DH_D), 1.0),
        'cache_na_v': nrm((DEC_BATCH, L, H_D, PAST_LEN, DH_D), 1.0),
        'c_ctx': nrm((D_MODEL,), 1.0),
        'w_ada': nrm((L, D_MODEL, 6 * D_MODEL), 0.5 * D_MODEL ** -0.5),
        'b_ada': nrm((L, 6 * D_MODEL), 0.02),
        'norm1_g': 1.0 + nrm((L, D_MODEL), 0.1),
        'w_in': nrm((L, D_MODEL, D_IN), D_MODEL ** -0.5),
        'gdn_conv_w': nrm((L, GDN_CONV_CH, CONV_W), CONV_W ** -0.5),
        'gdn_A_log': a_log,
        'gdn_dt_bias': dt + jnp.log(-jnp.expm1(-dt)),
        'gdn_norm_g': 1.0 + nrm((L, DV_A), 0.1),
        'diff_lambda': nrm((L, 4, DK_B), 0.1),
        'diff_norm_g': 1.0 + nrm((L, DV_B), 0.1),
        'hgrn_lb_logits': nrm((L, 2, H_C * DK_C), 1.0),
        'hgrn_norm_g': 1.0 + nrm((L, DV_C), 0.1),
        'na_rpb': nrm((L, H_D, 2 * WIN_R - 1, 2 * WIN_C - 1), 0.1),
        'w_branch': nrm((L, N_BRANCH, BRANCH_W, D_MODEL), BRANCH_W ** -0.5),
        'w_out': nrm((L, D_MODEL, D_MODEL), D_MODEL ** -0.5),
        'norm2_g': 1.0 + nrm((L, D_MODEL), 0.1),
        'w_ffn_gate': nrm((L, D_MODEL, D_FF), D_MODEL ** -0.5),
        'w_ffn_up': nrm((L, D_MODEL, D_FF), D_MODEL ** -0.5),
        'w_ffn_down': nrm((L, D_FF, D_MODEL), D_FF ** -0.5),
        'final_norm_g': 1.0 + nrm((D_MODEL,), 0.1),
    }


def reference(x_prompt, x_sample, c, state_gdn, cache_diff_k, cache_diff_v, state_hgrn, cache_na_k, cache_na_v,
              c_ctx, w_ada, b_ada, norm1_g, w_in, gdn_conv_w, gdn_A_log, gdn_dt_bias, gdn_norm_g,
              diff_lambda, diff_norm_g, hgrn_lb_logits, hgrn_norm_g, na_rpb, w_branch, w_out, norm2_g,
              w_ffn_gate, w_ffn_up, w_ffn_down, final_norm_g):
    probs = jax.nn.softmax(hgrn_lb_logits.astype(F32), axis=0)
    lb_all = jnp.cumsum(probs, axis=0) - probs[0:1]
    xp, xs = x_prompt, x_sample
    ctx_out = []
    for l in range(DEPTH):
        lp = {
            'w_ada': w_ada[l], 'b_ada': b_ada[l], 'norm1_g': norm1_g[l], 'w_in': w_in[l],
            'gdn_conv_w': gdn_conv_w[l], 'gdn_A_log': gdn_A_log[l], 'gdn_dt_bias': gdn_dt_bias[l],
            'gdn_norm_g': gdn_norm_g[l], 'diff_lambda': diff_lambda[l], 'diff_norm_g': diff_norm_g[l],
            'lam_init': 0.8 - 0.6 * math.exp(-0.3 * l), 'hgrn_lb': lb_all[l], 'hgrn_norm_g': hgrn_norm_g[l],
            'na_rpb': na_rpb[l], 'w_branch': w_branch[l], 'w_out': w_out[l], 'norm2_g': norm2_g[l],
            'w_ffn_gate': w_ffn_gate[l], 'w_ffn_up': w_ffn_up[l], 'w_ffn_down': w_ffn_down[l],
        }
        xp, new_l = _layer(xp, c_ctx[None, :], lp, None)
        ctx_out.append(new_l)
        cached = (state_gdn[:, l], cache_diff_k[:, l], cache_diff_v[:, l], state_hgrn[:, l],
                  cache_na_k[:, l], cache_na_v[:, l])
        xs, _ = _layer(xs, c, lp, cached)
    y_prompt = _rmsnorm(xp, final_norm_g)
    y_sample = _rmsnorm(xs, final_norm_g)
    new_state_gdn = jnp.stack([n[0] for n in ctx_out], axis=1)
    new_cache_diff_k = jnp.stack([n[1] for n in ctx_out], axis=1)
    new_cache_diff_v = jnp.stack([n[2] for n in ctx_out], axis=1)
    new_state_hgrn = jnp.stack([n[3] for n in ctx_out], axis=1)
    new_cache_na_k = jnp.stack([n[4] for n in ctx_out], axis=1)
    new_cache_na_v = jnp.stack([n[5] for n in ctx_out], axis=1)
    return (y_prompt, y_sample, new_state_gdn, new_cache_diff_k, new_cache_diff_v, new_state_hgrn, new_cache_na_k, new_cache_na_v)
```

```python
import math
import os
import numpy as np
import ml_dtypes
import concourse.bass as bass
import concourse.mybir as mybir
from concourse.bass_utils import run_bass_kernel_spmd

F32 = mybir.dt.float32
BF16 = mybir.dt.bfloat16
AF = mybir.ActivationFunctionType
ALU = mybir.AluOpType
AX = mybir.AxisListType

D = 1024
DEPTH = 2
D_IN = 11792
D_FF = 2816
EPS = 1e-6
OFF = dict(aq=0, ak=512, av=1024, az=1536, aa=2048, ab=2056, bq=2064, bk=2576, bv=3088, cq=3600, cf=4112,
           ci=5136, cg=5648, dq=6160, dk=6672, dv=7184, mg=7696)
NEG = -30000.0
NDMA_SEMS = 8
STRICT_WAR = True


class _Stop(Exception):
    pass


def ck(i):
    if i > 999:
        raise _Stop()


class Res:
    __slots__ = ("name", "w", "r", "excl")

    def __init__(self, name="r"):
        self.name = name
        self.w = None
        self.r = []
        self.excl = False


class Tile:
    def __init__(self, ap, name="t", res=None):
        self.ap = ap
        self.res = res or Res(name)

    def __getitem__(self, k):
        return self.ap[k]


class Sched:
    ENG = ("pe", "dve", "act", "pool", "sp")

    def __init__(self, nc):
        self.nc = nc
        self.ops = {e: [] for e in self.ENG}
        self.sem = {e: nc.alloc_semaphore("s_" + e) for e in self.ENG}
        self.dsem = {e: [nc.alloc_semaphore(f"d_{e}{i}") for i in range(NDMA_SEMS)] for e in ("sp", "pool", "act")}
        self.dcnt = {e: [0] * NDMA_SEMS for e in self.dsem}
        self.drr = {e: 0 for e in self.dsem}
        self.out_tokens = []
        self.n = 0
        self.sb_bytes = 0

    def sb(self, shape, dtype=F32, name=None):
        self.n += 1
        name = name or f"t{self.n}"
        h = self.nc.alloc_sbuf_tensor(name, list(shape), dtype)
        fb = int(np.prod(shape[1:])) * (2 if dtype == BF16 else 4)
        self.sb_bytes += fb
        return Tile(h[:] if False else h, name)

    def ps(self, shape, dtype=F32, name=None):
        self.n += 1
        name = name or f"p{self.n}"
        t = Tile(self.nc.alloc_psum_tensor(name, list(shape), dtype), name)
        t.res.excl = True
        return t

    @staticmethod
    def _res(x):
        return x.res if isinstance(x, Tile) else x

    def _deps(self, eng, reads, writes):
        waits = []
        for r in reads:
            r = self._res(r)
            if r.w is not None:
                waits.append(r.w)
            if r.excl:
                for t in r.r:
                    if t[0] == "e" and t[1] != eng:
                        waits.append(t)
        for w in writes:
            w = self._res(w)
            if w.w is not None:
                waits.append(w.w)
            for t in w.r:
                if STRICT_WAR or not (t[0] == "e" and t[1] == eng):
                    waits.append(t)
        out = []
        for t in waits:
            if t[0] == "e" and t[1] == eng and eng == "pe":
                continue
            out.append(t)
        return out

    def _commit(self, tok, reads, writes):
        for r in reads:
            rr = self._res(r)
            if tok[0] == "e":
                rr.r = [t for t in rr.r if not (t[0] == "e" and t[1] == tok[1])]
            rr.r.append(tok)
        for w in writes:
            w = self._res(w)
            w.w = tok
            w.r = []

    def op(self, eng, fn, reads=(), writes=()):
        idx = len(self.ops[eng])
        tok = ("e", eng, idx)
        waits = self._deps(eng, reads, writes)
        self.ops[eng].append(dict(fn=fn, waits=waits, kind="c"))
        self._commit(tok, reads, writes)
        return tok

    def dma(self, q, out, in_, reads=(), writes=(), is_output=False):
        k = self.drr[q]
        self.drr[q] = (k + 1) % NDMA_SEMS
        prev = self.dcnt[q][k]
        self.dcnt[q][k] += 16
        tok = ("d", q, k, self.dcnt[q][k])
        waits = self._deps(q, reads, writes)
        if prev:
            waits.append(("d", q, k, prev))
        self.ops[q].append(dict(fn=lambda e, o=out, i=in_: e.dma_start(out=o, in_=i), waits=waits, kind="d",
                                dsem=self.dsem[q][k]))
        self._commit(tok, reads, writes)
        if is_output:
            self.out_tokens.append(tok)
        return tok

    def emit(self):
        nc = self.nc
        self.ops["sp"].append(dict(fn=None, waits=list(self.out_tokens), kind="w"))
        need = {e: set() for e in self.ENG}
        for e in self.ENG:
            for o in self.ops[e]:
                for t in o["waits"]:
                    if t[0] == "e":
                        need[t[1]].add(t[2])
        val = {}
        for e in self.ENG:
            c = 0
            for i, o in enumerate(self.ops[e]):
                if o["kind"] == "c" and i in need[e]:
                    c += 1
                    val[(e, i)] = c
        self.stats = {e: len(self.ops[e]) for e in self.ENG}
        nwait = {e: 0 for e in self.ENG}

        def run(e, engh):
            seen = {}
            for i, o in enumerate(self.ops[e]):
                for t in o["waits"]:
                    if t[0] == "e":
                        key, v, sem = ("e", t[1]), val[(t[1], t[2])], self.sem[t[1]]
                    else:
                        key, v, sem = ("d", t[1], t[2]), t[3], self.dsem[t[1]][t[2]]
                    if seen.get(key, 0) >= v:
                        continue
                    seen[key] = v
                    engh.wait_ge(sem, v)
                    nwait[e] += 1
                if o["fn"] is None:
                    continue
                ins = o["fn"](engh)
                if o["kind"] == "d":
                    ins.then_inc(o["dsem"], 16)
                elif i in need[e]:
                    ins.then_inc(self.sem[e], 1)

        with nc.Block() as block:
            @block.tensor
            def _(e):
                run("pe", e)

            @block.vector
            def _(e):
                run("dve", e)

            @block.scalar
            def _(e):
                run("act", e)

            @block.gpsimd
            def _(e):
                run("pool", e)

            @block.sync
            def _(e):
                run("sp", e)
        self.nwait = nwait


class Arena:
    def __init__(self, S, nbytes, name):
        self.S = S
        self.n = nbytes // 2
        self.t = S.nc.alloc_sbuf_tensor(name, [128, self.n], BF16)
        S.sb_bytes += nbytes
        self.live = []
        self.off = 0
        self.name = name

    def reset(self):
        toks = list(self.pending)
        for r in self.live:
            if r.w is not None:
                toks.append(r.w)
            toks.extend(r.r)
        self.pending = list(dict.fromkeys(toks))
        self.live = []
        self.off = 0

    pending = []

    def mark(self):
        return (self.off, len(self.live))

    def release(self, mark):
        toks = list(self.pending)
        for r in self.live[mark[1]:]:
            if r.w is not None:
                toks.append(r.w)
            toks.extend(r.r)
        self.pending = list(dict.fromkeys(toks))
        self.live = self.live[:mark[1]]
        self.off = mark[0]

    def view(self, shape, dtype=BF16, name="v"):
        nel = int(np.prod(shape))
        nb = nel * (2 if dtype == BF16 else 4)
        nb = (nb + 3) // 4 * 4
        assert self.off + nb // 2 <= self.n, f"arena {self.name} overflow: {self.off * 2 + nb} > {self.n * 2} ({name})"
        ap = self.t[:, self.off:self.off + nb // 2]
        if dtype != BF16:
            ap = ap.bitcast(dtype)
        ap = ap[:, 0:nel]
        if len(shape) == 2:
            ap = ap.rearrange("p (a b) -> p a b", a=shape[0])
        elif len(shape) == 3:
            ap = ap.rearrange("p (a b c) -> p a b c", a=shape[0], b=shape[1])
        elif len(shape) == 4:
            ap = ap.rearrange("p (a b c d) -> p a b c d", a=shape[0], b=shape[1], c=shape[2])
        self.off += nb // 2
        r = Res(name)
        r.r = list(self.pending)
        self.live.append(r)
        return Tile(ap, name, r)


def _consts():
    c = {}
    c["ident"] = np.eye(128, dtype=np.float32)
    c["ones"] = np.ones((128, 128), np.float32)
    r = np.arange(128)
    same = (r[:, None] // 64) == (r[None, :] // 64)
    c["tri2_f"] = (same & (r[:, None] <= r[None, :])).astype(np.float32)
    c["tri2_b"] = (same & (r[:, None] >= r[None, :])).astype(np.float32)
    c["ones2"] = same.astype(np.float32)
    c["csel0"] = np.repeat((r < 64).astype(np.float32)[:, None], 128, 1)
    c["csel1"] = np.repeat((r >= 64).astype(np.float32)[:, None], 128, 1)
    vf = same & (r[:, None] <= r[None, :])
    vfs = same & (r[:, None] < r[None, :])
    vb = same & (r[:, None] >= r[None, :])
    vbs = same & (r[:, None] > r[None, :])
    c["nm_f_incl"] = np.where(vf, 0.0, NEG).astype(np.float32)
    c["nm_f_str"] = np.where(vfs, 0.0, NEG).astype(np.float32)
    c["nm_b_incl"] = np.where(vb, 0.0, NEG).astype(np.float32)
    c["nm_b_str"] = np.where(vbs, 0.0, NEG).astype(np.float32)
    c["pm_f_str"] = np.where(vbs, 0.0, -NEG).astype(np.float32)
    c["pm_b_str"] = np.where(vfs, 0.0, -NEG).astype(np.float32)
    c["m01_f"] = (r[:, None] <= r[None, :]).astype(np.float32)
    c["m01_b"] = (r[:, None] >= r[None, :]).astype(np.float32)
    t = np.arange(1024)
    c["smask128"] = np.repeat((t % 128 != 0).astype(np.float32)[None, :], 128, 0)
    c["smask16"] = np.repeat((t % 16 != 0).astype(np.float32)[None, :], 128, 0)
    nf = 16
    inv = 10000.0 ** (-np.arange(nf, dtype=np.float32) / nf)
    tt = np.arange(1024)
    ang = np.concatenate([(tt // 64).astype(np.float32)[:, None] * inv, (tt % 64).astype(np.float32)[:, None] * inv], -1)
    c["rope_cos"] = np.cos(ang).astype(np.float32).reshape(8, 128, 32).transpose(1, 0, 2).copy()
    c["rope_sin"] = np.sin(ang).astype(np.float32).reshape(8, 128, 32).transpose(1, 0, 2).copy()
    c["eps"] = np.full((128, 1), EPS, np.float32)
    ii = np.arange(8)[:, None]; ss = np.arange(128)[None, :]
    c["hm_f"] = np.repeat(np.where(ss < 16 * (ii + 1), 0.0, NEG).astype(np.float32)[None], 128, 0)
    c["hm_b"] = np.repeat(np.where(ss >= 16 * ii, 0.0, NEG).astype(np.float32)[None], 128, 0)
    return c


NA_PAIRS = []
for _j in range(8):
    _rows = []
    for _r in (2 * _j, 2 * _j + 1):
        _r0 = min(max(_r - 4, 0), 8)
        _rows += list(range(_r0, _r0 + 8))
    for _i in sorted(set(x // 2 for x in _rows)):
        NA_PAIRS.append((_j, _i))


def _na_masks():
    col = np.arange(64)
    c0 = np.clip(col - 8, 0, 48)
    colok = (col[None, :] >= c0[:, None]) & (col[None, :] < c0[:, None] + 16)
    m = np.zeros((len(NA_PAIRS), 128, 128), np.float32)
    for p, (j, i) in enumerate(NA_PAIRS):
        for a in range(2):
            for b in range(2):
                kr, qr = 2 * i + a, 2 * j + b
                r0 = min(max(qr - 4, 0), 8)
                rowok = r0 <= kr < r0 + 8
                blk = np.where(colok.T & rowok, 0.0, NEG * 8)
                m[p, a * 64:(a + 1) * 64, b * 64:(b + 1) * 64] = blk
    return m.transpose(1, 0, 2).copy()


class Prog:
    def __init__(self, debug=(), stage=99):
        self.debug = set(debug)
        self.stage = stage
        nc = self.nc = bass.Bass("TRN2", target_bir_lowering=False)
        self.S = Sched(nc)
        self.dbg_outs = {}
        self.ins = {}
        self.outs = {}

    def din(self, name, shape):
        self.ins[name] = self.nc.dram_tensor(name, list(shape), F32, kind="ExternalInput").ap()
        return self.ins[name]

    def dout(self, name, shape):
        self.outs[name] = self.nc.dram_tensor(name, list(shape), F32, kind="ExternalOutput").ap()
        return self.outs[name]

    def dbg(self, name, tile, ap, shape):
        if name not in self.debug:
            return
        o = self.nc.dram_tensor("dbg_" + name, list(shape), ap.dtype, kind="ExternalOutput").ap()
        self.dbg_outs[name] = o
        self.S.dma("sp", o, ap, reads=(tile if isinstance(tile, list) else [tile]), is_output=True)

    def mm(self, out, lhsT, rhs, start, stop, R, W):
        self.S.op("pe", lambda e: e.matmul(out, lhsT=lhsT, rhs=rhs, start=start, stop=stop), R, W)

    def tr(self, out, in_, ident, R, W):
        self.S.op("pe", lambda e: e.transpose(out, in_, ident), R, W)

    def tt(self, eng, out, in0, in1, op, R, W):
        self.S.op(eng, lambda e: e.tensor_tensor(out=out, in0=in0, in1=in1, op=op), R, W)

    def ts(self, eng, out, in0, s1, op0, R, W, s2=None, op1=None):
        if op1 is None:
            self.S.op(eng, lambda e: e.tensor_scalar(out=out, in0=in0, scalar1=s1, scalar2=None, op0=op0), R, W)
        else:
            self.S.op(eng, lambda e: e.tensor_scalar(out=out, in0=in0, scalar1=s1, scalar2=s2, op0=op0, op1=op1), R, W)

    def stt(self, out, in0, scalar, in1, op0, op1, R, W):
        self.S.op("dve", lambda e: e.scalar_tensor_tensor(out=out, in0=in0, scalar=scalar, in1=in1, op0=op0, op1=op1), R, W)

    def act(self, out, in_, func, R, W, bias=None, scale=1.0, accum=None):
        kw = {}
        if bias is not None:
            kw["bias"] = bias
        if accum is not None:
            kw["accum_out"] = accum
        self.S.op("act", lambda e: e.activation(out, in_, func, scale=scale, **kw), R, W)

    def cp(self, eng, out, in_, R, W):
        if eng == "act":
            self.S.op("act", lambda e: e.copy(out, in_), R, W)
        else:
            self.S.op(eng, lambda e: e.tensor_copy(out, in_), R, W)

    def recip(self, out, in_, R, W):
        self.S.op("dve", lambda e: e.reciprocal(out, in_), R, W)

    def build(self):
        nc, S = self.nc, self.S
        din, dout = self.din, self.dout
        mm, tr, tt, ts, stt, act, cp = self.mm, self.tr, self.tt, self.ts, self.stt, self.act, self.cp
        xp = din("xp", [512, D]); xs = din("xs", [1024, D])
        c_s = din("c", [1, D]); c_ctx = din("c_ctx", [1, D])
        st_gdn = din("state_gdn", [2, 2, 4, 128, 128]); st_hg = din("state_hgrn", [2, 2, 4, 128, 128])
        ck_b = din("cache_diff_k", [2, 4, 512, 128]); cv_b = din("cache_diff_v", [2, 4, 512, 128])
        ck_d = din("cache_na_k", [2, 8, 512, 64]); cv_d = din("cache_na_v", [2, 8, 512, 64])
        w_ada = din("w_ada", [2, D, 6 * D]); b_ada = din("b_ada", [2, 6 * D])
        n1g = din("norm1_g", [2, D]); n2g = din("norm2_g", [2, D]); fng = din("final_norm_g", [1, D])
        w_in = din("w_in", [2, D, D_IN]); conv_w = din("gdn_conv_w", [2, 1536, 5])
        a_log = din("gdn_A_log", [2, 8]); dt_b = din("gdn_dt_bias", [2, 8])
        gdn_ng = din("gdn_norm_g", [2, 128]); dlam = din("diff_lambda", [2, 256]); diff_ng = din("diff_norm_g", [2, 128])
        lb_log = din("hgrn_lb_logits", [2, 2, 512]); hg_ng = din("hgrn_norm_g", [2, 128])
        rpb = din("na_rpb", [2, 120, 31])
        w_br = din("w_branch", [2, 4, 512, D]); w_out = din("w_out", [2, D, D])
        w_fg = din("w_ffn_gate", [2, D, D_FF]); w_fu = din("w_ffn_up", [2, D, D_FF]); w_fd = din("w_ffn_down", [2, D_FF, D])
        cdram = {k: din("k_" + k, list(v.shape)) for k, v in _consts().items()}
        na_mask_d = din("k_na_mask", [128, len(NA_PAIRS), 128])
        zeros_d = din("k_zeros", [128, 2048])
        yp = dout("yp", [512, D]); ys = dout("ys", [1024, D])
        o_st_gdn = dout("o_st_gdn", [2, 2, 2, 4, 128, 128]); o_st_hg = dout("o_st_hg", [2, 2, 2, 4, 128, 128])
        o_ck_b = dout("o_ck_b", [2, 2, 4, 256, 128]); o_cv_b = dout("o_cv_b", [2, 2, 4, 256, 128])
        o_ck_d = dout("o_ck_d", [2, 2, 8, 256, 64]); o_cv_d = dout("o_cv_d", [2, 2, 8, 256, 64])
        rpb_scr = nc.dram_tensor("rpb_scr", [120, 64, 127], F32).ap()
        self.rpb_scr_res = Res("rpb_scr")

        K = {}
        for k, v in _consts().items():
            if k in ("smask128", "smask16", "hm_f", "hm_b"):
                continue
            K[k] = S.sb(list(v.shape), F32, "sk_" + k)
            S.dma("sp", K[k][:], cdram[k], writes=[K[k]])
        identb = S.sb([128, 128], BF16, "identb")
        S.dma("pool", identb[:], cdram["ident"], writes=[identb])
        onesb = S.sb([128, 128], BF16, "onesb")
        S.dma("pool", onesb[:], cdram["ones"], writes=[onesb])
        sm128 = S.sb([128, 1024], BF16, "sm128"); sm16 = S.sb([128, 1024], BF16, "sm16")
        S.dma("pool", sm128[:], cdram["smask128"], writes=[sm128]); S.dma("pool", sm16[:], cdram["smask16"], writes=[sm16])
        namask = S.sb([128, len(NA_PAIRS), 128], BF16, "namask")
        S.dma("pool", namask[:], na_mask_d, writes=[namask])
        onecol = S.sb([128, 1], F32, "onecol")
        S.dma("sp", onecol[:], cdram["ones"][:, 0:1], writes=[onecol])
        self.K = K
        identf, onesf, epsc = K["ident"], K["ones"], K["eps"]

        banks = [S.ps([128, 512], F32, f"bank{i}") for i in range(8)]
        self._bk = 0

        def bank():
            b = banks[self._bk]
            self._bk = (self._bk + 1) % 8
            return b

        X = S.sb([128, 8, D], F32, "X")
        HT = S.sb([128, 8, 1024], BF16, "HT")
        OTall = S.sb([128, 16384], BF16, "OTall")
        _slot = [0, 2, 1, 3]
        OT = [OTall.ap[:, _slot[n] * 4096:(_slot[n] + 1) * 4096].rearrange("p (a b) -> p a b", a=4) for n in range(4)]
        RS = [Res(f"otslot{i}") for i in range(4)]
        R_OT = [[RS[_slot[n]]] for n in range(4)]
        R_OACC = [RS[2], RS[3]]
        R_WD = [RS[0], RS[1], RS[2]]
        zcol = S.sb([128, 8], F32, "zcol")
        S.dma("sp", zcol[:], zeros_d[:, 0:8], writes=[zcol])
        NW = 3
        wring = [S.sb([128, 4096], BF16, f"w{i}") for i in range(NW)]
        self._wr = 0

        def wslot():
            w = wring[self._wr]
            self._wr = (self._wr + 1) % NW
            return w
        modc = S.sb([128, 48], F32, "modc"); badac = S.sb([128, 48], F32, "badac")
        n1c = S.sb([128, 8], F32, "n1c"); n2c = S.sb([128, 8], F32, "n2c")
        s1c = S.sb([128, 8], F32, "s1c"); s2c = S.sb([128, 8], F32, "s2c")
        condT = S.sb([128, 8], F32, "condT"); scb = S.sb([128, 8], BF16, "scb")
        GB = S.sb([128, D], F32, "GB")
        gcolb = [S.sb([128, 128], F32, f"gcolb{i}") for i in range(2)]
        xn = [S.sb([128, D], BF16, f"xn{i}") for i in range(2)]
        sm = [S.sb([128, 8], F32, f"sm{i}") for i in range(4)]
        self._sm = 0

        def small():
            t = sm[self._sm]
            self._sm = (self._sm + 1) % 4
            return t
        AR = Arena(S, 67 * 1024, "arena")

        def load_w(dram_ap, shape_free, view_fn=None):
            w = wslot()
            nel = int(np.prod(shape_free))
            ap = w.ap[:, 0:nel]
            if len(shape_free) == 2:
                ap = ap.rearrange("p (a b) -> p a b", a=shape_free[0])
            elif len(shape_free) == 3:
                ap = ap.rearrange("p (a b c) -> p a b c", a=shape_free[0], b=shape_free[1])
            S.dma("pool", ap, dram_ap, writes=[w])
            return w, ap

        def w_kc(wmat, c0, nc_):
            return wmat[:, c0:c0 + nc_].rearrange("(kc p) n -> p kc n", p=128)

        MOD = [[S.sb([128, 48], F32, f"mod{l}{gi}") for gi in range(2)] for l in range(2)]
        S1 = [[S.sb([128, 8], F32, f"s1_{l}{gi}") for gi in range(2)] for l in range(2)]
        S2 = [[S.sb([128, 8], F32, f"s2_{l}{gi}") for gi in range(2)] for l in range(2)]
        condT2 = S.sb([128, 8, 2], F32, "condT2"); scb2 = S.sb([128, 8, 2], BF16, "scb2")

        def compute_mod_all():
            S.dma("sp", condT2[:, :, 0], c_ctx.rearrange("o (kc p) -> p (o kc)", p=128), writes=[condT2])
            S.dma("sp", condT2[:, :, 1], c_s.rearrange("o (kc p) -> p (o kc)", p=128), writes=[condT2])
            act(scb2[:], condT2[:], AF.Silu, [condT2], [scb2])
            for l in range(DEPTH):
                S.dma("sp", badac[:], b_ada[l:l + 1, :].rearrange("o (j p) -> p (o j)", p=128), writes=[badac])
                S.dma("sp", n1c[:], n1g[l:l + 1, :].rearrange("o (j p) -> p (o j)", p=128), writes=[n1c])
                S.dma("sp", n2c[:], n2g[l:l + 1, :].rearrange("o (j p) -> p (o j)", p=128), writes=[n2c])
                b = bank()
                for ch in range(12):
                    w, wv = load_w(w_kc(w_ada[l], ch * 512, 512), [8, 512])
                    for jj in range(4):
                        j = ch * 4 + jj
                        for kc in range(8):
                            mm(b[:, 2 * j:2 * j + 2], wv[:, kc, jj * 128:(jj + 1) * 128], scb2[:, kc, :], kc == 0, kc == 7, [w, scb2], [b])
                for gi in range(2):
                    m_ = MOD[l][gi]
                    tt("dve", m_[:], b[:, gi:96:2], badac[:], ALU.add, [b, badac], [m_])
                    stt(S1[l][gi][:], m_[:, 8:16], 1.0, n1c[:], ALU.add, ALU.mult, [m_, n1c], [S1[l][gi]])
                    stt(S2[l][gi][:], m_[:, 32:40], 1.0, n2c[:], ALU.add, ALU.mult, [m_, n2c], [S2[l][gi]])

        def gate_bcast(which):
            for half in range(2):
                b = bank()
                for q in range(4):
                    fc = half * 4 + q
                    gt = gcolb[fc % 2]
                    cp("dve", gt[:], self.cm[:, which * 8 + fc:which * 8 + fc + 1].to_broadcast([128, 128]), [self.cm], [gt])
                    mm(b[:, q * 128:(q + 1) * 128], gt[:], identf[:], True, True, [gt, identf], [b])
                cp("act", GB[:, half * 512:(half + 1) * 512], b[:], [b], [GB])

        def norm_to_hT(NT, scol, shcol_ap, shcol_t):
            for t in range(NT):
                junk = xn[(t + 1) % 2]
                st = small()
                act(junk[:], X[:, t, :], AF.Square, [X], [junk, st], accum=st[:, 0:1])
                act(st[:, 1:2], st[:, 0:1], AF.Ln, [st, epsc], [st], bias=epsc[:], scale=1.0 / D)
                act(st[:, 2:3], st[:, 1:2], AF.Exp, [st], [st], scale=-0.5)
                xt = xn[t % 2]
                ts("dve", xt[:], X[:, t, :], st[:, 2:3], ALU.mult, [X, st], [xt])
                b = bank()
                bb = b.ap[:].bitcast(BF16)
                for kc in range(8):
                    tr(bb[:, kc * 128:(kc + 1) * 128], xt[:, kc * 128:(kc + 1) * 128], identb[:], [xt, identb], [b])
                for kc in range(8):
                    ts("dve", HT[:, kc, t * 128:(t + 1) * 128], bb[:, kc * 128:(kc + 1) * 128], scol[:, kc:kc + 1], ALU.mult,
                       [b, scol, shcol_t], [HT], s2=shcol_ap[:, kc:kc + 1], op1=ALU.add)

        def final_norm(NT, ydram):
            fg = AR.view([D], F32, "fng_b")
            S.dma("sp", fg[:], fng.partition_broadcast(128).rearrange("p o d -> p (o d)") if False else fng[0:1, :].partition_broadcast(128).rearrange("p o d -> p (o d)"), writes=[fg])
            for t in range(NT):
                junk = xn[(t + 1) % 2]
                st = small()
                act(junk[:], X[:, t, :], AF.Square, [X], [junk, st], accum=st[:, 0:1])
                act(st[:, 1:2], st[:, 0:1], AF.Ln, [st, epsc], [st], bias=epsc[:], scale=1.0 / D)
                act(st[:, 2:3], st[:, 1:2], AF.Exp, [st], [st], scale=-0.5)
                yo = AR.view([D], F32, "yo")
                stt(yo[:], X[:, t, :], st[:, 2:3], fg[:], ALU.mult, ALU.mult, [X, st, fg], [yo])
                S.dma("sp", ydram[t * 128:(t + 1) * 128, :], yo[:], reads=[yo], is_output=True)

        def ffn(l, NT, T):
            AR.reset()
            actT = AR.view([22, T], BF16, "actT")
            sgt = [AR.view([512], F32, f"sg{i}") for i in range(2)]
            gate_bcast(5)
            for ch in range(6):
                ncols = 512 if ch < 5 else 256
                wg, wgv = load_w(w_kc(w_fg[l], ch * 512, ncols), [8, ncols])
                wu, wuv = load_w(w_kc(w_fu[l], ch * 512, ncols), [8, ncols])
                for fc in range(ncols // 128):
                    ffc = ch * 4 + fc
                    for tg in range(T // 512):
                        bg = bank(); bu = bank()
                        for kc in range(8):
                            mm(bg[:, :], wgv[:, kc, fc * 128:(fc + 1) * 128], HT[:, kc, tg * 512:(tg + 1) * 512], kc == 0, kc == 7, [wg, HT], [bg])
                        for kc in range(8):
                            mm(bu[:, :], wuv[:, kc, fc * 128:(fc + 1) * 128], HT[:, kc, tg * 512:(tg + 1) * 512], kc == 0, kc == 7, [wu, HT], [bu])
                        sg = sgt[(ffc * 2 + tg) % 2]
                        act(sg[:], bg[:, :], AF.Silu, [bg], [sg])
                        tt("dve", actT[:, ffc, tg * 512:(tg + 1) * 512], sg[:], bu[:, :], ALU.mult, [sg, bu], [actT])
            wd = OTall.ap[:, 0:22 * 512].rearrange("p (f n) -> p f n", f=22)
            tmpx = [AR.view([512], F32, f"tmpx{i}") for i in range(2)]
            for half in range(2):
                S.dma("pool", wd, w_fd[l][:, half * 512:(half + 1) * 512].rearrange("(f p) n -> p f n", p=128), writes=R_WD)
                for t in range(NT):
                    b = bank()
                    for ffc in range(22):
                        mm(b[:, :], actT[:, ffc, t * 128:(t + 1) * 128], wd[:, ffc, :], ffc == 0, ffc == 21, [actT] + R_WD, [b])
                    tx = tmpx[t % 2]
                    tt("dve", tx[:], b[:, :], GB[:, half * 512:(half + 1) * 512], ALU.mult, [b, GB], [tx])
                    tt("dve", X[:, t, half * 512:(half + 1) * 512], X[:, t, half * 512:(half + 1) * 512], tx[:], ALU.add, [X, tx], [X])

        def merge_out(l, NT, T):
            AR.reset()
            MT = AR.view([8, T], BF16, "mergedT")
            sgt = [AR.view([512], F32, f"msg{i}") for i in range(2)]
            acc = [AR.view([512], F32, f"macc{i}") for i in range(2)]
            gate_bcast(2)
            for fc in range(8):
                wm = wslot()
                wmv = wm.ap[:, 0:4096].rearrange("p (a b c) -> p a b c", a=8, b=4)
                wb = wslot()
                wbv = wb.ap[:, 0:2048].rearrange("p (a b c) -> p a b c", a=4, b=4)
                for n in range(4):
                    c0 = OFF["mg"] + n * 1024 + fc * 128
                    S.dma("pool", wmv[:, :, n, :], w_in[l][:, c0:c0 + 128].rearrange("(kc p) c -> p kc c", p=128), writes=[wm])
                    S.dma("pool", wbv[:, n, :, :], w_br[l, n][:, fc * 128:(fc + 1) * 128].rearrange("(kc p) c -> p kc c", p=128), writes=[wb])
                for tg in range(T // 512):
                    tsl = slice(tg * 512, (tg + 1) * 512)
                    ac = acc[tg % 2]
                    for n in range(4):
                        bgt = bank(); bp = bank()
                        for kc in range(8):
                            mm(bgt[:, :], wmv[:, kc, n, :], HT[:, kc, tsl], kc == 0, kc == 7, [wm, HT], [bgt])
                        for kc in range(4):
                            mm(bp[:, :], wbv[:, n, kc, :], OT[n][:, kc, tsl], kc == 0, kc == 3, [wb] + R_OT[n], [bp])
                        sg = sgt[n % 2]
                        act(sg[:], bgt[:, :], AF.Sigmoid, [bgt], [sg])
                        if n == 0:
                            tt("dve", ac[:], sg[:], bp[:, :], ALU.mult, [sg, bp], [ac])
                        else:
                            tt("dve", sg[:], sg[:], bp[:, :], ALU.mult, [sg, bp], [sg])
                            if n < 3:
                                tt("dve", ac[:], ac[:], sg[:], ALU.add, [ac, sg], [ac])
                            else:
                                tt("dve", MT[:, fc, tsl], ac[:], sg[:], ALU.add, [ac, sg], [MT])
            tmpx = [AR.view([512], F32, f"mtmpx{i}") for i in range(2)]
            for half in range(2):
                wo, wov = load_w(w_kc(w_out[l], half * 512, 512), [8, 512])
                for t in range(NT):
                    b = bank()
                    for kc in range(8):
                        mm(b[:, :], MT[:, kc, t * 128:(t + 1) * 128], wov[:, kc, :], kc == 0, kc == 7, [MT, wo], [b])
                    tx = tmpx[t % 2]
                    tt("dve", tx[:], b[:, :], GB[:, half * 512:(half + 1) * 512], ALU.mult, [b, GB], [tx])
                    tt("dve", X[:, t, half * 512:(half + 1) * 512], X[:, t, half * 512:(half + 1) * 512], tx[:], ALU.add, [X, tx], [X])

        def proj_fm(w, wv, c0, tok0, ntok):
            b = bank()
            for kc in range(8):
                mm(b[:, 0:ntok], wv[:, kc, c0:c0 + 128], HT[:, kc, tok0:tok0 + ntok], kc == 0, kc == 7, [w, HT], [b])
            return b

        def proj_tm(w, wv, c0, ncols, t):
            b = bank()
            for kc in range(8):
                mm(b[:, 0:ncols], HT[:, kc, t * 128:(t + 1) * 128], wv[:, kc, c0:c0 + ncols], kc == 0, kc == 7, [HT, w], [b])
            return b

        def head_rms_out(l, g, oacc, ng_dram, gate_col, OTn, R_OTn, extra_scale=None):
            NT = g["NT"]
            ngb = AR.view([128], F32, "ngb")
            S.dma("sp", ngb[:], ng_dram[l:l + 1, :].partition_broadcast(128).rearrange("p o d -> p (o d)"), writes=[ngb])
            if extra_scale is not None:
                ts("dve", ngb[:], ngb[:], float(extra_scale), ALU.mult, [ngb], [ngb])
            gzs = [AR.view([4, 128], F32, f"gz{i}") for i in range(2)]
            obs = [AR.view([512], BF16, f"ob{i}") for i in range(2)]
            if gate_col is not None:
                wz, wzv = load_w(w_kc(w_in[l], gate_col, 512), [8, 512])
            def tile_gen(t):
                gz = gzs[t % 2]
                if gate_col is not None:
                    b = proj_tm(wz, wzv, 0, 512, t)
                    yield
                    act(gz[:].rearrange("p a b -> p (a b)"), b[:, :], AF.Silu, [b], [gz])
                st = sts[t % 2]
                junk = xn[t % 2]
                for h in range(4):
                    act(junk[:, h * 128:(h + 1) * 128], oacc[:, t, h * 128:(h + 1) * 128], AF.Square, R_OACC, [junk, st], accum=st[:, h:h + 1])
                act(st[:, 4:8], st[:, 0:4], AF.Ln, [st, epsc], [st], bias=epsc[:], scale=1.0 / 128)
                act(st[:, 4:8], st[:, 4:8], AF.Exp, [st], [st], scale=-0.5)
                yield
                ob = obs[t % 2]
                o3 = oacc[:, t, :].rearrange("p (a b) -> p a b", a=4)
                if gate_col is not None:
                    tt("dve", gz[:], gz[:], ngb[:].unsqueeze(1).to_broadcast([128, 4, 128]), ALU.mult, [gz, ngb], [gz])
                    tt("dve", gz[:], gz[:], st[:, 4:8].unsqueeze(2).to_broadcast([128, 4, 128]), ALU.mult, [gz, st], [gz])
                    tt("dve", ob[:].rearrange("p (a b) -> p a b", a=4), o3, gz[:], ALU.mult, R_OACC + [gz], [ob])
                else:
                    tt("dve", gz[:], o3, st[:, 4:8].unsqueeze(2).to_broadcast([128, 4, 128]), ALU.mult, R_OACC + [st], [gz])
                    tt("dve", ob[:].rearrange("p (a b) -> p a b", a=4), gz[:], ngb[:].unsqueeze(1).to_broadcast([128, 4, 128]), ALU.mult, [gz, ngb], [ob])
                yield
                b = bank()
                bb = b.ap[:].bitcast(BF16)
                for kc in range(4):
                    tr(bb[:, kc * 128:(kc + 1) * 128], ob[:, kc * 128:(kc + 1) * 128], identb[:], [ob, identb], [b])
                yield
                cp("act", OTn[:, :, t * 128:(t + 1) * 128], bb[:, 0:512].rearrange("p (a b) -> p a b", a=4), [b], R_OTn)
                yield

            sts = [AR.view([8], F32, f"hst{i}") for i in range(2)]
            for t0_ in range(0, NT, 2):
                gens = [tile_gen(t) for t in range(t0_, min(NT, t0_ + 2))]
                while gens:
                    for gg in list(gens):
                        try:
                            next(gg)
                        except StopIteration:
                            gens.remove(gg)

        def mixer_C(l, g):
            NT, T = g["NT"], g["T"]
            AR.reset()
            oacc = OTall.ap[:, 8192:16384].bitcast(F32)[:, 0:NT * 512].rearrange("p (t c) -> p t c", c=512)
            qb = AR.view([4, T], BF16, "c_qb")
            vtm = AR.view([NT, 512], BF16, "c_vtm")
            Bc = AR.view([NT, 128], F32, "c_B"); Bw = AR.view([NT, 128], F32, "c_Bw")
            tmpf = AR.view([NT, 128], F32, "c_tmp")
            logf = tmpf
            kf = AR.view([NT, 128], BF16, "c_kf")
            te8 = [AR.view([8, 128], F32, f"c_te8{i}") for i in range(2)]
            ks8 = [AR.view([8, 128], BF16, "c_ks8")]
            ATs = [AR.view([128], BF16, f"c_AT{i}") for i in range(2)]
            kdtm = [AR.view([128], BF16, f"c_kdtm{i}") for i in range(2)]
            S32 = AR.view([128], F32, "c_S32"); S16 = AR.view([128], BF16, "c_S16")
            tot16 = AR.view([NT * 8], F32, "c_tot16")
            lbl = AR.view([2, 8], F32, "c_lbl"); lbc = AR.view([8], F32, "c_lbc"); oml = AR.view([8], F32, "c_oml")
            for ll in range(2):
                S.dma("sp", lbl[:, ll, :], lb_log[ll].rearrange("d (h p) -> p (d h)", p=128), writes=[lbl])
            tt("dve", lbc[:], lbl[:, 1, :], lbl[:, 0, :], ALU.subtract, [lbl], [lbc])
            act(lbc[:], lbc[:], AF.Sigmoid, [lbc], [lbc])
            ts("dve", lbc[:], lbc[:], float(l), ALU.mult, [lbc], [lbc])
            ts("dve", oml[:], lbc[:], -1.0, ALU.mult, [lbc], [oml], s2=1.0, op1=ALU.add)
            wq, wqv = load_w(w_kc(w_in[l], OFF["cq"], 512), [8, 512])
            for h in range(4):
                for tg in range(T // 512):
                    b = proj_fm(wq, wqv, h * 128, tg * 512, 512)
                    act(qb[:, h, tg * 512:(tg + 1) * 512], b[:, :], AF.Silu, [b], [qb])
            wv_, wvv = load_w(w_kc(w_in[l], OFF["ci"], 512), [8, 512])
            for t in range(NT):
                b = proj_tm(wv_, wvv, 0, 512, t)
                cp("act", vtm[:, t, :], b[:, :], [b], [vtm])
            logf2 = logf[:].rearrange("p a b -> p (a b)"); B2 = Bc[:].rearrange("p a b -> p (a b)"); Bw2 = Bw[:].rearrange("p a b -> p (a b)")
            tmp2 = tmpf[:].rearrange("p a b -> p (a b)")
            sets = []
            for i in range(2):
                sets.append(dict(kfl=AR.view([NT, 128], F32, f"c_kfl{i}"), qs=AR.view([NT, 128], BF16, f"c_qs{i}"), qd=AR.view([NT, 128], BF16, f"c_qd{i}"),
                                 kdT=AR.view([NT, 128], BF16, f"c_kdT{i}"), Rall=AR.view([NT, 8], F32, f"c_Rall{i}"), tots=AR.view([2, NT], F32, f"c_tot{i}")))
            wfs = {}

            def pre_gen(dr, h, st):
                kfl, qs, qd, kdT, Rall, tots = st["kfl"], st["qs"], st["qd"], st["kdT"], st["Rall"], st["tots"]
                kfl2 = kfl[:].rearrange("p a b -> p (a b)")
                if h == 0:
                    wfs[dr] = load_w(w_kc(w_in[l], OFF["cf"] + dr * 512, 512), [8, 512])
                wf, wfv = wfs[dr]
                col = dr * 4 + h
                for tg in range(T // 512):
                    b = proj_fm(wf, wfv, h * 128, tg * 512, 512)
                    act(tmp2[:, tg * 512:(tg + 1) * 512], b[:, :], AF.Sigmoid, [b], [tmpf])
                    yield
                ts("dve", tmp2, tmp2, oml[:, col:col + 1], ALU.mult, [tmpf, oml, lbc], [tmpf], s2=lbc[:, col:col + 1], op1=ALU.add)
                yield
                ts("dve", tmp2, tmp2, 1e-30, ALU.max, [tmpf], [tmpf])
                yield
                ts("dve", kfl2, tmp2, -1.0, ALU.mult, [tmpf], [kfl], s2=1.0, op1=ALU.add)
                yield
                act(logf2, tmp2, AF.Ln, [tmpf], [logf])
                yield
                ts("dve", kf[:].rearrange("p a b -> p (a b)"), kfl2, 1.0, ALU.mult, [kfl], [kf])
                yield
                ts("dve", kfl2, kfl2, 1e-18, ALU.max, [kfl], [kfl])
                yield
                act(kfl2, kfl2, AF.Ln, [kfl], [kfl])
                yield
                S.op("dve", lambda e, o=B2, d0=sm128[:, 0:T], d1=logf2: e.tensor_tensor_scan(o, d0, d1, zcol[:, 0:1], ALU.mult, ALU.add),
                     [sm128, logf, zcol], [Bc])
                yield
                S.op("dve", lambda e, o=Bw2, d0=sm16[:, 0:T], d1=logf2: e.tensor_tensor_scan(o, d0, d1, zcol[:, 0:1], ALU.mult, ALU.add),
                     [sm16, logf, zcol], [Bw])
                yield
                cp("dve", tots[:, 0, :], Bc[:, :, 127], [Bc], [tots])
                yield
                if dr == 1:
                    tt("dve", Bc[:], tots[:, 0, :].unsqueeze(2).to_broadcast([128, NT, 128]), Bc[:], ALU.subtract, [tots, Bc], [Bc])
                    yield
                    tt("dve", Bc[:], Bc[:], logf[:], ALU.add, [Bc, logf], [Bc])
                    yield
                    bw3 = Bw2.rearrange("p (a b) -> p a b", b=16)
                    cp("dve", tot16[:], bw3[:, :, 15], [Bw], [tot16])
                    yield
                    tt("dve", bw3, tot16[:].unsqueeze(2).to_broadcast([128, NT * 8, 16]), bw3, ALU.subtract, [tot16, Bw], [Bw])
                    yield
                    tt("dve", Bw[:], Bw[:], logf[:], ALU.add, [Bw, logf], [Bw])
                    yield
                act(tots[:, 1, :], tots[:, 0, :], AF.Exp, [tots], [tots])
                yield
                act(tmp2, Bw2, AF.Exp, [Bw], [tmpf])
                yield
                tt("dve", qs[:].rearrange("p a b -> p (a b)"), qb[:, h, :], tmp2, ALU.mult, [qb, tmpf], [qs])
                yield
                act(tmp2, B2, AF.Exp, [Bc], [tmpf])
                yield
                tt("dve", qd[:].rearrange("p a b -> p (a b)"), qb[:, h, :], tmp2, ALU.mult, [qb, tmpf], [qd])
                yield
                tt("dve", tmpf[:], tots[:, 0, :].unsqueeze(2).to_broadcast([128, NT, 128]), Bc[:], ALU.subtract, [tots, Bc], [tmpf])
                yield
                act(tmp2, tmp2, AF.Exp, [tmpf], [tmpf])
                yield
                tt("dve", kdT[:], kf[:], tmpf[:], ALU.mult, [kf, tmpf], [kdT])
                yield
                tt("dve", kfl[:], Bc[:], kfl[:], ALU.subtract, [Bc, kfl], [kfl])
                yield
                if dr == 0:
                    cp("dve", Rall[:, :, 0:1], zcol[:, 0:1].unsqueeze(1).to_broadcast([128, NT, 1]), [zcol], [Rall])
                    cp("dve", Rall[:, :, 1:8], Bc[:, :, 15:112:16], [Bc], [Rall])
                else:
                    cp("dve", Rall[:, :, 7:8], zcol[:, 0:1].unsqueeze(1).to_broadcast([128, NT, 1]), [zcol], [Rall])
                    cp("dve", Rall[:, :, 0:7], Bc[:, :, 16:128:16], [Bc], [Rall])
                yield

            def tiles_gen(dr, h, st):
                kfl, qs, qd, kdT, Rall, tots = st["kfl"], st["qs"], st["qd"], st["kdT"], st["Rall"], st["tots"]
                m01 = K["m01_f"] if dr == 0 else K["m01_b"]

                def prep(t, k):
                    te, ks = te8[k], ks8[0]
                    tt("pool", te[:], Rall[:, t, :].unsqueeze(2).to_broadcast([128, 8, 128]), kfl[:, t, :].unsqueeze(1).to_broadcast([128, 8, 128]),
                       ALU.subtract, [Rall, kfl], [te])
                    tef = te[:].rearrange("p a b -> p (a b)")
                    ts("dve", tef, tef, 80.0, ALU.min, [te], [te])
                    act(ks[:].rearrange("p a b -> p (a b)"), tef, AF.Exp, [te], [ks])
                    bat = bank()
                    for i in range(8):
                        mm(bat[:, 16 * i:16 * i + 16], ks[:, i, :], qs[:, t, 16 * i:16 * i + 16], True, True, [ks, qs], [bat])
                    AT = ATs[k]
                    tt("dve", AT[:], bat[:, 0:128], m01[:], ALU.mult, [bat, m01], [AT])
                    bt = bank()
                    btb = bt.ap[:].bitcast(BF16)
                    tr(btb[:, 0:128], kdT[:, t, :], identb[:], [kdT, identb], [bt])
                    kd = kdtm[k]
                    cp("act", kd[:], btb[:, 0:128], [bt], [kd])

                def spart(t, k):
                    AT, kd = ATs[k], kdtm[k]
                    bo = bank()
                    mm(bo[:, 0:128], AT[:], vtm[:, t, h * 128:(h + 1) * 128], True, False, [AT, vtm], [bo])
                    mm(bo[:, 0:128], qd[:, t, :], S16[:], False, True, [qd, S16], [bo])
                    mm(bo[:, 128:256], kd[:], vtm[:, t, h * 128:(h + 1) * 128], True, True, [kd, vtm], [bo])
                    stt(S32[:], S32[:], tots[:, 1, t:t + 1], bo[:, 128:256], ALU.mult, ALU.add, [S32, tots, bo], [S32])
                    if dr == 0:
                        tt("dve", oacc[:, t, h * 128:(h + 1) * 128], bo[:, 0:128], K["ones"][:], ALU.mult, [bo, K["ones"]], R_OACC)
                    else:
                        tt("dve", oacc[:, t, h * 128:(h + 1) * 128], oacc[:, t, h * 128:(h + 1) * 128], bo[:, 0:128], ALU.add, [bo] + R_OACC, R_OACC)
                    cp("act", S16[:], S32[:], [S32], [S16])

                for (t0, nt) in g["seqs"]:
                    si = t0 // nt
                    if g["lat"]:
                        S.dma("sp", S32[:], st_hg[l, dr, h], writes=[S32])
                    else:
                        S.dma("sp", S32[:], zeros_d[:, 0:128], writes=[S32])
                    cp("act", S16[:], S32[:], [S32], [S16])
                    order = list(range(t0, t0 + nt)) if dr == 0 else list(range(t0 + nt - 1, t0 - 1, -1))
                    prep(order[0], 0)
                    yield
                    for idx, t in enumerate(order):
                        if idx + 1 < len(order):
                            prep(order[idx + 1], (idx + 1) % 2)
                        spart(t, idx % 2)
                        yield
                    if not g["lat"]:
                        S.dma("sp", o_st_hg[si, l, dr, h], S32[:], reads=[S32], is_output=True)

            combos = [(dr, h) for dr in range(2) for h in range(4)]
            for _ in pre_gen(combos[0][0], combos[0][1], sets[0]):
                pass
            for i, (dr, h) in enumerate(combos):
                nxt = pre_gen(combos[i + 1][0], combos[i + 1][1], sets[(i + 1) % 2]) if i + 1 < len(combos) else None
                tg_ = tiles_gen(dr, h, sets[i % 2])
                alive_t, alive_n = True, nxt is not None
                while alive_t or alive_n:
                    if alive_t:
                        try:
                            next(tg_)
                        except StopIteration:
                            alive_t = False
                    if alive_n:
                        for _ in range(5):
                            try:
                                next(nxt)
                            except StopIteration:
                                alive_n = False
                                break
            AR.reset()
            self.dbg(f"oaccC_{g['name']}{l}", R_OACC, oacc, [128, NT, 512])
            head_rms_out(l, g, oacc, hg_ng, OFF["cg"], OT[2], R_OT[2])

        def mixer_A(l, g):
            NT, T, seqs, lat = g["NT"], g["T"], g["seqs"], g["lat"]
            AR.reset()
            oacc = OTall.ap[:, 8192:16384].bitcast(F32)[:, 0:NT * 512].rearrange("p (t c) -> p t c", c=512)
            qT = AR.view([4, T], BF16, "a_qT"); kT = AR.view([4, T], BF16, "a_kT"); vT = AR.view([4, T], BF16, "a_vT")
            mark = AR.mark()
            nseq = len(seqs); Ls = T // nseq; Lp = T + 4 * nseq; n = Lp - 4
            CB = []
            for i in range(2):
                CB.append(dict(cpad=AR.view([Lp], F32, f"a_cpad{i}"), cacc=AR.view([Lp], F32, f"a_cacc{i}"), csil=AR.view([Lp], F32, f"a_csil{i}"),
                               rtmp=AR.view([512], F32, f"a_rtmp{i}")))
                S.dma("sp", CB[i]["cpad"][:], zeros_d[:, 0:Lp], writes=[CB[i]["cpad"]])
            cw = AR.view([12, 5], F32, "a_cw")
            S.dma("sp", cw[:], conv_w[l].rearrange("(c p) k -> p c k", p=128), writes=[cw])
            wq_ = {}

            def conv_gen(ci_):
                which, h = ci_ // 4, ci_ % 4
                cb = CB[ci_ % 2]
                cpad, cacc, csil, rtmp = cb["cpad"], cb["cacc"], cb["csil"], cb["rtmp"]
                if h == 0:
                    wq_[which] = load_w(w_kc(w_in[l], which * 512, 512), [8, 512])
                w, wv = wq_[which]
                for tg in range(T // 512):
                    b = proj_fm(w, wv, h * 128, tg * 512, 512)
                    for si, (t0, nt) in enumerate(seqs):
                        lo = max(t0 * 128, tg * 512); hi = min((t0 + nt) * 128, (tg + 1) * 512)
                        if lo < hi:
                            cp("act", cpad[:, lo + 4 * si + 2:hi + 4 * si + 2], b[:, lo - tg * 512:hi - tg * 512], [b], [cpad])
                    yield
                ts("dve", cacc[:, 0:n], cpad[:, 0:n], cw[:, ci_, 0:1], ALU.mult, [cpad, cw], [cacc])
                yield
                for j in range(1, 5):
                    stt(cacc[:, 0:n], cpad[:, j:j + n], cw[:, ci_, j:j + 1], cacc[:, 0:n], ALU.mult, ALU.add, [cpad, cw, cacc], [cacc])
                    yield
                if which == 2:
                    for si, (t0, nt) in enumerate(seqs):
                        act(vT[:, h, t0 * 128:(t0 + nt) * 128], cacc[:, t0 * 128 + 4 * si:(t0 + nt) * 128 + 4 * si], AF.Silu, [cacc], [vT])
                    yield
                else:
                    act(csil[:, 0:n], cacc[:, 0:n], AF.Silu, [cacc], [csil])
                    act(cacc[:, 0:n], csil[:, 0:n], AF.Square, [csil], [cacc])
                    yield
                    for si, (t0, nt) in enumerate(seqs):
                        for p0 in range(t0 * 128, (t0 + nt) * 128, 512):
                            np_ = min(512, (t0 + nt) * 128 - p0)
                            i0 = p0 + 4 * si
                            b = bank()
                            mm(b[:, 0:np_], onesf[:], cacc[:, i0:i0 + np_], True, True, [onesf, cacc], [b])
                            yield
                            act(rtmp[:, 0:np_], b[:, 0:np_], AF.Ln, [b, epsc], [rtmp], bias=epsc[:])
                            act(rtmp[:, 0:np_], rtmp[:, 0:np_], AF.Exp, [rtmp], [rtmp], scale=-0.5)
                            yield
                            dst = (qT if which == 0 else kT)
                            stt(dst[:, h, p0:p0 + np_], csil[:, i0:i0 + np_], (128.0 ** -0.5 if which == 0 else 1.0), rtmp[:, 0:np_],
                                ALU.mult, ALU.mult, [csil, rtmp], [dst])
                            yield

            for c0_ in range(0, 12, 2):
                gens = [conv_gen(c0_), conv_gen(c0_ + 1)]
                while gens:
                    for gg in list(gens):
                        try:
                            next(gg)
                        except StopIteration:
                            gens.remove(gg)
            self.dbg(f"qT_{g['name']}{l}", qT, qT[:], [128, 4, T])
            self.dbg(f"kT_{g['name']}{l}", kT, kT[:], [128, 4, T])
            self.dbg(f"vT_{g['name']}{l}", vT, vT[:], [128, 4, T])
            AR.release(mark)
            AB = AR.view([NT, 16], F32, "a_AB"); Gg = AR.view([NT, 8], F32, "a_G"); Lnb = AR.view([NT, 8], F32, "a_lnb")
            Beta = AR.view([NT, 8], F32, "a_beta"); PT_ = AR.view([NT, 2, 24], F32, "a_PT")
            dtb = AR.view([8], F32, "a_dtb"); negA = AR.view([8], F32, "a_negA")
            S.dma("sp", dtb[:], dt_b[l:l + 1, :].partition_broadcast(128).rearrange("p o d -> p (o d)"), writes=[dtb])
            S.dma("sp", negA[:], a_log[l:l + 1, :].partition_broadcast(128).rearrange("p o d -> p (o d)"), writes=[negA])
            act(negA[:], negA[:], AF.Exp, [negA], [negA])
            ts("dve", negA[:], negA[:], -1.0, ALU.mult, [negA], [negA])
            wab, wabv = load_w(w_kc(w_in[l], OFF["aa"], 16), [8, 16])
            for t in range(NT):
                b = proj_tm(wab, wabv, 0, 16, t)
                cp("act", AB[:, t, :], b[:, 0:16], [b], [AB])
            tt("dve", Gg[:], AB[:, :, 0:8], dtb[:].unsqueeze(1).to_broadcast([128, NT, 8]), ALU.add, [AB, dtb], [Gg])
            act(Gg[:], Gg[:], AF.Exp, [Gg], [Gg])
            act(Gg[:], Gg[:], AF.Ln, [Gg, onecol], [Gg], bias=onecol[:])
            tt("dve", Gg[:], Gg[:], negA[:].unsqueeze(1).to_broadcast([128, NT, 8]), ALU.mult, [Gg, negA], [Gg])
            act(Lnb[:], AB[:, :, 8:16], AF.Exp, [AB], [Lnb], scale=-1.0)
            act(Lnb[:], Lnb[:], AF.Ln, [Lnb, onecol], [Lnb], bias=onecol[:])
            ts("dve", Lnb[:], Lnb[:], -1.0, ALU.mult, [Lnb], [Lnb])
            act(Beta[:], Lnb[:], AF.Exp, [Lnb], [Beta])
            for t in range(NT):
                for dr in range(2):
                    b = bank()
                    gsl = Gg[:, t, dr * 4:(dr + 1) * 4]
                    tri = K["tri2_f"] if dr == 0 else K["tri2_b"]
                    mm(b[:, 0:4], tri[:], gsl, True, True, [tri, Gg], [b])
                    mm(b[:, 4:8], K["ones2"][:], gsl, True, True, [K["ones2"], Gg], [b])
                    mm(b[:, 8:12], K["csel0"][:], gsl, True, True, [K["csel0"], Gg], [b])
                    mm(b[:, 12:16], K["csel1"][:], gsl, True, True, [K["csel1"], Gg], [b])
                    pt = PT_[:, t, dr, :]
                    cp("act", pt[:, 0:4], b[:, 0:4], [b], [PT_])
                    tt("dve", pt[:, 4:8], b[:, 0:4], Lnb[:, t, dr * 4:(dr + 1) * 4], ALU.add, [b, Lnb], [PT_])
                    act(pt[:, 8:12], pt[:, 4:8], AF.Exp, [PT_], [PT_])
                    tt("dve", pt[:, 12:16], b[:, 4:8], pt[:, 0:4], ALU.subtract, [b, PT_], [PT_])
                    act(pt[:, 12:16], pt[:, 12:16], AF.Exp, [PT_], [PT_])
                    act(pt[:, 16:24], b[:, 8:16], AF.Exp, [b], [PT_])
            vb = AR.view([4, 128], BF16, "a_vb"); kbg = AR.view([4, 128], BF16, "a_kbg"); kdec = AR.view([4, 128], BF16, "a_kdec")
            S32 = [AR.view([128], F32, f"a_S32_{h}") for h in range(4)]
            S16 = [AR.view([128], BF16, f"a_S16_{h}") for h in range(4)]
            HB = []
            for i in range(4):
                d_ = {}
                for nm in ("E1", "E2", "E3", "U", "L", "eA", "R0", "R1", "u"):
                    d_[nm] = AR.view([128], F32, f"a_{nm}_{i}")
                for nm in ("AT", "qd", "Mi", "wT", "vn"):
                    d_[nm] = AR.view([128], BF16, f"a_{nm}_{i}")
                d_["P1"], d_["P0"], d_["Q1"], d_["Q0"] = d_["E2"], d_["E1"], d_["eA"], d_["E3"]
                HB.append(d_)

            def head_gen(h, t, dr, pt, tok, nm_incl, nm_str, pm_str):
                hb = HB[h]
                bm = bank()
                g0, g1 = hb["R0"], hb["R1"]
                cp("pool", g0[:], pt[:, h:h + 1].to_broadcast([128, 128]), [PT_], [g0])
                mm(bm[:, 0:128], g0[:], identf[:], True, True, [g0, identf], [bm])
                cp("pool", g1[:], pt[:, 4 + h:5 + h].to_broadcast([128, 128]), [PT_], [g1])
                mm(bm[:, 128:256], g1[:], identf[:], True, True, [g1, identf], [bm])
                mm(bm[:, 256:384], kT[:, h, tok], kT[:, h, tok], True, True, [kT], [bm])
                mm(bm[:, 384:512], kT[:, h, tok], qT[:, h, tok], True, True, [kT, qT], [bm])
                yield
                E1, E2, E3, U, L, eA = hb["E1"], hb["E2"], hb["E3"], hb["U"], hb["L"], hb["eA"]
                stt(E1[:], bm[:, 0:128], pt[:, h:h + 1], nm_incl[:], ALU.subtract, ALU.min, [bm, PT_, nm_incl], [E1])
                stt(E2[:], bm[:, 128:256], pt[:, h:h + 1], nm_str[:], ALU.subtract, ALU.min, [bm, PT_, nm_str], [E2])
                stt(E3[:], bm[:, 0:128], pt[:, 4 + h:5 + h], pm_str[:], ALU.subtract, ALU.max, [bm, PT_, pm_str], [E3])
                yield
                act(E1[:], E1[:], AF.Exp, [E1], [E1])
                act(E2[:], E2[:], AF.Exp, [E2], [E2])
                act(E3[:], E3[:], AF.Exp, [E3], [E3], scale=-1.0)
                yield
                AT = hb["AT"]
                tt("dve", AT[:], bm[:, 384:512], E1[:], ALU.mult, [bm, E1], [AT])
                tt("dve", U[:], bm[:, 256:384], E2[:], ALU.mult, [bm, E2], [U])
                tt("dve", L[:], bm[:, 256:384], E3[:], ALU.mult, [bm, E3], [L])
                yield
                act(eA[:], bm[:, 0:128], AF.Exp, [bm], [eA])
                R0 = hb["R0"]
                tt("pool", R0[:], identf[:], U[:], ALU.subtract, [identf, U], [R0])
                yield
                qd = hb["qd"]
                tt("pool", qd[:], qT[:, h, tok], eA[:], ALU.mult, [qT, eA], [qd])
                Pk, Qk, Rk = L, U, R0
                for lev in range(1, 6):
                    bn = bank()
                    mm(bn[:, 0:128], Qk[:], Pk[:], True, True, [Qk, Pk], [bn])
                    if lev < 5:
                        mm(bn[:, 128:256], Pk[:], Qk[:], True, True, [Pk, Qk], [bn])
                    yield
                    Pn = hb[f"P{lev % 2}"]
                    cp("act", Pn[:], bn[:, 0:128], [bn], [Pn])
                    Qn = None
                    if lev < 5:
                        Qn = hb[f"Q{lev % 2}"]
                        cp("act", Qn[:], bn[:, 128:256], [bn], [Qn])
                    yield
                    mm(bn[:, 256:384], Pn[:], Rk[:], True, True, [Pn, Rk], [bn])
                    yield
                    Rn = hb["Mi"] if lev == 5 else hb[f"R{lev % 2}"]
                    tt("dve", Rn[:], bn[:, 256:384], Rk[:], ALU.add, [bn, Rk], [Rn])
                    Pk, Qk, Rk = Pn, Qn, Rn
                    yield
                Mi = hb["Mi"]
                bu = bank()
                mm(bu[:, 0:128], Mi[:], vb[:, h, :], True, True, [Mi, vb], [bu])
                mm(bu[:, 128:256], kbg[:, h, :], Mi[:], True, True, [kbg, Mi], [bu])
                yield
                u_sb, wT, vn = hb["u"], hb["wT"], hb["vn"]
                cp("act", u_sb[:], bu[:, 0:128], [bu], [u_sb])
                cp("act", wT[:], bu[:, 128:256], [bu], [wT])
                yield
                for j in ((0, 1) if dr == 0 else (1, 0)):
                    ps_ = slice(64 * j, 64 * j + 64)
                    bs = bank()
                    mm(bs[:, 0:128], wT[:], S16[h][:], True, True, [wT, S16[h]], [bs])
                    yield
                    tt("dve", vn[ps_, :], u_sb[ps_, :], bs[ps_, 0:128], ALU.subtract, [u_sb, bs], [vn])
                    yield
                    mm(bs[:, 128:256], qd[:], S16[h][:], True, False, [qd, S16[h]], [bs])
                    mm(bs[:, 128:256], AT[ps_, :], vn[ps_, :], False, True, [AT, vn], [bs])
                    mm(bs[:, 256:384], kdec[ps_, h, :], vn[ps_, :], True, True, [kdec, vn], [bs])
                    yield
                    stt(S32[h][:], S32[h][:], pt[:, 16 + 4 * j + h:17 + 4 * j + h], bs[:, 256:384], ALU.mult, ALU.add,
                        [S32[h], PT_, bs], [S32[h]])
                    if dr == 0:
                        tt("dve", oacc[ps_, t, h * 128:(h + 1) * 128], bs[ps_, 128:256], K["ones"][ps_, :], ALU.mult, [bs, K["ones"]], R_OACC)
                    else:
                        tt("dve", oacc[ps_, t, h * 128:(h + 1) * 128], oacc[ps_, t, h * 128:(h + 1) * 128], bs[ps_, 128:256], ALU.add,
                           [bs] + R_OACC, R_OACC)
                    yield
                    cp("pool", S16[h][:], S32[h][:], [S32[h]], [S16[h]])
                    yield

            for dr in range(2):
                nm_incl = K["nm_f_incl"] if dr == 0 else K["nm_b_incl"]
                nm_str = K["nm_f_str"] if dr == 0 else K["nm_b_str"]
                pm_str = K["pm_f_str"] if dr == 0 else K["pm_b_str"]
                for (t0, nt) in seqs:
                    si = t0 // nt
                    for h in range(4):
                        if lat:
                            S.dma("sp", S32[h][:], st_gdn[l, dr, h], writes=[S32[h]])
                        else:
                            S.dma("sp", S32[h][:], zeros_d[:, 0:128], writes=[S32[h]])
                        cp("act", S16[h][:], S32[h][:], [S32[h]], [S16[h]])
                    order = range(t0, t0 + nt) if dr == 0 else range(t0 + nt - 1, t0 - 1, -1)
                    for t in order:
                        tok = slice(t * 128, (t + 1) * 128)
                        pt = PT_[:, t, dr, :]
                        bkv = bank()
                        bb = bkv.ap[:].bitcast(BF16)
                        for h in range(4):
                            tr(bb[:, h * 128:(h + 1) * 128], kT[:, h, tok], identb[:], [kT, identb], [bkv])
                            tr(bb[:, 512 + h * 128:512 + (h + 1) * 128], vT[:, h, tok], identb[:], [vT, identb], [bkv])
                        kv3 = bb[:, 0:512].rearrange("p (a b) -> p a b", a=4)
                        vv3 = bb[:, 512:1024].rearrange("p (a b) -> p a b", a=4)
                        tt("dve", vb[:], vv3, Beta[:, t, dr * 4:(dr + 1) * 4].unsqueeze(2).to_broadcast([128, 4, 128]), ALU.mult, [bkv, Beta], [vb])
                        tt("dve", kbg[:], kv3, pt[:, 8:12].unsqueeze(2).to_broadcast([128, 4, 128]), ALU.mult, [bkv, PT_], [kbg])
                        tt("dve", kdec[:], kv3, pt[:, 12:16].unsqueeze(2).to_broadcast([128, 4, 128]), ALU.mult, [bkv, PT_], [kdec])
                        gens = [head_gen(h, t, dr, pt, tok, nm_incl, nm_str, pm_str) for h in range(4)]
                        while gens:
                            for gg in list(gens):
                                try:
                                    next(gg)
                                except StopIteration:
                                    gens.remove(gg)
                    if not lat:
                        for h in range(4):
                            S.dma("sp", o_st_gdn[si, l, dr, h], S32[h][:], reads=[S32[h]], is_output=True)
            self.dbg(f"oaccA_{g['name']}{l}", R_OACC, oacc, [128, NT, 512])
            head_rms_out(l, g, oacc, gdn_ng, OFF["az"], OT[0], R_OT[0])

        self._bkr = {}

        def bank_r(lo, hi):
            k = self._bkr.get((lo, hi), lo)
            self._bkr[(lo, hi)] = lo + (k + 1 - lo) % (hi - lo)
            return banks[k]

        def mixer_B(l, g):
            NT, T, seqs, lat = g["NT"], g["T"], g["seqs"], g["lat"]
            lam_init = 0.8 - 0.6 * math.exp(-0.3 * l)
            AR.reset()
            qT = AR.view([4, T], BF16, "b_qT"); kT = AR.view([4, T], BF16, "b_kT")
            vx = AR.view([NT, 4, 130], BF16, "b_vx")
            dl = AR.view([4, 64], F32, "b_dl"); lam = AR.view([8], F32, "b_lam")
            S.dma("sp", dl[:].rearrange("p a b -> p (a b)"), dlam[l:l + 1, :].partition_broadcast(128).rearrange("p o d -> p (o d)"), writes=[dl])
            tt("dve", dl[:, 0, :], dl[:, 0, :], dl[:, 1, :], ALU.mult, [dl], [dl])
            tt("dve", dl[:, 2, :], dl[:, 2, :], dl[:, 3, :], ALU.mult, [dl], [dl])
            S.op("dve", lambda e: e.reduce_sum(out=lam[:, 0:1], in_=dl[:, 0, :], axis=AX.X), [dl], [lam])
            S.op("dve", lambda e: e.reduce_sum(out=lam[:, 1:2], in_=dl[:, 2, :], axis=AX.X), [dl], [lam])
            act(lam[:, 0:2], lam[:, 0:2], AF.Exp, [lam], [lam])
            tt("dve", lam[:, 2:3], lam[:, 0:1], lam[:, 1:2], ALU.subtract, [lam], [lam])
            ts("dve", lam[:, 3:4], lam[:, 2:3], lam_init, ALU.add, [lam], [lam], s2=-1.0, op1=ALU.mult)
            for t in range(NT):
                for h in range(4):
                    S.dma("pool", vx[:, t, h, 128:130], cdram["ones"][:, 0:2], writes=[vx])
            if lat:
                ckT = AR.view([4, 512], BF16, "b_ckT")
                cvx = AR.view([4, 4, 130], BF16, "b_cvx")
            markb = AR.mark()
            stg = [AR.view([512], F32, f"b_stg{i}") for i in range(2)]
            rq = [AR.view([512], BF16, f"b_rq{i}") for i in range(2)]
            rt = [AR.view([256], F32, f"b_rt{i}") for i in range(4)]
            wq, wqv = load_w(w_kc(w_in[l], OFF["bq"], 512), [8, 512])
            wk, wkv = load_w(w_kc(w_in[l], OFF["bk"], 512), [8, 512])
            wv_, wvv = load_w(w_kc(w_in[l], OFF["bv"], 512), [8, 512])
            for t in range(NT):
                si = [i for i, (t0, nt) in enumerate(seqs) if t0 <= t < t0 + nt][0]
                pos0 = (t - seqs[si][0]) * 128
                for qi, (w, wv, dst) in enumerate(((wq, wqv, qT), (wk, wkv, kT))):
                    b = proj_tm(w, wv, 0, 512, t)
                    r_ = rq[qi]
                    if lat:
                        x4 = b[:, :].rearrange("p (a c d) -> p a c d", a=8, c=2)
                        o4 = r_[:].rearrange("p (a c d) -> p a c d", a=8, c=2)
                        cosb = K["rope_cos"][:, t, :].unsqueeze(1).to_broadcast([128, 8, 32])
                        sinb = K["rope_sin"][:, t, :].unsqueeze(1).to_broadcast([128, 8, 32])
                        r3 = [x[:].rearrange("p (a d) -> p a d", a=8) for x in rt]
                        tt("dve", r3[0], x4[:, :, 0, :], cosb, ALU.mult, [b, K["rope_cos"]], [rt[0]])
                        tt("dve", r3[1], x4[:, :, 1, :], sinb, ALU.mult, [b, K["rope_sin"]], [rt[1]])
                        tt("pool", o4[:, :, 0, :], r3[0], r3[1], ALU.subtract, [rt[0], rt[1]], [r_])
                        tt("dve", r3[2], x4[:, :, 1, :], cosb, ALU.mult, [b, K["rope_cos"]], [rt[2]])
                        tt("dve", r3[3], x4[:, :, 0, :], sinb, ALU.mult, [b, K["rope_sin"]], [rt[3]])
                        tt("pool", o4[:, :, 1, :], r3[2], r3[3], ALU.add, [rt[2], rt[3]], [r_])
                    else:
                        cp("act", r_[:], b[:, :], [b], [r_])
                        if qi == 1:
                            sg_ = stg[0]
                            cp("dve", sg_[:], b[:, :], [b], [sg_])
                            S.dma("sp", o_ck_b[si, l, :, pos0:pos0 + 128, :].rearrange("h t d -> t h d"),
                                  sg_[:].rearrange("p (h d) -> p h d", h=4), reads=[sg_], is_output=True)
                    bt = bank()
                    bb = bt.ap[:].bitcast(BF16)
                    for h in range(4):
                        tr(bb[:, h * 128:(h + 1) * 128], r_[:, h * 128:(h + 1) * 128], identb[:], [r_, identb], [bt])
                    cp("act", dst[:, :, t * 128:(t + 1) * 128], bb[:, 0:512].rearrange("p (a b) -> p a b", a=4), [bt], [dst])
                b = proj_tm(wv_, wvv, 0, 512, t)
                cp("act", vx[:, t, :, 0:128], b[:, :].rearrange("p (h d) -> p h d", h=4), [b], [vx])
                if not lat:
                    sg_ = stg[1]
                    cp("dve", sg_[:], b[:, :], [b], [sg_])
                    S.dma("sp", o_cv_b[si, l, :, pos0:pos0 + 128, :].rearrange("h t d -> t h d"),
                          sg_[:].rearrange("p (h d) -> p h d", h=4), reads=[sg_], is_output=True)
            ckeys = []
            if lat:
                ckb = AR.view([4, 4, 128], BF16, "b_ckb")
                for h in range(4):
                    S.dma("pool", ckb[:, h, :, :], ck_b[l, h].rearrange("(kt p) d -> p kt d", p=128), writes=[ckb])
                for kt in range(4):
                    S.dma("pool", cvx[:, kt, :, 0:128], cv_b[l][:, kt * 128:(kt + 1) * 128, :].rearrange("h p d -> p h d"), writes=[cvx])
                for kt in range(4):
                    for h in range(4):
                        S.dma("pool", cvx[:, kt, h, 128:130], cdram["ones"][:, 0:2], writes=[cvx])
                for h in range(4):
                    bt = bank()
                    bb = bt.ap[:].bitcast(BF16)
                    for kt in range(4):
                        tr(bb[:, kt * 128:(kt + 1) * 128], ckb[:, h, kt, :], identb[:], [ckb, identb], [bt])
                    cp("act", ckT[:, h, :], bb[:, 0:512], [bt], [ckT])
            AR.release(markb)
            PTb = [AR.view([512], BF16, f"b_PT{i}") for i in range(4)]
            ostg = AR.view([4, 512], F32, "b_ostg")
            rr = AR.view([16], F32, "b_rr")
            ngb = AR.view([128], F32, "b_ngb")
            S.dma("sp", ngb[:], diff_ng[l:l + 1, :].partition_broadcast(128).rearrange("p o d -> p (o d)"), writes=[ngb])
            ts("dve", ngb[:], ngb[:], float(1.0 - lam_init), ALU.mult, [ngb], [ngb])
            gzs = [AR.view([4, 128], F32, f"b_gz{i}") for i in range(2)]
            obs = [AR.view([512], BF16, f"b_ob{i}") for i in range(2)]
            pti = 0
            for (t0, nt) in seqs:
                keys = []
                if lat:
                    for kt in range(4):
                        keys.append(("c", kt))
                for t in range(t0, t0 + nt):
                    keys.append(("l", t))
                for qg0 in range(t0, t0 + nt, 4):
                    nqt = min(4, t0 + nt - qg0)
                    nq = nqt * 128
                    for h in range(4):
                        accs = [[banks[0], banks[1]], [banks[2], banks[3]]]
                        items = [(ki, kind, kx, m) for ki, (kind, kx) in enumerate(keys) for m in range(2)]

                        def b_score(it):
                            ki, kind, kx, m = it
                            pb = slice(64 * m, 64 * m + 64)
                            bsc = bank_r(4, 8)
                            if kind == "c":
                                kTap, kr = ckT[pb, h, kx * 128:(kx + 1) * 128], ckT
                            else:
                                kTap, kr = kT[pb, h, kx * 128:(kx + 1) * 128], kT
                            mm(bsc[:, 0:nq], kTap, qT[pb, h, qg0 * 128:qg0 * 128 + nq], True, True, [kr, qT], [bsc])
                            return bsc

                        def b_rest(it, bsc):
                            nonlocal pti
                            ki, kind, kx, m = it
                            if kind == "c":
                                vap, vr = cvx[:, kx, h, 0:129], cvx
                            else:
                                vap, vr = vx[:, kx, h, 0:129], vx
                            PT_ = PTb[pti % 4]; pti += 1
                            act(PT_[:, 0:nq], bsc[:, 0:nq], AF.Exp, [bsc], [PT_], scale=0.125)
                            for j in range(nqt):
                                ab = accs[m][j // 2]
                                mm(ab[:, (j % 2) * 256:(j % 2) * 256 + 129], PT_[:, j * 128:(j + 1) * 128], vap, ki == 0 and j % 2 == 0,
                                   ki == len(keys) - 1 and (j % 2 == 1 or j == nqt - 1), [PT_, vr], [ab])

                        LA = 3
                        pend = [b_score(it) for it in items[:LA]]
                        for idx, it in enumerate(items):
                            if idx + LA < len(items):
                                pend.append(b_score(items[idx + LA]))
                            b_rest(it, pend.pop(0))
                        for j in range(nqt):
                            a0 = accs[0][j // 2]; a1 = accs[1][j // 2]; c0 = (j % 2) * 256
                            S.op("dve", lambda e, o=rr[:, 2 * j:2 * j + 1], i=a0[:, c0 + 128:c0 + 129]: e.reciprocal(o, i), [a0], [rr])
                            S.op("dve", lambda e, o=rr[:, 2 * j + 1:2 * j + 2], i=a1[:, c0 + 128:c0 + 129]: e.reciprocal(o, i), [a1], [rr])
                            tt("dve", rr[:, 2 * j + 1:2 * j + 2], rr[:, 2 * j + 1:2 * j + 2], lam[:, 3:4], ALU.mult, [rr, lam], [rr])
                            od = ostg[:, j, h * 128:(h + 1) * 128]
                            ts("dve", od, a0[:, c0:c0 + 128], rr[:, 2 * j:2 * j + 1], ALU.mult, [a0, rr], [ostg])
                            stt(od, a1[:, c0:c0 + 128], rr[:, 2 * j + 1:2 * j + 2], od, ALU.mult, ALU.add, [a1, rr, ostg], [ostg])
                    for j in range(nqt):
                        t = qg0 + j
                        st = small(); junk = xn[t % 2]; gz = gzs[t % 2]; ob = obs[t % 2]
                        for h in range(4):
                            act(junk[:, h * 128:(h + 1) * 128], ostg[:, j, h * 128:(h + 1) * 128], AF.Square, [ostg], [junk, st], accum=st[:, h:h + 1])
                        act(st[:, 4:8], st[:, 0:4], AF.Ln, [st, epsc], [st], bias=epsc[:], scale=1.0 / 128)
                        act(st[:, 4:8], st[:, 4:8], AF.Exp, [st], [st], scale=-0.5)
                        o3 = ostg[:, j, :].rearrange("p (a b) -> p a b", a=4)
                        tt("dve", gz[:], o3, st[:, 4:8].unsqueeze(2).to_broadcast([128, 4, 128]), ALU.mult, [ostg, st], [gz])
                        tt("dve", ob[:].rearrange("p (a b) -> p a b", a=4), gz[:], ngb[:].unsqueeze(1).to_broadcast([128, 4, 128]), ALU.mult, [gz, ngb], [ob])
                        bt = bank_r(4, 8)
                        bb = bt.ap[:].bitcast(BF16)
                        for kc in range(4):
                            tr(bb[:, kc * 128:(kc + 1) * 128], ob[:, kc * 128:(kc + 1) * 128], identb[:], [ob, identb], [bt])
                        cp("act", OT[1][:, :, t * 128:(t + 1) * 128], bb[:, 0:512].rearrange("p (a b) -> p a b", a=4), [bt], R_OT[1])

        def mixer_D(l, g):
            NT, T, seqs, lat = g["NT"], g["T"], g["seqs"], g["lat"]
            AR.reset()
            qT = AR.view([4, T], BF16, "d_qT"); kT = AR.view([4, T], BF16, "d_kT")
            vx = AR.view([NT, 8, 66], BF16, "d_vx")
            oall = AR.view([NT, 512], BF16, "d_oall")
            if lat:
                ckT = AR.view([4, 512], BF16, "d_ckT")
                cvx = AR.view([4, 8, 66], BF16, "d_cvx")
            markd = AR.mark()
            ones16 = cdram["ones"][:, 0:16].rearrange("p (h c) -> p h c", c=2)
            for t in range(NT):
                S.dma("pool", vx[:, t, :, 64:66], ones16, writes=[vx])
            stg = [AR.view([512], F32, f"d_stg{i}") for i in range(2)]
            wq, wqv = load_w(w_kc(w_in[l], OFF["dq"], 512), [8, 512])
            wk, wkv = load_w(w_kc(w_in[l], OFF["dk"], 512), [8, 512])
            wv_, wvv = load_w(w_kc(w_in[l], OFF["dv"], 512), [8, 512])
            for c in range(4):
                for tg in range(T // 512):
                    b = proj_fm(wq, wqv, c * 128, tg * 512, 512)
                    cp("act", qT[:, c, tg * 512:(tg + 1) * 512], b[:, :], [b], [qT])
                    b = proj_fm(wk, wkv, c * 128, tg * 512, 512)
                    cp("dve", kT[:, c, tg * 512:(tg + 1) * 512], b[:, :], [b], [kT])
            for t in range(NT):
                si = [i for i, (t0, nt) in enumerate(seqs) if t0 <= t < t0 + nt][0]
                pos0 = (t - seqs[si][0]) * 128
                b = proj_tm(wv_, wvv, 0, 512, t)
                cp("act", vx[:, t, :, 0:64], b[:, :].rearrange("p (h d) -> p h d", h=8), [b], [vx])
                if not lat:
                    sg_ = stg[0]
                    cp("dve", sg_[:], b[:, :], [b], [sg_])
                    S.dma("sp", o_cv_d[si, l, :, pos0:pos0 + 128, :].rearrange("h t d -> t h d"),
                          sg_[:].rearrange("p (h d) -> p h d", h=8), reads=[sg_], is_output=True)
                    b2 = proj_tm(wk, wkv, 0, 512, t)
                    sg2 = stg[1]
                    cp("act", sg2[:], b2[:, :], [b2], [sg2])
                    S.dma("sp", o_ck_d[si, l, :, pos0:pos0 + 128, :].rearrange("h t d -> t h d"),
                          sg2[:].rearrange("p (h d) -> p h d", h=8), reads=[sg2], is_output=True)
            if lat:
                ckd = AR.view([4, 8, 64], BF16, "d_ckd")
                for h in range(8):
                    S.dma("pool", ckd[:, :, h, :], ck_d[l, h].rearrange("(kt p) d -> p kt d", p=128), writes=[ckd])
                for kt in range(4):
                    S.dma("pool", cvx[:, kt, :, 0:64], cv_d[l][:, kt * 128:(kt + 1) * 128, :].rearrange("h p d -> p h d"), writes=[cvx])
                    S.dma("pool", cvx[:, kt, :, 64:66], ones16, writes=[cvx])
                for c in range(4):
                    bt = bank()
                    bb = bt.ap[:].bitcast(BF16)
                    for kt in range(4):
                        tr(bb[:, kt * 128:(kt + 1) * 128], ckd[:, kt, 2 * c:2 * c + 2, :].rearrange("p a b -> p (a b)"), identb[:], [ckd, identb], [bt])
                    cp("act", ckT[:, c, :], bb[:, 0:512], [bt], [ckT])
                R1 = AR.view([31], F32, "d_R1"); fpad = AR.view([127], F32, "d_fpad")
                S.dma("sp", R1[0:120, :], rpb[l], writes=[R1])
                S.dma("sp", fpad[:], zeros_d[:, 0:127], writes=[fpad])
                r1ap = R1[0:120, :]
                rev = bass.AP(r1ap.tensor, r1ap[:, 30:31].offset, [list(r1ap.ap[0]), [-1, 31]])
                cp("dve", fpad[0:120, 48:79], rev, [R1, fpad], [fpad])
                S.dma("sp", rpb_scr, fpad[0:120, :].unsqueeze(1).to_broadcast([120, 64, 127]), reads=[fpad], writes=[self.rpb_scr_res])
            AR.release(markd)
            rr = AR.view([8], F32, "d_rr")
            PTc = [AR.view([512], BF16, f"d_PTc{i}") for i in range(2)]
            if lat:
                toep = [AR.view([7, 128], F32, f"d_toep{i}") for i in range(2)]
                comb = [AR.view([128], BF16, f"d_comb{i}") for i in range(10)]
                PTw = [AR.view([512], BF16, f"d_PTw{i}") for i in range(4)]
                ci_ = 0; pw_ = 0
                units = []
                for h in range(8):
                    tp = toep[h % 2]
                    for j in range(8):
                        units.append((h, j, tp))

                def d_score(u):
                    nonlocal ci_
                    h, j, tp = u
                    if j == 0:
                        for di, dl_ in enumerate(range(-3, 4)):
                            for a in range(2):
                                for b_ in range(2):
                                    dr = 2 * dl_ + a - b_ + 7
                                    assert 0 <= dr <= 14
                                    src = bass.AP(rpb_scr.tensor, (h * 15 + dr) * 64 * 127 + 63, [[126, 64], [1, 64]])
                                    S.dma("sp", tp[a * 64:(a + 1) * 64, di, b_ * 64:(b_ + 1) * 64], src, reads=[self.rpb_scr_res], writes=[tp])
                    c = h // 2; pb = slice(64 * (h % 2), 64 * (h % 2) + 64)
                    pairs_j = [(pi, i) for pi, (jj, i) in enumerate(NA_PAIRS) if jj == j]
                    bsc = bank_r(2, 8)
                    for kt in range(4):
                        mm(bsc[:, kt * 128:(kt + 1) * 128], ckT[pb, c, kt * 128:(kt + 1) * 128], qT[pb, c, j * 128:(j + 1) * 128], True, True, [ckT, qT], [bsc])
                    nw = len(pairs_j)
                    bws = [bank_r(2, 8)] + ([bank_r(2, 8)] if nw > 4 else [])
                    for idx, (pi, i) in enumerate(pairs_j):
                        cb = comb[ci_ % len(comb)]; ci_ += 1
                        stt(cb[:], tp[:, i - j + 3, :], 8.0, namask[:, pi, :], ALU.mult, ALU.add, [tp, namask], [cb])
                        bw = bws[idx // 4]; col = (idx % 4) * 128
                        mm(bw[:, col:col + 128], kT[pb, c, i * 128:(i + 1) * 128], qT[pb, c, j * 128:(j + 1) * 128], True, False, [kT, qT], [bw])
                        mm(bw[:, col:col + 128], identb[:], cb[:], False, True, [identb, cb], [bw])
                    return (bsc, bws, pairs_j)

                def d_rest(u, sc):
                    nonlocal pw_
                    h, j, tp = u
                    bsc, bws, pairs_j = sc
                    nw = len(pairs_j)
                    acc = banks[(h * 8 + j) % 2]
                    ptc = PTc[j % 2]
                    act(ptc[:], bsc[:, :], AF.Exp, [bsc], [ptc], scale=0.125)
                    pws = []
                    for bi, bw in enumerate(bws):
                        ncol = min(4, nw - 4 * bi) * 128
                        pw = PTw[pw_ % 4]; pw_ += 1
                        act(pw[:, 0:ncol], bw[:, 0:ncol], AF.Exp, [bw], [pw], scale=0.125)
                        pws.append(pw)
                    nk = 4 + nw; kk = 0
                    for kt in range(4):
                        mm(acc[:, 0:65], ptc[:, kt * 128:(kt + 1) * 128], cvx[:, kt, h, 0:65], kk == 0, kk == nk - 1, [ptc, cvx], [acc]); kk += 1
                    for idx, (pi, i) in enumerate(pairs_j):
                        pw = pws[idx // 4]; col = (idx % 4) * 128
                        mm(acc[:, 0:65], pw[:, col:col + 128], vx[:, i, h, 0:65], kk == 0, kk == nk - 1, [pw, vx], [acc]); kk += 1
                    S.op("dve", lambda e, o=rr[:, j:j + 1], i_=acc[:, 64:65]: e.reciprocal(o, i_), [acc], [rr])
                    ts("dve", oall[:, j, h * 64:(h + 1) * 64], acc[:, 0:64], rr[:, j:j + 1], ALU.mult, [acc, rr], [oall])

                prev = d_score(units[0])
                for ui, u in enumerate(units):
                    nxt = d_score(units[ui + 1]) if ui + 1 < len(units) else None
                    d_rest(u, prev)
                    prev = nxt
            else:
                for (t0, nt) in seqs:
                    for h in range(8):
                        c = h // 2; pb = slice(64 * (h % 2), 64 * (h % 2) + 64)
                        bsc = bank_r(2, 8)
                        for kt in range(2):
                            mm(bsc[:, kt * 256:(kt + 1) * 256], kT[pb, c, (t0 + kt) * 128:(t0 + kt + 1) * 128], qT[pb, c, t0 * 128:(t0 + 2) * 128], True, True, [kT, qT], [bsc])
                        ptc = PTc[h % 2]
                        act(ptc[:], bsc[:, :], AF.Exp, [bsc], [ptc], scale=0.125)
                        acc = banks[h % 2]
                        for jq in range(2):
                            for kt in range(2):
                                mm(acc[:, jq * 128:jq * 128 + 65], ptc[:, kt * 256 + jq * 128:kt * 256 + (jq + 1) * 128], vx[:, t0 + kt, h, 0:65], kt == 0, kt == 1, [ptc, vx], [acc])
                        for jq in range(2):
                            S.op("dve", lambda e, o=rr[:, jq:jq + 1], i_=acc[:, jq * 128 + 64:jq * 128 + 65]: e.reciprocal(o, i_), [acc], [rr])
                            ts("dve", oall[:, t0 + jq, h * 64:(h + 1) * 64], acc[:, jq * 128:jq * 128 + 64], rr[:, jq:jq + 1], ALU.mult, [acc, rr], [oall])
            for t in range(NT):
                bt = bank_r(2, 8)
                bb = bt.ap[:].bitcast(BF16)
                for kc in range(4):
                    tr(bb[:, kc * 128:(kc + 1) * 128], oall[:, t, kc * 128:(kc + 1) * 128], identb[:], [oall, identb], [bt])
                cp("act", OT[3][:, :, t * 128:(t + 1) * 128], bb[:, 0:512].rearrange("p (a b) -> p a b", a=4), [bt], R_OT[3])

        def mixers(l, g):
            sel = "ACBD"
            if "A" in sel:
                try:
                    mixer_A(l, g)
                except _Stop:
                    pass
            if "C" in sel:
                mixer_C(l, g)
            if "B" in sel:
                mixer_B(l, g)
            if "D" in sel:
                mixer_D(l, g)
        self.mixers = mixers

        self.h = dict(locals())
        groups = [dict(name="p", NT=4, T=512, seqs=[(0, 2), (2, 2)], lat=False, xd=xp, yd=ys if False else yp, cond=c_ctx),
                  dict(name="s", NT=8, T=1024, seqs=[(0, 8)], lat=True, xd=xs, yd=ys, cond=c_s)]
        if self.stage >= 1:
            compute_mod_all()
        for gi_, g in enumerate(groups):
            NT = g["NT"]
            S.dma("sp", X[:, 0:NT, :], g["xd"].rearrange("(t p) d -> p t d", p=128), writes=[X])
            for l in range(DEPTH):
                if self.stage < 1:
                    break
                self.cm = MOD[l][gi_]
                norm_to_hT(NT, S1[l][gi_], self.cm[:, 0:8], self.cm)
                self.dbg(f"hT_{g['name']}{l}", HT, HT[:, :, 0:g["T"]], [128, 8, g["T"]])
                if self.stage < 2:
                    break
                if self.stage >= 4:
                    self.mixers(l, g)
                    self.dbg(f"OT_{g['name']}{l}", list(RS), OTall.ap[:, :], [128, 16384])
                if self.stage >= 5:
                    merge_out(l, NT, g["T"])
                    self.dbg(f"x1_{g['name']}{l}", X, X[:, 0:NT, :], [128, NT, D])
                norm_to_hT(NT, S2[l][gi_], self.cm[:, 24:32], self.cm)
                ffn(l, NT, g["T"])
                self.dbg(f"x2_{g['name']}{l}", X, X[:, 0:NT, :], [128, NT, D])
            AR.reset()
            final_norm(NT, g["yd"])
            AR.reset()
        S.emit_ctx = nc.allow_non_contiguous_dma(reason="small strided constant loads")
        with S.emit_ctx:
            S.emit()
        return self


def _core_inputs(inp, c, consts, na_mask):
    f = lambda a: np.ascontiguousarray(a, dtype=np.float32)
    m = {
        "xp": f(inp["x_prompt"][2 * c:2 * c + 2].reshape(512, D)), "xs": f(inp["x_sample"][c]),
        "c": f(inp["c"][c:c + 1]), "c_ctx": f(inp["c_ctx"].reshape(1, D)),
        "state_gdn": f(inp["state_gdn"][c]), "state_hgrn": f(inp["state_hgrn"][c]),
        "cache_diff_k": f(inp["cache_diff_k"][c]), "cache_diff_v": f(inp["cache_diff_v"][c]),
        "cache_na_k": f(inp["cache_na_k"][c]), "cache_na_v": f(inp["cache_na_v"][c]),
        "w_ada": f(inp["w_ada"]), "b_ada": f(inp["b_ada"]), "norm1_g": f(inp["norm1_g"]), "norm2_g": f(inp["norm2_g"]),
        "final_norm_g": f(inp["final_norm_g"].reshape(1, D)), "w_in": f(inp["w_in"]), "gdn_conv_w": f(inp["gdn_conv_w"]),
        "gdn_A_log": f(inp["gdn_A_log"].reshape(2, 8)), "gdn_dt_bias": f(inp["gdn_dt_bias"].reshape(2, 8)),
        "gdn_norm_g": f(inp["gdn_norm_g"]), "diff_lambda": f(inp["diff_lambda"].reshape(2, 256)),
        "diff_norm_g": f(inp["diff_norm_g"]), "hgrn_lb_logits": f(inp["hgrn_lb_logits"]), "hgrn_norm_g": f(inp["hgrn_norm_g"]),
        "na_rpb": f(inp["na_rpb"].reshape(2, 120, 31)), "w_branch": f(inp["w_branch"]), "w_out": f(inp["w_out"]),
        "w_ffn_gate": f(inp["w_ffn_gate"]), "w_ffn_up": f(inp["w_ffn_up"]), "w_ffn_down": f(inp["w_ffn_down"]),
        "k_na_mask": na_mask, "k_zeros": np.zeros((128, 2048), np.float32),
    }
    for k, v in consts.items():
        m["k_" + k] = v
    return m


_PROG = {}


def _run(inputs, cores, debug=(), stage=99):
    key = (tuple(sorted(debug)), stage)
    if key not in _PROG:
        _PROG[key] = Prog(debug, stage).build()
    P = _PROG[key]
    consts = _consts()
    na_mask = _na_masks()
    in_maps = [_core_inputs(inputs, c, consts, na_mask) for c in cores]
    res = run_bass_kernel_spmd(P.nc, in_maps, core_ids=list(range(len(cores))))
    return P, res.results


def kernel(**inputs):
    P, rs = _run(inputs, list(range(8)))
    cat = lambda k: np.concatenate([r[k] for r in rs], axis=0)
    y_prompt = cat("yp").reshape(16, 256, D)
    y_sample = np.stack([r["ys"] for r in rs], 0)
    return (y_prompt.astype(np.float32), y_sample.astype(np.float32), cat("o_st_gdn"), cat("o_ck_b"), cat("o_cv_b"),
            cat("o_st_hg"), cat("o_ck_d"), cat("o_cv_d"))
```

```python
import math
import os
import numpy as np
import ml_dtypes
import concourse.bass as bass
import concourse.mybir as mybir
from concourse.bass_utils import run_bass_kernel_spmd

F32 = mybir.dt.float32
BF16 = mybir.dt.bfloat16
AF = mybir.ActivationFunctionType
ALU = mybir.AluOpType
AX = mybir.AxisListType

D = 1024
DEPTH = 2
D_IN = 11792
D_FF = 2816
EPS = 1e-6
OFF = dict(aq=0, ak=512, av=1024, az=1536, aa=2048, ab=2056, bq=2064, bk=2576, bv=3088, cq=3600, cf=4112,
           ci=5136, cg=5648, dq=6160, dk=6672, dv=7184, mg=7696)
NEG = -30000.0
NDMA_SEMS = 8
STRICT_WAR = True


class _Stop(Exception):
    pass


def ck(i):
    if i > 999:
        raise _Stop()


class Res:
    __slots__ = ("name", "w", "r", "excl")

    def __init__(self, name="r"):
        self.name = name
        self.w = None
        self.r = []
        self.excl = False


class Tile:
    def __init__(self, ap, name="t", res=None):
        self.ap = ap
        self.res = res or Res(name)

    def __getitem__(self, k):
        return self.ap[k]


class Sched:
    ENG = ("pe", "dve", "act", "pool", "sp")

    def __init__(self, nc):
        self.nc = nc
        self.ops = {e: [] for e in self.ENG}
        self.sem = {e: nc.alloc_semaphore("s_" + e) for e in self.ENG}
        self.dsem = {e: [nc.alloc_semaphore(f"d_{e}{i}") for i in range(NDMA_SEMS)] for e in ("sp", "pool", "act")}
        self.dcnt = {e: [0] * NDMA_SEMS for e in self.dsem}
        self.drr = {e: 0 for e in self.dsem}
        self.out_tokens = []
        self.n = 0
        self.sb_bytes = 0

    def sb(self, shape, dtype=F32, name=None):
        self.n += 1
        name = name or f"t{self.n}"
        h = self.nc.alloc_sbuf_tensor(name, list(shape), dtype)
        fb = int(np.prod(shape[1:])) * (2 if dtype == BF16 else 4)
        self.sb_bytes += fb
        return Tile(h[:] if False else h, name)

    def ps(self, shape, dtype=F32, name=None):
        self.n += 1
        name = name or f"p{self.n}"
        t = Tile(self.nc.alloc_psum_tensor(name, list(shape), dtype), name)
        t.res.excl = True
        return t

    @staticmethod
    def _res(x):
        return x.res if isinstance(x, Tile) else x

    def _deps(self, eng, reads, writes):
        waits = []
        for r in reads:
            r = self._res(r)
            if r.w is not None:
                waits.append(r.w)
            if r.excl:
                for t in r.r:
                    if t[0] == "e" and t[1] != eng:
                        waits.append(t)
        for w in writes:
            w = self._res(w)
            if w.w is not None:
                waits.append(w.w)
            for t in w.r:
                if STRICT_WAR or not (t[0] == "e" and t[1] == eng):
                    waits.append(t)
        out = []
        for t in waits:
            if t[0] == "e" and t[1] == eng and eng == "pe":
                continue
            out.append(t)
        return out

    def _commit(self, tok, reads, writes):
        for r in reads:
            rr = self._res(r)
            if tok[0] == "e":
                rr.r = [t for t in rr.r if not (t[0] == "e" and t[1] == tok[1])]
            rr.r.append(tok)
        for w in writes:
            w = self._res(w)
            w.w = tok
            w.r = []

    def op(self, eng, fn, reads=(), writes=()):
        idx = len(self.ops[eng])
        tok = ("e", eng, idx)
        waits = self._deps(eng, reads, writes)
        self.ops[eng].append(dict(fn=fn, waits=waits, kind="c"))
        self._commit(tok, reads, writes)
        return tok

    def dma(self, q, out, in_, reads=(), writes=(), is_output=False):
        k = self.drr[q]
        self.drr[q] = (k + 1) % NDMA_SEMS
        prev = self.dcnt[q][k]
        self.dcnt[q][k] += 16
        tok = ("d", q, k, self.dcnt[q][k])
        waits = self._deps(q, reads, writes)
        if prev:
            waits.append(("d", q, k, prev))
        self.ops[q].append(dict(fn=lambda e, o=out, i=in_: e.dma_start(out=o, in_=i), waits=waits, kind="d",
                                dsem=self.dsem[q][k]))
        self._commit(tok, reads, writes)
        if is_output:
            self.out_tokens.append(tok)
        return tok

    def emit(self):
        nc = self.nc
        self.ops["sp"].append(dict(fn=None, waits=list(self.out_tokens), kind="w"))
        need = {e: set() for e in self.ENG}
        for e in self.ENG:
            for o in self.ops[e]:
                for t in o["waits"]:
                    if t[0] == "e":
                        need[t[1]].add(t[2])
        val = {}
        for e in self.ENG:
            c = 0
            for i, o in enumerate(self.ops[e]):
                if o["kind"] == "c" and i in need[e]:
                    c += 1
                    val[(e, i)] = c
        self.stats = {e: len(self.ops[e]) for e in self.ENG}
        nwait = {e: 0 for e in self.ENG}

        def run(e, engh):
            seen = {}
            for i, o in enumerate(self.ops[e]):
                for t in o["waits"]:
                    if t[0] == "e":
                        key, v, sem = ("e", t[1]), val[(t[1], t[2])], self.sem[t[1]]
                    else:
                        key, v, sem = ("d", t[1], t[2]), t[3], self.dsem[t[1]][t[2]]
                    if seen.get(key, 0) >= v:
                        continue
                    seen[key] = v
                    engh.wait_ge(sem, v)
                    nwait[e] += 1
                if o["fn"] is None:
                    continue
                ins = o["fn"](engh)
                if o["kind"] == "d":
                    ins.then_inc(o["dsem"], 16)
                elif i in need[e]:
                    ins.then_inc(self.sem[e], 1)

        with nc.Block() as block:
            @block.tensor
            def _(e):
                run("pe", e)

            @block.vector
            def _(e):
                run("dve", e)

            @block.scalar
            def _(e):
                run("act", e)

            @block.gpsimd
            def _(e):
                run("pool", e)

            @block.sync
            def _(e):
                run("sp", e)
        self.nwait = nwait


class Arena:
    def __init__(self, S, nbytes, name):
        self.S = S
        self.n = nbytes // 2
        self.t = S.nc.alloc_sbuf_tensor(name, [128, self.n], BF16)
        S.sb_bytes += nbytes
        self.live = []
        self.off = 0
        self.name = name

    def reset(self):
        toks = list(self.pending)
        for r in self.live:
            if r.w is not None:
                toks.append(r.w)
            toks.extend(r.r)
        self.pending = list(dict.fromkeys(toks))
        self.live = []
        self.off = 0

    pending = []

    def mark(self):
        return (self.off, len(self.live))

    def release(self, mark):
        toks = list(self.pending)
        for r in self.live[mark[1]:]:
            if r.w is not None:
                toks.append(r.w)
            toks.extend(r.r)
        self.pending = list(dict.fromkeys(toks))
        self.live = self.live[:mark[1]]
        self.off = mark[0]

    def view(self, shape, dtype=BF16, name="v"):
        nel = int(np.prod(shape))
        nb = nel * (2 if dtype == BF16 else 4)
        nb = (nb + 3) // 4 * 4
        assert self.off + nb // 2 <= self.n, f"arena {self.name} overflow: {self.off * 2 + nb} > {self.n * 2} ({name})"
        ap = self.t[:, self.off:self.off + nb // 2]
        if dtype != BF16:
            ap = ap.bitcast(dtype)
        ap = ap[:, 0:nel]
        if len(shape) == 2:
            ap = ap.rearrange("p (a b) -> p a b", a=shape[0])
        elif len(shape) == 3:
            ap = ap.rearrange("p (a b c) -> p a b c", a=shape[0], b=shape[1])
        elif len(shape) == 4:
            ap = ap.rearrange("p (a b c d) -> p a b c d", a=shape[0], b=shape[1], c=shape[2])
        self.off += nb // 2
        r = Res(name)
        r.r = list(self.pending)
        self.live.append(r)
        return Tile(ap, name, r)


def _consts():
    c = {}
    c["ident"] = np.eye(128, dtype=np.float32)
    c["ones"] = np.ones((128, 128), np.float32)
    r = np.arange(128)
    same = (r[:, None] // 64) == (r[None, :] // 64)
    c["tri2_f"] = (same & (r[:, None] <= r[None, :])).astype(np.float32)
    c["tri2_b"] = (same & (r[:, None] >= r[None, :])).astype(np.float32)
    c["ones2"] = same.astype(np.float32)
    c["csel0"] = np.repeat((r < 64).astype(np.float32)[:, None], 128, 1)
    c["csel1"] = np.repeat((r >= 64).astype(np.float32)[:, None], 128, 1)
    vf = same & (r[:, None] <= r[None, :])
    vfs = same & (r[:, None] < r[None, :])
    vb = same & (r[:, None] >= r[None, :])
    vbs = same & (r[:, None] > r[None, :])
    c["nm_f_incl"] = np.where(vf, 0.0, NEG).astype(np.float32)
    c["nm_f_str"] = np.where(vfs, 0.0, NEG).astype(np.float32)
    c["nm_b_incl"] = np.where(vb, 0.0, NEG).astype(np.float32)
    c["nm_b_str"] = np.where(vbs, 0.0, NEG).astype(np.float32)
    c["pm_f_str"] = np.where(vbs, 0.0, -NEG).astype(np.float32)
    c["pm_b_str"] = np.where(vfs, 0.0, -NEG).astype(np.float32)
    c["m01_f"] = (r[:, None] <= r[None, :]).astype(np.float32)
    c["m01_b"] = (r[:, None] >= r[None, :]).astype(np.float32)
    t = np.arange(1024)
    c["smask128"] = np.repeat((t % 128 != 0).astype(np.float32)[None, :], 128, 0)
    c["smask16"] = np.repeat((t % 16 != 0).astype(np.float32)[None, :], 128, 0)
    nf = 16
    inv = 10000.0 ** (-np.arange(nf, dtype=np.float32) / nf)
    tt = np.arange(1024)
    ang = np.concatenate([(tt // 64).astype(np.float32)[:, None] * inv, (tt % 64).astype(np.float32)[:, None] * inv], -1)
    c["rope_cos"] = np.cos(ang).astype(np.float32).reshape(8, 128, 32).transpose(1, 0, 2).copy()
    c["rope_sin"] = np.sin(ang).astype(np.float32).reshape(8, 128, 32).transpose(1, 0, 2).copy()
    c["eps"] = np.full((128, 1), EPS, np.float32)
    ii = np.arange(8)[:, None]; ss = np.arange(128)[None, :]
    c["hm_f"] = np.repeat(np.where(ss < 16 * (ii + 1), 0.0, NEG).astype(np.float32)[None], 128, 0)
    c["hm_b"] = np.repeat(np.where(ss >= 16 * ii, 0.0, NEG).astype(np.float32)[None], 128, 0)
    return c


NA_PAIRS = []
for _j in range(8):
    _rows = []
    for _r in (2 * _j, 2 * _j + 1):
        _r0 = min(max(_r - 4, 0), 8)
        _rows += list(range(_r0, _r0 + 8))
    for _i in sorted(set(x // 2 for x in _rows)):
        NA_PAIRS.append((_j, _i))


def _na_masks():
    col = np.arange(64)
    c0 = np.clip(col - 8, 0, 48)
    colok = (col[None, :] >= c0[:, None]) & (col[None, :] < c0[:, None] + 16)
    m = np.zeros((len(NA_PAIRS), 128, 128), np.float32)
    for p, (j, i) in enumerate(NA_PAIRS):
        for a in range(2):
            for b in range(2):
                kr, qr = 2 * i + a, 2 * j + b
                r0 = min(max(qr - 4, 0), 8)
                rowok = r0 <= kr < r0 + 8
                blk = np.where(colok.T & rowok, 0.0, NEG * 8)
                m[p, a * 64:(a + 1) * 64, b * 64:(b + 1) * 64] = blk
    return m.transpose(1, 0, 2).copy()


class Prog:
    def __init__(self, debug=(), stage=99):
        self.debug = set(debug)
        self.stage = stage
        nc = self.nc = bass.Bass("TRN2", target_bir_lowering=False)
        self.S = Sched(nc)
        self.dbg_outs = {}
        self.ins = {}
        self.outs = {}

    def din(self, name, shape):
        self.ins[name] = self.nc.dram_tensor(name, list(shape), F32, kind="ExternalInput").ap()
        return self.ins[name]

    def dout(self, name, shape):
        self.outs[name] = self.nc.dram_tensor(name, list(shape), F32, kind="ExternalOutput").ap()
        return self.outs[name]

    def dbg(self, name, tile, ap, shape):
        if name not in self.debug:
            return
        o = self.nc.dram_tensor("dbg_" + name, list(shape), ap.dtype, kind="ExternalOutput").ap()
        self.dbg_outs[name] = o
        self.S.dma("sp", o, ap, reads=(tile if isinstance(tile, list) else [tile]), is_output=True)

    def mm(self, out, lhsT, rhs, start, stop, R, W):
        self.S.op("pe", lambda e: e.matmul(out, lhsT=lhsT, rhs=rhs, start=start, stop=stop), R, W)

    def tr(self, out, in_, ident, R, W):
        self.S.op("pe", lambda e: e.transpose(out, in_, ident), R, W)

    def tt(self, eng, out, in0, in1, op, R, W):
        self.S.op(eng, lambda e: e.tensor_tensor(out=out, in0=in0, in1=in1, op=op), R, W)

    def ts(self, eng, out, in0, s1, op0, R, W, s2=None, op1=None):
        if op1 is None:
            self.S.op(eng, lambda e: e.tensor_scalar(out=out, in0=in0, scalar1=s1, scalar2=None, op0=op0), R, W)
        else:
            self.S.op(eng, lambda e: e.tensor_scalar(out=out, in0=in0, scalar1=s1, scalar2=s2, op0=op0, op1=op1), R, W)

    def stt(self, out, in0, scalar, in1, op0, op1, R, W):
        self.S.op("dve", lambda e: e.scalar_tensor_tensor(out=out, in0=in0, scalar=scalar, in1=in1, op0=op0, op1=op1), R, W)

    def act(self, out, in_, func, R, W, bias=None, scale=1.0, accum=None):
        kw = {}
        if bias is not None:
            kw["bias"] = bias
        if accum is not None:
            kw["accum_out"] = accum
        self.S.op("act", lambda e: e.activation(out, in_, func, scale=scale, **kw), R, W)

    def cp(self, eng, out, in_, R, W):
        if eng == "act":
            self.S.op("act", lambda e: e.copy(out, in_), R, W)
        else:
            self.S.op(eng, lambda e: e.tensor_copy(out, in_), R, W)

    def recip(self, out, in_, R, W):
        self.S.op("dve", lambda e: e.reciprocal(out, in_), R, W)

    def build(self):
        nc, S = self.nc, self.S
        din, dout = self.din, self.dout
        mm, tr, tt, ts, stt, act, cp = self.mm, self.tr, self.tt, self.ts, self.stt, self.act, self.cp
        xp = din("xp", [512, D]); xs = din("xs", [1024, D])
        c_s = din("c", [1, D]); c_ctx = din("c_ctx", [1, D])
        st_gdn = din("state_gdn", [2, 2, 4, 128, 128]); st_hg = din("state_hgrn", [2, 2, 4, 128, 128])
        ck_b = din("cache_diff_k", [2, 4, 512, 128]); cv_b = din("cache_diff_v", [2, 4, 512, 128])
        ck_d = din("cache_na_k", [2, 8, 512, 64]); cv_d = din("cache_na_v", [2, 8, 512, 64])
        w_ada = din("w_ada", [2, D, 6 * D]); b_ada = din("b_ada", [2, 6 * D])
        n1g = din("norm1_g", [2, D]); n2g = din("norm2_g", [2, D]); fng = din("final_norm_g", [1, D])
        w_in = din("w_in", [2, D, D_IN]); conv_w = din("gdn_conv_w", [2, 1536, 5])
        a_log = din("gdn_A_log", [2, 8]); dt_b = din("gdn_dt_bias", [2, 8])
        gdn_ng = din("gdn_norm_g", [2, 128]); dlam = din("diff_lambda", [2, 256]); diff_ng = din("diff_norm_g", [2, 128])
        lb_log = din("hgrn_lb_logits", [2, 2, 512]); hg_ng = din("hgrn_norm_g", [2, 128])
        rpb = din("na_rpb", [2, 120, 31])
        w_br = din("w_branch", [2, 4, 512, D]); w_out = din("w_out", [2, D, D])
        w_fg = din("w_ffn_gate", [2, D, D_FF]); w_fu = din("w_ffn_up", [2, D, D_FF]); w_fd = din("w_ffn_down", [2, D_FF, D])
        cdram = {k: din("k_" + k, list(v.shape)) for k, v in _consts().items()}
        na_mask_d = din("k_na_mask", [128, len(NA_PAIRS), 128])
        zeros_d = din("k_zeros", [128, 2048])
        yp = dout("yp", [512, D]); ys = dout("ys", [1024, D])
        o_st_gdn = dout("o_st_gdn", [2, 2, 2, 4, 128, 128]); o_st_hg = dout("o_st_hg", [2, 2, 2, 4, 128, 128])
        o_ck_b = dout("o_ck_b", [2, 2, 4, 256, 128]); o_cv_b = dout("o_cv_b", [2, 2, 4, 256, 128])
        o_ck_d = dout("o_ck_d", [2, 2, 8, 256, 64]); o_cv_d = dout("o_cv_d", [2, 2, 8, 256, 64])
        rpb_scr = nc.dram_tensor("rpb_scr", [120, 64, 127], F32).ap()
        self.rpb_scr_res = Res("rpb_scr")

        K = {}
        for k, v in _consts().items():
            if k in ("smask128", "smask16", "hm_f", "hm_b"):
                continue
            K[k] = S.sb(list(v.shape), F32, "sk_" + k)
            S.dma("sp", K[k][:], cdram[k], writes=[K[k]])
        identb = S.sb([128, 128], BF16, "identb")
        S.dma("pool", identb[:], cdram["ident"], writes=[identb])
        onesb = S.sb([128, 128], BF16, "onesb")
        S.dma("pool", onesb[:], cdram["ones"], writes=[onesb])
        sm128 = S.sb([128, 1024], BF16, "sm128"); sm16 = S.sb([128, 1024], BF16, "sm16")
        S.dma("pool", sm128[:], cdram["smask128"], writes=[sm128]); S.dma("pool", sm16[:], cdram["smask16"], writes=[sm16])
        namask = S.sb([128, len(NA_PAIRS), 128], BF16, "namask")
        S.dma("pool", namask[:], na_mask_d, writes=[namask])
        onecol = S.sb([128, 1], F32, "onecol")
        S.dma("sp", onecol[:], cdram["ones"][:, 0:1], writes=[onecol])
        self.K = K
        identf, onesf, epsc = K["ident"], K["ones"], K["eps"]

        banks = [S.ps([128, 512], F32, f"bank{i}") for i in range(8)]
        self._bk = 0

        def bank():
            b = banks[self._bk]
            self._bk = (self._bk + 1) % 8
            return b

        X = S.sb([128, 8, D], F32, "X")
        HT = S.sb([128, 8, 1024], BF16, "HT")
        OTall = S.sb([128, 16384], BF16, "OTall")
        _slot = [0, 2, 1, 3]
        OT = [OTall.ap[:, _slot[n] * 4096:(_slot[n] + 1) * 4096].rearrange("p (a b) -> p a b", a=4) for n in range(4)]
        RS = [Res(f"otslot{i}") for i in range(4)]
        R_OT = [[RS[_slot[n]]] for n in range(4)]
        R_OACC = [RS[2], RS[3]]
        R_WD = [RS[0], RS[1], RS[2]]
        zcol = S.sb([128, 8], F32, "zcol")
        S.dma("sp", zcol[:], zeros_d[:, 0:8], writes=[zcol])
        NW = 3
        wring = [S.sb([128, 4096], BF16, f"w{i}") for i in range(NW)]
        self._wr = 0

        def wslot():
            w = wring[self._wr]
            self._wr = (self._wr + 1) % NW
            return w
        modc = S.sb([128, 48], F32, "modc"); badac = S.sb([128, 48], F32, "badac")
        n1c = S.sb([128, 8], F32, "n1c"); n2c = S.sb([128, 8], F32, "n2c")
        s1c = S.sb([128, 8], F32, "s1c"); s2c = S.sb([128, 8], F32, "s2c")
        condT = S.sb([128, 8], F32, "condT"); scb = S.sb([128, 8], BF16, "scb")
        GB = S.sb([128, D], F32, "GB")
        gcolb = [S.sb([128, 128], F32, f"gcolb{i}") for i in range(2)]
        xn = [S.sb([128, D], BF16, f"xn{i}") for i in range(2)]
        sm = [S.sb([128, 8], F32, f"sm{i}") for i in range(4)]
        self._sm = 0

        def small():
            t = sm[self._sm]
            self._sm = (self._sm + 1) % 4
            return t
        AR = Arena(S, 67 * 1024, "arena")

        def load_w(dram_ap, shape_free, view_fn=None):
            w = wslot()
            nel = int(np.prod(shape_free))
            ap = w.ap[:, 0:nel]
            if len(shape_free) == 2:
                ap = ap.rearrange("p (a b) -> p a b", a=shape_free[0])
            elif len(shape_free) == 3:
                ap = ap.rearrange("p (a b c) -> p a b c", a=shape_free[0], b=shape_free[1])
            S.dma("pool", ap, dram_ap, writes=[w])
            return w, ap

        def w_kc(wmat, c0, nc_):
            return wmat[:, c0:c0 + nc_].rearrange("(kc p) n -> p kc n", p=128)

        MOD = [[S.sb([128, 48], F32, f"mod{l}{gi}") for gi in range(2)] for l in range(2)]
        S1 = [[S.sb([128, 8], F32, f"s1_{l}{gi}") for gi in range(2)] for l in range(2)]
        S2 = [[S.sb([128, 8], F32, f"s2_{l}{gi}") for gi in range(2)] for l in range(2)]
        condT2 = S.sb([128, 8, 2], F32, "condT2"); scb2 = S.sb([128, 8, 2], BF16, "scb2")

        def compute_mod_all():
            S.dma("sp", condT2[:, :, 0], c_ctx.rearrange("o (kc p) -> p (o kc)", p=128), writes=[condT2])
            S.dma("sp", condT2[:, :, 1], c_s.rearrange("o (kc p) -> p (o kc)", p=128), writes=[condT2])
            act(scb2[:], condT2[:], AF.Silu, [condT2], [scb2])
            for l in range(DEPTH):
                S.dma("sp", badac[:], b_ada[l:l + 1, :].rearrange("o (j p) -> p (o j)", p=128), writes=[badac])
                S.dma("sp", n1c[:], n1g[l:l + 1, :].rearrange("o (j p) -> p (o j)", p=128), writes=[n1c])
                S.dma("sp", n2c[:], n2g[l:l + 1, :].rearrange("o (j p) -> p (o j)", p=128), writes=[n2c])
                b = bank()
                for ch in range(12):
                    w, wv = load_w(w_kc(w_ada[l], ch * 512, 512), [8, 512])
                    for jj in range(4):
                        j = ch * 4 + jj
                        for kc in range(8):
                            mm(b[:, 2 * j:2 * j + 2], wv[:, kc, jj * 128:(jj + 1) * 128], scb2[:, kc, :], kc == 0, kc == 7, [w, scb2], [b])
                for gi in range(2):
                    m_ = MOD[l][gi]
                    tt("dve", m_[:], b[:, gi:96:2], badac[:], ALU.add, [b, badac], [m_])
                    stt(S1[l][gi][:], m_[:, 8:16], 1.0, n1c[:], ALU.add, ALU.mult, [m_, n1c], [S1[l][gi]])
                    stt(S2[l][gi][:], m_[:, 32:40], 1.0, n2c[:], ALU.add, ALU.mult, [m_, n2c], [S2[l][gi]])

        def gate_bcast(which):
            for half in range(2):
                b = bank()
                for q in range(4):
                    fc = half * 4 + q
                    gt = gcolb[fc % 2]
                    cp("dve", gt[:], self.cm[:, which * 8 + fc:which * 8 + fc + 1].to_broadcast([128, 128]), [self.cm], [gt])
                    mm(b[:, q * 128:(q + 1) * 128], gt[:], identf[:], True, True, [gt, identf], [b])
                cp("act", GB[:, half * 512:(half + 1) * 512], b[:], [b], [GB])

        def norm_to_hT(NT, scol, shcol_ap, shcol_t):
            for t in range(NT):
                junk = xn[(t + 1) % 2]
                st = small()
                act(junk[:], X[:, t, :], AF.Square, [X], [junk, st], accum=st[:, 0:1])
                act(st[:, 1:2], st[:, 0:1], AF.Ln, [st, epsc], [st], bias=epsc[:], scale=1.0 / D)
                act(st[:, 2:3], st[:, 1:2], AF.Exp, [st], [st], scale=-0.5)
                xt = xn[t % 2]
                ts("dve", xt[:], X[:, t, :], st[:, 2:3], ALU.mult, [X, st], [xt])
                b = bank()
                bb = b.ap[:].bitcast(BF16)
                for kc in range(8):
                    tr(bb[:, kc * 128:(kc + 1) * 128], xt[:, kc * 128:(kc + 1) * 128], identb[:], [xt, identb], [b])
                for kc in range(8):
                    ts("dve", HT[:, kc, t * 128:(t + 1) * 128], bb[:, kc * 128:(kc + 1) * 128], scol[:, kc:kc + 1], ALU.mult,
                       [b, scol, shcol_t], [HT], s2=shcol_ap[:, kc:kc + 1], op1=ALU.add)

        def final_norm(NT, ydram):
            fg = AR.view([D], F32, "fng_b")
            S.dma("sp", fg[:], fng.partition_broadcast(128).rearrange("p o d -> p (o d)") if False else fng[0:1, :].partition_broadcast(128).rearrange("p o d -> p (o d)"), writes=[fg])
            for t in range(NT):
                junk = xn[(t + 1) % 2]
                st = small()
                act(junk[:], X[:, t, :], AF.Square, [X], [junk, st], accum=st[:, 0:1])
                act(st[:, 1:2], st[:, 0:1], AF.Ln, [st, epsc], [st], bias=epsc[:], scale=1.0 / D)
                act(st[:, 2:3], st[:, 1:2], AF.Exp, [st], [st], scale=-0.5)
                yo = AR.view([D], F32, "yo")
                stt(yo[:], X[:, t, :], st[:, 2:3], fg[:], ALU.mult, ALU.mult, [X, st, fg], [yo])
                S.dma("sp", ydram[t * 128:(t + 1) * 128, :], yo[:], reads=[yo], is_output=True)

        def ffn(l, NT, T):
            AR.reset()
            actT = AR.view([22, T], BF16, "actT")
            sgt = [AR.view([512], F32, f"sg{i}") for i in range(2)]
            gate_bcast(5)
            for ch in range(6):
                ncols = 512 if ch < 5 else 256
                wg, wgv = load_w(w_kc(w_fg[l], ch * 512, ncols), [8, ncols])
                wu, wuv = load_w(w_kc(w_fu[l], ch * 512, ncols), [8, ncols])
                for fc in range(ncols // 128):
                    ffc = ch * 4 + fc
                    for tg in range(T // 512):
                        bg = bank(); bu = bank()
                        for kc in range(8):
                            mm(bg[:, :], wgv[:, kc, fc * 128:(fc + 1) * 128], HT[:, kc, tg * 512:(tg + 1) * 512], kc == 0, kc == 7, [wg, HT], [bg])
                        for kc in range(8):
                            mm(bu[:, :], wuv[:, kc, fc * 128:(fc + 1) * 128], HT[:, kc, tg * 512:(tg + 1) * 512], kc == 0, kc == 7, [wu, HT], [bu])
                        sg = sgt[(ffc * 2 + tg) % 2]
                        act(sg[:], bg[:, :], AF.Silu, [bg], [sg])
                        tt("dve", actT[:, ffc, tg * 512:(tg + 1) * 512], sg[:], bu[:, :], ALU.mult, [sg, bu], [actT])
            wd = OTall.ap[:, 0:22 * 512].rearrange("p (f n) -> p f n", f=22)
            tmpx = [AR.view([512], F32, f"tmpx{i}") for i in range(2)]
            for half in range(2):
                S.dma("pool", wd, w_fd[l][:, half * 512:(half + 1) * 512].rearrange("(f p) n -> p f n", p=128), writes=R_WD)
                for t in range(NT):
                    b = bank()
                    for ffc in range(22):
                        mm(b[:, :], actT[:, ffc, t * 128:(t + 1) * 128], wd[:, ffc, :], ffc == 0, ffc == 21, [actT] + R_WD, [b])
                    tx = tmpx[t % 2]
                    tt("dve", tx[:], b[:, :], GB[:, half * 512:(half + 1) * 512], ALU.mult, [b, GB], [tx])
                    tt("dve", X[:, t, half * 512:(half + 1) * 512], X[:, t, half * 512:(half + 1) * 512], tx[:], ALU.add, [X, tx], [X])

        def merge_out(l, NT, T):
            AR.reset()
            MT = AR.view([8, T], BF16, "mergedT")
            sgt = [AR.view([512], F32, f"msg{i}") for i in range(2)]
            acc = [AR.view([512], F32, f"macc{i}") for i in range(2)]
            gate_bcast(2)
            for fc in range(8):
                wm = wslot()
                wmv = wm.ap[:, 0:4096].rearrange("p (a b c) -> p a b c", a=8, b=4)
                wb = wslot()
                wbv = wb.ap[:, 0:2048].rearrange("p (a b c) -> p a b c", a=4, b=4)
                for n in range(4):
                    c0 = OFF["mg"] + n * 1024 + fc * 128
                    S.dma("pool", wmv[:, :, n, :], w_in[l][:, c0:c0 + 128].rearrange("(kc p) c -> p kc c", p=128), writes=[wm])
                    S.dma("pool", wbv[:, n, :, :], w_br[l, n][:, fc * 128:(fc + 1) * 128].rearrange("(kc p) c -> p kc c", p=128), writes=[wb])
                for tg in range(T // 512):
                    tsl = slice(tg * 512, (tg + 1) * 512)
                    ac = acc[tg % 2]
                    for n in range(4):
                        bgt = bank(); bp = bank()
                        for kc in range(8):
                            mm(bgt[:, :], wmv[:, kc, n, :], HT[:, kc, tsl], kc == 0, kc == 7, [wm, HT], [bgt])
                        for kc in range(4):
                            mm(bp[:, :], wbv[:, n, kc, :], OT[n][:, kc, tsl], kc == 0, kc == 3, [wb] + R_OT[n], [bp])
                        sg = sgt[n % 2]
                        act(sg[:], bgt[:, :], AF.Sigmoid, [bgt], [sg])
                        if n == 0:
                            tt("dve", ac[:], sg[:], bp[:, :], ALU.mult, [sg, bp], [ac])
                        else:
                            tt("dve", sg[:], sg[:], bp[:, :], ALU.mult, [sg, bp], [sg])
                            if n < 3:
                                tt("dve", ac[:], ac[:], sg[:], ALU.add, [ac, sg], [ac])
                            else:
                                tt("dve", MT[:, fc, tsl], ac[:], sg[:], ALU.add, [ac, sg], [MT])
            tmpx = [AR.view([512], F32, f"mtmpx{i}") for i in range(2)]
            for half in range(2):
                wo, wov = load_w(w_kc(w_out[l], half * 512, 512), [8, 512])
                for t in range(NT):
                    b = bank()
                    for kc in range(8):
                        mm(b[:, :], MT[:, kc, t * 128:(t + 1) * 128], wov[:, kc, :], kc == 0, kc == 7, [MT, wo], [b])
                    tx = tmpx[t % 2]
                    tt("dve", tx[:], b[:, :], GB[:, half * 512:(half + 1) * 512], ALU.mult, [b, GB], [tx])
                    tt("dve", X[:, t, half * 512:(half + 1) * 512], X[:, t, half * 512:(half + 1) * 512], tx[:], ALU.add, [X, tx], [X])

        def proj_fm(w, wv, c0, tok0, ntok):
            b = bank()
            for kc in range(8):
                mm(b[:, 0:ntok], wv[:, kc, c0:c0 + 128], HT[:, kc, tok0:tok0 + ntok], kc == 0, kc == 7, [w, HT], [b])
            return b

        def proj_tm(w, wv, c0, ncols, t):
            b = bank()
            for kc in range(8):
                mm(b[:, 0:ncols], HT[:, kc, t * 128:(t + 1) * 128], wv[:, kc, c0:c0 + ncols], kc == 0, kc == 7, [HT, w], [b])
            return b

        def head_rms_out(l, g, oacc, ng_dram, gate_col, OTn, R_OTn, extra_scale=None):
            NT = g["NT"]
            ngb = AR.view([128], F32, "ngb")
            S.dma("sp", ngb[:], ng_dram[l:l + 1, :].partition_broadcast(128).rearrange("p o d -> p (o d)"), writes=[ngb])
            if extra_scale is not None:
                ts("dve", ngb[:], ngb[:], float(extra_scale), ALU.mult, [ngb], [ngb])
            gzs = [AR.view([4, 128], F32, f"gz{i}") for i in range(2)]
            obs = [AR.view([512], BF16, f"ob{i}") for i in range(2)]
            if gate_col is not None:
                wz, wzv = load_w(w_kc(w_in[l], gate_col, 512), [8, 512])
            def tile_gen(t):
                gz = gzs[t % 2]
                if gate_col is not None:
                    b = proj_tm(wz, wzv, 0, 512, t)
                    yield
                    act(gz[:].rearrange("p a b -> p (a b)"), b[:, :], AF.Silu, [b], [gz])
                st = sts[t % 2]
                junk = xn[t % 2]
                for h in range(4):
                    act(junk[:, h * 128:(h + 1) * 128], oacc[:, t, h * 128:(h + 1) * 128], AF.Square, R_OACC, [junk, st], accum=st[:, h:h + 1])
                act(st[:, 4:8], st[:, 0:4], AF.Ln, [st, epsc], [st], bias=epsc[:], scale=1.0 / 128)
                act(st[:, 4:8], st[:, 4:8], AF.Exp, [st], [st], scale=-0.5)
                yield
                ob = obs[t % 2]
                o3 = oacc[:, t, :].rearrange("p (a b) -> p a b", a=4)
                if gate_col is not None:
                    tt("dve", gz[:], gz[:], ngb[:].unsqueeze(1).to_broadcast([128, 4, 128]), ALU.mult, [gz, ngb], [gz])
                    tt("dve", gz[:], gz[:], st[:, 4:8].unsqueeze(2).to_broadcast([128, 4, 128]), ALU.mult, [gz, st], [gz])
                    tt("dve", ob[:].rearrange("p (a b) -> p a b", a=4), o3, gz[:], ALU.mult, R_OACC + [gz], [ob])
                else:
                    tt("dve", gz[:], o3, st[:, 4:8].unsqueeze(2).to_broadcast([128, 4, 128]), ALU.mult, R_OACC + [st], [gz])
                    tt("dve", ob[:].rearrange("p (a b) -> p a b", a=4), gz[:], ngb[:].unsqueeze(1).to_broadcast([128, 4, 128]), ALU.mult, [gz, ngb], [ob])
                yield
                b = bank()
                bb = b.ap[:].bitcast(BF16)
                for kc in range(4):
                    tr(bb[:, kc * 128:(kc + 1) * 128], ob[:, kc * 128:(kc + 1) * 128], identb[:], [ob, identb], [b])
                yield
                cp("act", OTn[:, :, t * 128:(t + 1) * 128], bb[:, 0:512].rearrange("p (a b) -> p a b", a=4), [b], R_OTn)
                yield

            sts = [AR.view([8], F32, f"hst{i}") for i in range(2)]
            for t0_ in range(0, NT, 2):
                gens = [tile_gen(t) for t in range(t0_, min(NT, t0_ + 2))]
                while gens:
                    for gg in list(gens):
                        try:
                            next(gg)
                        except StopIteration:
                            gens.remove(gg)

        def mixer_C(l, g):
            NT, T = g["NT"], g["T"]
            AR.reset()
            oacc = OTall.ap[:, 8192:16384].bitcast(F32)[:, 0:NT * 512].rearrange("p (t c) -> p t c", c=512)
            qb = AR.view([4, T], BF16, "c_qb")
            vtm = AR.view([NT, 512], BF16, "c_vtm")
            Bc = AR.view([NT, 128], F32, "c_B"); Bw = AR.view([NT, 128], F32, "c_Bw")
            tmpf = AR.view([NT, 128], F32, "c_tmp")
            logf = tmpf
            kf = AR.view([NT, 128], BF16, "c_kf")
            te8 = [AR.view([8, 128], F32, f"c_te8{i}") for i in range(2)]
            ks8 = [AR.view([8, 128], BF16, "c_ks8")]
            ATs = [AR.view([128], BF16, f"c_AT{i}") for i in range(2)]
            kdtm = [AR.view([128], BF16, f"c_kdtm{i}") for i in range(2)]
            S32 = AR.view([128], F32, "c_S32"); S16 = AR.view([128], BF16, "c_S16")
            tot16 = AR.view([NT * 8], F32, "c_tot16")
            lbl = AR.view([2, 8], F32, "c_lbl"); lbc = AR.view([8], F32, "c_lbc"); oml = AR.view([8], F32, "c_oml")
            for ll in range(2):
                S.dma("sp", lbl[:, ll, :], lb_log[ll].rearrange("d (h p) -> p (d h)", p=128), writes=[lbl])
            tt("dve", lbc[:], lbl[:, 1, :], lbl[:, 0, :], ALU.subtract, [lbl], [lbc])
            act(lbc[:], lbc[:], AF.Sigmoid, [lbc], [lbc])
            ts("dve", lbc[:], lbc[:], float(l), ALU.mult, [lbc], [lbc])
            ts("dve", oml[:], lbc[:], -1.0, ALU.mult, [lbc], [oml], s2=1.0, op1=ALU.add)
            wq, wqv = load_w(w_kc(w_in[l], OFF["cq"], 512), [8, 512])
            for h in range(4):
                for tg in range(T // 512):
                    b = proj_fm(wq, wqv, h * 128, tg * 512, 512)
                    act(qb[:, h, tg * 512:(tg + 1) * 512], b[:, :], AF.Silu, [b], [qb])
            wv_, wvv = load_w(w_kc(w_in[l], OFF["ci"], 512), [8, 512])
            for t in range(NT):
                b = proj_tm(wv_, wvv, 0, 512, t)
                cp("act", vtm[:, t, :], b[:, :], [b], [vtm])
            logf2 = logf[:].rearrange("p a b -> p (a b)"); B2 = Bc[:].rearrange("p a b -> p (a b)"); Bw2 = Bw[:].rearrange("p a b -> p (a b)")
            tmp2 = tmpf[:].rearrange("p a b -> p (a b)")
            sets = []
            for i in range(2):
                sets.append(dict(kfl=AR.view([NT, 128], F32, f"c_kfl{i}"), qs=AR.view([NT, 128], BF16, f"c_qs{i}"), qd=AR.view([NT, 128], BF16, f"c_qd{i}"),
                                 kdT=AR.view([NT, 128], BF16, f"c_kdT{i}"), Rall=AR.view([NT, 8], F32, f"c_Rall{i}"), tots=AR.view([2, NT], F32, f"c_tot{i}")))
            wfs = {}

            def pre_gen(dr, h, st):
                kfl, qs, qd, kdT, Rall, tots = st["kfl"], st["qs"], st["qd"], st["kdT"], st["Rall"], st["tots"]
                kfl2 = kfl[:].rearrange("p a b -> p (a b)")
                if h == 0:
                    wfs[dr] = load_w(w_kc(w_in[l], OFF["cf"] + dr * 512, 512), [8, 512])
                wf, wfv = wfs[dr]
                col = dr * 4 + h
                for tg in range(T // 512):
                    b = proj_fm(wf, wfv, h * 128, tg * 512, 512)
                    act(tmp2[:, tg * 512:(tg + 1) * 512], b[:, :], AF.Sigmoid, [b], [tmpf])
                    yield
                ts("dve", tmp2, tmp2, oml[:, col:col + 1], ALU.mult, [tmpf, oml, lbc], [tmpf], s2=lbc[:, col:col + 1], op1=ALU.add)
                yield
                ts("dve", tmp2, tmp2, 1e-30, ALU.max, [tmpf], [tmpf])
                yield
                ts("dve", kfl2, tmp2, -1.0, ALU.mult, [tmpf], [kfl], s2=1.0, op1=ALU.add)
                yield
                act(logf2, tmp2, AF.Ln, [tmpf], [logf])
                yield
                ts("dve", kf[:].rearrange("p a b -> p (a b)"), kfl2, 1.0, ALU.mult, [kfl], [kf])
                yield
                ts("dve", kfl2, kfl2, 1e-18, ALU.max, [kfl], [kfl])
                yield
                act(kfl2, kfl2, AF.Ln, [kfl], [kfl])
                yield
                S.op("dve", lambda e, o=B2, d0=sm128[:, 0:T], d1=logf2: e.tensor_tensor_scan(o, d0, d1, zcol[:, 0:1], ALU.mult, ALU.add),
                     [sm128, logf, zcol], [Bc])
                yield
                S.op("dve", lambda e, o=Bw2, d0=sm16[:, 0:T], d1=logf2: e.tensor_tensor_scan(o, d0, d1, zcol[:, 0:1], ALU.mult, ALU.add),
                     [sm16, logf, zcol], [Bw])
                yield
                cp("dve", tots[:, 0, :], Bc[:, :, 127], [Bc], [tots])
                yield
                if dr == 1:
                    tt("dve", Bc[:], tots[:, 0, :].unsqueeze(2).to_broadcast([128, NT, 128]), Bc[:], ALU.subtract, [tots, Bc], [Bc])
                    yield
                    tt("dve", Bc[:], Bc[:], logf[:], ALU.add, [Bc, logf], [Bc])
                    yield
                    bw3 = Bw2.rearrange("p (a b) -> p a b", b=16)
                    cp("dve", tot16[:], bw3[:, :, 15], [Bw], [tot16])
                    yield
                    tt("dve", bw3, tot16[:].unsqueeze(2).to_broadcast([128, NT * 8, 16]), bw3, ALU.subtract, [tot16, Bw], [Bw])
                    yield
                    tt("dve", Bw[:], Bw[:], logf[:], ALU.add, [Bw, logf], [Bw])
                    yield
                act(tots[:, 1, :], tots[:, 0, :], AF.Exp, [tots], [tots])
                yield
                act(tmp2, Bw2, AF.Exp, [Bw], [tmpf])
                yield
                tt("dve", qs[:].rearrange("p a b -> p (a b)"), qb[:, h, :], tmp2, ALU.mult, [qb, tmpf], [qs])
                yield
                act(tmp2, B2, AF.Exp, [Bc], [tmpf])
                yield
                tt("dve", qd[:].rearrange("p a b -> p (a b)"), qb[:, h, :], tmp2, ALU.mult, [qb, tmpf], [qd])
                yield
                tt("dve", tmpf[:], tots[:, 0, :].unsqueeze(2).to_broadcast([128, NT, 128]), Bc[:], ALU.subtract, [tots, Bc], [tmpf])
                yield
                act(tmp2, tmp2, AF.Exp, [tmpf], [tmpf])
                yield
                tt("dve", kdT[:], kf[:], tmpf[:], ALU.mult, [kf, tmpf], [kdT])
                yield
                tt("dve", kfl[:], Bc[:], kfl[:], ALU.subtract, [Bc, kfl], [kfl])
                yield
                if dr == 0:
                    cp("dve", Rall[:, :, 0:1], zcol[:, 0:1].unsqueeze(1).to_broadcast([128, NT, 1]), [zcol], [Rall])
                    cp("dve", Rall[:, :, 1:8], Bc[:, :, 15:112:16], [Bc], [Rall])
                else:
                    cp("dve", Rall[:, :, 7:8], zcol[:, 0:1].unsqueeze(1).to_broadcast([128, NT, 1]), [zcol], [Rall])
                    cp("dve", Rall[:, :, 0:7], Bc[:, :, 16:128:16], [Bc], [Rall])
                yield

            def tiles_gen(dr, h, st):
                kfl, qs, qd, kdT, Rall, tots = st["kfl"], st["qs"], st["qd"], st["kdT"], st["Rall"], st["tots"]
                m01 = K["m01_f"] if dr == 0 else K["m01_b"]

                def prep(t, k):
                    te, ks = te8[k], ks8[0]
                    tt("pool", te[:], Rall[:, t, :].unsqueeze(2).to_broadcast([128, 8, 128]), kfl[:, t, :].unsqueeze(1).to_broadcast([128, 8, 128]),
                       ALU.subtract, [Rall, kfl], [te])
                    tef = te[:].rearrange("p a b -> p (a b)")
                    ts("dve", tef, tef, 80.0, ALU.min, [te], [te])
                    act(ks[:].rearrange("p a b -> p (a b)"), tef, AF.Exp, [te], [ks])
                    bat = bank()
                    for i in range(8):
                        mm(bat[:, 16 * i:16 * i + 16], ks[:, i, :], qs[:, t, 16 * i:16 * i + 16], True, True, [ks, qs], [bat])
                    AT = ATs[k]
                    tt("dve", AT[:], bat[:, 0:128], m01[:], ALU.mult, [bat, m01], [AT])
                    bt = bank()
                    btb = bt.ap[:].bitcast(BF16)
                    tr(btb[:, 0:128], kdT[:, t, :], identb[:], [kdT, identb], [bt])
                    kd = kdtm[k]
                    cp("act", kd[:], btb[:, 0:128], [bt], [kd])

                def spart(t, k):
                    AT, kd = ATs[k], kdtm[k]
                    bo = bank()
                    mm(bo[:, 0:128], AT[:], vtm[:, t, h * 128:(h + 1) * 128], True, False, [AT, vtm], [bo])
                    mm(bo[:, 0:128], qd[:, t, :], S16[:], False, True, [qd, S16], [bo])
                    mm(bo[:, 128:256], kd[:], vtm[:, t, h * 128:(h + 1) * 128], True, True, [kd, vtm], [bo])
                    stt(S32[:], S32[:], tots[:, 1, t:t + 1], bo[:, 128:256], ALU.mult, ALU.add, [S32, tots, bo], [S32])
                    if dr == 0:
                        tt("dve", oacc[:, t, h * 128:(h + 1) * 128], bo[:, 0:128], K["ones"][:], ALU.mult, [bo, K["ones"]], R_OACC)
                    else:
                        tt("dve", oacc[:, t, h * 128:(h + 1) * 128], oacc[:, t, h * 128:(h + 1) * 128], bo[:, 0:128], ALU.add, [bo] + R_OACC, R_OACC)
                    cp("act", S16[:], S32[:], [S32], [S16])

                for (t0, nt) in g["seqs"]:
                    si = t0 // nt
                    if g["lat"]:
                        S.dma("sp", S32[:], st_hg[l, dr, h], writes=[S32])
                    else:
                        S.dma("sp", S32[:], zeros_d[:, 0:128], writes=[S32])
                    cp("act", S16[:], S32[:], [S32], [S16])
                    order = list(range(t0, t0 + nt)) if dr == 0 else list(range(t0 + nt - 1, t0 - 1, -1))
                    prep(order[0], 0)
                    yield
                    for idx, t in enumerate(order):
                        if idx + 1 < len(order):
                            prep(order[idx + 1], (idx + 1) % 2)
                        spart(t, idx % 2)
                        yield
                    if not g["lat"]:
                        S.dma("sp", o_st_hg[si, l, dr, h], S32[:], reads=[S32], is_output=True)

            combos = [(dr, h) for dr in range(2) for h in range(4)]
            for _ in pre_gen(combos[0][0], combos[0][1], sets[0]):
                pass
            for i, (dr, h) in enumerate(combos):
                nxt = pre_gen(combos[i + 1][0], combos[i + 1][1], sets[(i + 1) % 2]) if i + 1 < len(combos) else None
                tg_ = tiles_gen(dr, h, sets[i % 2])
                alive_t, alive_n = True, nxt is not None
                while alive_t or alive_n:
                    if alive_t:
                        try:
                            next(tg_)
                        except StopIteration:
                            alive_t = False
                    if alive_n:
                        for _ in range(4):
                            try:
                                next(nxt)
                            except StopIteration:
                                alive_n = False
                                break
            AR.reset()
            self.dbg(f"oaccC_{g['name']}{l}", R_OACC, oacc, [128, NT, 512])
            head_rms_out(l, g, oacc, hg_ng, OFF["cg"], OT[2], R_OT[2])

        def mixer_A(l, g):
            NT, T, seqs, lat = g["NT"], g["T"], g["seqs"], g["lat"]
            AR.reset()
            oacc = OTall.ap[:, 8192:16384].bitcast(F32)[:, 0:NT * 512].rearrange("p (t c) -> p t c", c=512)
            qT = AR.view([4, T], BF16, "a_qT"); kT = AR.view([4, T], BF16, "a_kT"); vT = AR.view([4, T], BF16, "a_vT")
            mark = AR.mark()
            nseq = len(seqs); Ls = T // nseq; Lp = T + 4 * nseq; n = Lp - 4
            CB = []
            for i in range(2):
                CB.append(dict(cpad=AR.view([Lp], F32, f"a_cpad{i}"), cacc=AR.view([Lp], F32, f"a_cacc{i}"), csil=AR.view([Lp], F32, f"a_csil{i}"),
                               rtmp=AR.view([512], F32, f"a_rtmp{i}")))
                S.dma("sp", CB[i]["cpad"][:], zeros_d[:, 0:Lp], writes=[CB[i]["cpad"]])
            cw = AR.view([12, 5], F32, "a_cw")
            S.dma("sp", cw[:], conv_w[l].rearrange("(c p) k -> p c k", p=128), writes=[cw])
            wq_ = {}

            def conv_gen(ci_):
                which, h = ci_ // 4, ci_ % 4
                cb = CB[ci_ % 2]
                cpad, cacc, csil, rtmp = cb["cpad"], cb["cacc"], cb["csil"], cb["rtmp"]
                if h == 0:
                    wq_[which] = load_w(w_kc(w_in[l], which * 512, 512), [8, 512])
                w, wv = wq_[which]
                for tg in range(T // 512):
                    b = proj_fm(w, wv, h * 128, tg * 512, 512)
                    for si, (t0, nt) in enumerate(seqs):
                        lo = max(t0 * 128, tg * 512); hi = min((t0 + nt) * 128, (tg + 1) * 512)
                        if lo < hi:
                            cp("act", cpad[:, lo + 4 * si + 2:hi + 4 * si + 2], b[:, lo - tg * 512:hi - tg * 512], [b], [cpad])
                    yield
                ts("dve", cacc[:, 0:n], cpad[:, 0:n], cw[:, ci_, 0:1], ALU.mult, [cpad, cw], [cacc])
                yield
                for j in range(1, 5):
                    stt(cacc[:, 0:n], cpad[:, j:j + n], cw[:, ci_, j:j + 1], cacc[:, 0:n], ALU.mult, ALU.add, [cpad, cw, cacc], [cacc])
                    yield
                if which == 2:
                    for si, (t0, nt) in enumerate(seqs):
                        act(vT[:, h, t0 * 128:(t0 + nt) * 128], cacc[:, t0 * 128 + 4 * si:(t0 + nt) * 128 + 4 * si], AF.Silu, [cacc], [vT])
                    yield
                else:
                    act(csil[:, 0:n], cacc[:, 0:n], AF.Silu, [cacc], [csil])
                    act(cacc[:, 0:n], csil[:, 0:n], AF.Square, [csil], [cacc])
                    yield
                    for si, (t0, nt) in enumerate(seqs):
                        for p0 in range(t0 * 128, (t0 + nt) * 128, 512):
                            np_ = min(512, (t0 + nt) * 128 - p0)
                            i0 = p0 + 4 * si
                            b = bank()
                            mm(b[:, 0:np_], onesf[:], cacc[:, i0:i0 + np_], True, True, [onesf, cacc], [b])
                            yield
                            act(rtmp[:, 0:np_], b[:, 0:np_], AF.Ln, [b, epsc], [rtmp], bias=epsc[:])
                            act(rtmp[:, 0:np_], rtmp[:, 0:np_], AF.Exp, [rtmp], [rtmp], scale=-0.5)
                            yield
                            dst = (qT if which == 0 else kT)
                            stt(dst[:, h, p0:p0 + np_], csil[:, i0:i0 + np_], (128.0 ** -0.5 if which == 0 else 1.0), rtmp[:, 0:np_],
                                ALU.mult, ALU.mult, [csil, rtmp], [dst])
                            yield

            for c0_ in range(0, 12, 2):
                gens = [conv_gen(c0_), conv_gen(c0_ + 1)]
                while gens:
                    for gg in list(gens):
                        try:
                            next(gg)
                        except StopIteration:
                            gens.remove(gg)
            self.dbg(f"qT_{g['name']}{l}", qT, qT[:], [128, 4, T])
            self.dbg(f"kT_{g['name']}{l}", kT, kT[:], [128, 4, T])
            self.dbg(f"vT_{g['name']}{l}", vT, vT[:], [128, 4, T])
            AR.release(mark)
            AB = AR.view([NT, 16], F32, "a_AB"); Gg = AR.view([NT, 8], F32, "a_G"); Lnb = AR.view([NT, 8], F32, "a_lnb")
            Beta = AR.view([NT, 8], F32, "a_beta"); PT_ = AR.view([NT, 2, 24], F32, "a_PT")
            dtb = AR.view([8], F32, "a_dtb"); negA = AR.view([8], F32, "a_negA")
            S.dma("sp", dtb[:], dt_b[l:l + 1, :].partition_broadcast(128).rearrange("p o d -> p (o d)"), writes=[dtb])
            S.dma("sp", negA[:], a_log[l:l + 1, :].partition_broadcast(128).rearrange("p o d -> p (o d)"), writes=[negA])
            act(negA[:], negA[:], AF.Exp, [negA], [negA])
            ts("dve", negA[:], negA[:], -1.0, ALU.mult, [negA], [negA])
            wab, wabv = load_w(w_kc(w_in[l], OFF["aa"], 16), [8, 16])
            for t in range(NT):
                b = proj_tm(wab, wabv, 0, 16, t)
                cp("act", AB[:, t, :], b[:, 0:16], [b], [AB])
            tt("dve", Gg[:], AB[:, :, 0:8], dtb[:].unsqueeze(1).to_broadcast([128, NT, 8]), ALU.add, [AB, dtb], [Gg])
            act(Gg[:], Gg[:], AF.Exp, [Gg], [Gg])
            act(Gg[:], Gg[:], AF.Ln, [Gg, onecol], [Gg], bias=onecol[:])
            tt("dve", Gg[:], Gg[:], negA[:].unsqueeze(1).to_broadcast([128, NT, 8]), ALU.mult, [Gg, negA], [Gg])
            act(Lnb[:], AB[:, :, 8:16], AF.Exp, [AB], [Lnb], scale=-1.0)
            act(Lnb[:], Lnb[:], AF.Ln, [Lnb, onecol], [Lnb], bias=onecol[:])
            ts("dve", Lnb[:], Lnb[:], -1.0, ALU.mult, [Lnb], [Lnb])
            act(Beta[:], Lnb[:], AF.Exp, [Lnb], [Beta])
            for t in range(NT):
                for dr in range(2):
                    b = bank()
                    gsl = Gg[:, t, dr * 4:(dr + 1) * 4]
                    tri = K["tri2_f"] if dr == 0 else K["tri2_b"]
                    mm(b[:, 0:4], tri[:], gsl, True, True, [tri, Gg], [b])
                    mm(b[:, 4:8], K["ones2"][:], gsl, True, True, [K["ones2"], Gg], [b])
                    mm(b[:, 8:12], K["csel0"][:], gsl, True, True, [K["csel0"], Gg], [b])
                    mm(b[:, 12:16], K["csel1"][:], gsl, True, True, [K["csel1"], Gg], [b])
                    pt = PT_[:, t, dr, :]
                    cp("act", pt[:, 0:4], b[:, 0:4], [b], [PT_])
                    tt("dve", pt[:, 4:8], b[:, 0:4], Lnb[:, t, dr * 4:(dr + 1) * 4], ALU.add, [b, Lnb], [PT_])
                    act(pt[:, 8:12], pt[:, 4:8], AF.Exp, [PT_], [PT_])
                    tt("dve", pt[:, 12:16], b[:, 4:8], pt[:, 0:4], ALU.subtract, [b, PT_], [PT_])
                    act(pt[:, 12:16], pt[:, 12:16], AF.Exp, [PT_], [PT_])
                    act(pt[:, 16:24], b[:, 8:16], AF.Exp, [b], [PT_])
            vb = AR.view([4, 128], BF16, "a_vb"); kbg = AR.view([4, 128], BF16, "a_kbg"); kdec = AR.view([4, 128], BF16, "a_kdec")
            S32 = [AR.view([128], F32, f"a_S32_{h}") for h in range(4)]
            S16 = [AR.view([128], BF16, f"a_S16_{h}") for h in range(4)]
            HB = []
            for i in range(4):
                d_ = {}
                for nm in ("E1", "E2", "E3", "U", "L", "eA", "R0", "R1", "u"):
                    d_[nm] = AR.view([128], F32, f"a_{nm}_{i}")
                for nm in ("AT", "qd", "Mi", "wT", "vn"):
                    d_[nm] = AR.view([128], BF16, f"a_{nm}_{i}")
                d_["P1"], d_["P0"], d_["Q1"], d_["Q0"] = d_["E2"], d_["E1"], d_["eA"], d_["E3"]
                HB.append(d_)

            def head_gen(h, t, dr, pt, tok, nm_incl, nm_str, pm_str):
                hb = HB[h]
                bm = bank()
                g0, g1 = hb["R0"], hb["R1"]
                cp("pool", g0[:], pt[:, h:h + 1].to_broadcast([128, 128]), [PT_], [g0])
                mm(bm[:, 0:128], g0[:], identf[:], True, True, [g0, identf], [bm])
                cp("pool", g1[:], pt[:, 4 + h:5 + h].to_broadcast([128, 128]), [PT_], [g1])
                mm(bm[:, 128:256], g1[:], identf[:], True, True, [g1, identf], [bm])
                mm(bm[:, 256:384], kT[:, h, tok], kT[:, h, tok], True, True, [kT], [bm])
                mm(bm[:, 384:512], kT[:, h, tok], qT[:, h, tok], True, True, [kT, qT], [bm])
                yield
                E1, E2, E3, U, L, eA = hb["E1"], hb["E2"], hb["E3"], hb["U"], hb["L"], hb["eA"]
                stt(E1[:], bm[:, 0:128], pt[:, h:h + 1], nm_incl[:], ALU.subtract, ALU.min, [bm, PT_, nm_incl], [E1])
                stt(E2[:], bm[:, 128:256], pt[:, h:h + 1], nm_str[:], ALU.subtract, ALU.min, [bm, PT_, nm_str], [E2])
                stt(E3[:], bm[:, 0:128], pt[:, 4 + h:5 + h], pm_str[:], ALU.subtract, ALU.max, [bm, PT_, pm_str], [E3])
                yield
                act(E1[:], E1[:], AF.Exp, [E1], [E1])
                act(E2[:], E2[:], AF.Exp, [E2], [E2])
                act(E3[:], E3[:], AF.Exp, [E3], [E3], scale=-1.0)
                yield
                AT = hb["AT"]
                tt("dve", AT[:], bm[:, 384:512], E1[:], ALU.mult, [bm, E1], [AT])
                tt("dve", U[:], bm[:, 256:384], E2[:], ALU.mult, [bm, E2], [U])
                tt("dve", L[:], bm[:, 256:384], E3[:], ALU.mult, [bm, E3], [L])
                yield
                act(eA[:], bm[:, 0:128], AF.Exp, [bm], [eA])
                R0 = hb["R0"]
                tt("pool", R0[:], identf[:], U[:], ALU.subtract, [identf, U], [R0])
                yield
                qd = hb["qd"]
                tt("pool", qd[:], qT[:, h, tok], eA[:], ALU.mult, [qT, eA], [qd])
                Pk, Qk, Rk = L, U, R0
                for lev in range(1, 6):
                    bn = bank()
                    mm(bn[:, 0:128], Qk[:], Pk[:], True, True, [Qk, Pk], [bn])
                    if lev < 5:
                        mm(bn[:, 128:256], Pk[:], Qk[:], True, True, [Pk, Qk], [bn])
                    yield
                    Pn = hb[f"P{lev % 2}"]
                    cp("act", Pn[:], bn[:, 0:128], [bn], [Pn])
                    Qn = None
                    if lev < 5:
                        Qn = hb[f"Q{lev % 2}"]
                        cp("act", Qn[:], bn[:, 128:256], [bn], [Qn])
                    yield
                    mm(bn[:, 256:384], Pn[:], Rk[:], True, True, [Pn, Rk], [bn])
                    yield
                    Rn = hb["Mi"] if lev == 5 else hb[f"R{lev % 2}"]
                    tt("dve", Rn[:], bn[:, 256:384], Rk[:], ALU.add, [bn, Rk], [Rn])
                    Pk, Qk, Rk = Pn, Qn, Rn
                    yield
                Mi = hb["Mi"]
                bu = bank()
                mm(bu[:, 0:128], Mi[:], vb[:, h, :], True, True, [Mi, vb], [bu])
                mm(bu[:, 128:256], kbg[:, h, :], Mi[:], True, True, [kbg, Mi], [bu])
                yield
                u_sb, wT, vn = hb["u"], hb["wT"], hb["vn"]
                cp("act", u_sb[:], bu[:, 0:128], [bu], [u_sb])
                cp("act", wT[:], bu[:, 128:256], [bu], [wT])
                yield
                for j in ((0, 1) if dr == 0 else (1, 0)):
                    ps_ = slice(64 * j, 64 * j + 64)
                    bs = bank()
                    mm(bs[:, 0:128], wT[:], S16[h][:], True, True, [wT, S16[h]], [bs])
                    yield
                    tt("dve", vn[ps_, :], u_sb[ps_, :], bs[ps_, 0:128], ALU.subtract, [u_sb, bs], [vn])
                    yield
                    mm(bs[:, 128:256], qd[:], S16[h][:], True, False, [qd, S16[h]], [bs])
                    mm(bs[:, 128:256], AT[ps_, :], vn[ps_, :], False, True, [AT, vn], [bs])
                    mm(bs[:, 256:384], kdec[ps_, h, :], vn[ps_, :], True, True, [kdec, vn], [bs])
                    yield
                    stt(S32[h][:], S32[h][:], pt[:, 16 + 4 * j + h:17 + 4 * j + h], bs[:, 256:384], ALU.mult, ALU.add,
                        [S32[h], PT_, bs], [S32[h]])
                    if dr == 0:
                        tt("dve", oacc[ps_, t, h * 128:(h + 1) * 128], bs[ps_, 128:256], K["ones"][ps_, :], ALU.mult, [bs, K["ones"]], R_OACC)
                    else:
                        tt("dve", oacc[ps_, t, h * 128:(h + 1) * 128], oacc[ps_, t, h * 128:(h + 1) * 128], bs[ps_, 128:256], ALU.add,
                           [bs] + R_OACC, R_OACC)
                    yield
                    cp("pool", S16[h][:], S32[h][:], [S32[h]], [S16[h]])
                    yield

            for dr in range(2):
                nm_incl = K["nm_f_incl"] if dr == 0 else K["nm_b_incl"]
                nm_str = K["nm_f_str"] if dr == 0 else K["nm_b_str"]
                pm_str = K["pm_f_str"] if dr == 0 else K["pm_b_str"]
                for (t0, nt) in seqs:
                    si = t0 // nt
                    for h in range(4):
                        if lat:
                            S.dma("sp", S32[h][:], st_gdn[l, dr, h], writes=[S32[h]])
                        else:
                            S.dma("sp", S32[h][:], zeros_d[:, 0:128], writes=[S32[h]])
                        cp("act", S16[h][:], S32[h][:], [S32[h]], [S16[h]])
                    order = range(t0, t0 + nt) if dr == 0 else range(t0 + nt - 1, t0 - 1, -1)
                    for t in order:
                        tok = slice(t * 128, (t + 1) * 128)
                        pt = PT_[:, t, dr, :]
                        bkv = bank()
                        bb = bkv.ap[:].bitcast(BF16)
                        for h in range(4):
                            tr(bb[:, h * 128:(h + 1) * 128], kT[:, h, tok], identb[:], [kT, identb], [bkv])
                            tr(bb[:, 512 + h * 128:512 + (h + 1) * 128], vT[:, h, tok], identb[:], [vT, identb], [bkv])
                        kv3 = bb[:, 0:512].rearrange("p (a b) -> p a b", a=4)
                        vv3 = bb[:, 512:1024].rearrange("p (a b) -> p a b", a=4)
                        tt("dve", vb[:], vv3, Beta[:, t, dr * 4:(dr + 1) * 4].unsqueeze(2).to_broadcast([128, 4, 128]), ALU.mult, [bkv, Beta], [vb])
                        tt("dve", kbg[:], kv3, pt[:, 8:12].unsqueeze(2).to_broadcast([128, 4, 128]), ALU.mult, [bkv, PT_], [kbg])
                        tt("dve", kdec[:], kv3, pt[:, 12:16].unsqueeze(2).to_broadcast([128, 4, 128]), ALU.mult, [bkv, PT_], [kdec])
                        gens = [head_gen(h, t, dr, pt, tok, nm_incl, nm_str, pm_str) for h in range(4)]
                        while gens:
                            for gg in list(gens):
                                try:
                                    next(gg)
                                except StopIteration:
                                    gens.remove(gg)
                    if not lat:
                        for h in range(4):
                            S.dma("sp", o_st_gdn[si, l, dr, h], S32[h][:], reads=[S32[h]], is_output=True)
            self.dbg(f"oaccA_{g['name']}{l}", R_OACC, oacc, [128, NT, 512])
            head_rms_out(l, g, oacc, gdn_ng, OFF["az"], OT[0], R_OT[0])

        self._bkr = {}

        def bank_r(lo, hi):
            k = self._bkr.get((lo, hi), lo)
            self._bkr[(lo, hi)] = lo + (k + 1 - lo) % (hi - lo)
            return banks[k]

        def mixer_B(l, g):
            NT, T, seqs, lat = g["NT"], g["T"], g["seqs"], g["lat"]
            lam_init = 0.8 - 0.6 * math.exp(-0.3 * l)
            AR.reset()
            qT = AR.view([4, T], BF16, "b_qT"); kT = AR.view([4, T], BF16, "b_kT")
            vx = AR.view([NT, 4, 130], BF16, "b_vx")
            dl = AR.view([4, 64], F32, "b_dl"); lam = AR.view([8], F32, "b_lam")
            S.dma("sp", dl[:].rearrange("p a b -> p (a b)"), dlam[l:l + 1, :].partition_broadcast(128).rearrange("p o d -> p (o d)"), writes=[dl])
            tt("dve", dl[:, 0, :], dl[:, 0, :], dl[:, 1, :], ALU.mult, [dl], [dl])
            tt("dve", dl[:, 2, :], dl[:, 2, :], dl[:, 3, :], ALU.mult, [dl], [dl])
            S.op("dve", lambda e: e.reduce_sum(out=lam[:, 0:1], in_=dl[:, 0, :], axis=AX.X), [dl], [lam])
            S.op("dve", lambda e: e.reduce_sum(out=lam[:, 1:2], in_=dl[:, 2, :], axis=AX.X), [dl], [lam])
            act(lam[:, 0:2], lam[:, 0:2], AF.Exp, [lam], [lam])
            tt("dve", lam[:, 2:3], lam[:, 0:1], lam[:, 1:2], ALU.subtract, [lam], [lam])
            ts("dve", lam[:, 3:4], lam[:, 2:3], lam_init, ALU.add, [lam], [lam], s2=-1.0, op1=ALU.mult)
            for t in range(NT):
                for h in range(4):
                    S.dma("pool", vx[:, t, h, 128:130], cdram["ones"][:, 0:2], writes=[vx])
            if lat:
                ckT = AR.view([4, 512], BF16, "b_ckT")
                cvx = AR.view([4, 4, 130], BF16, "b_cvx")
            markb = AR.mark()
            stg = [AR.view([512], F32, f"b_stg{i}") for i in range(2)]
            rq = [AR.view([512], BF16, f"b_rq{i}") for i in range(2)]
            rt = [AR.view([256], F32, f"b_rt{i}") for i in range(4)]
            wq, wqv = load_w(w_kc(w_in[l], OFF["bq"], 512), [8, 512])
            wk, wkv = load_w(w_kc(w_in[l], OFF["bk"], 512), [8, 512])
            wv_, wvv = load_w(w_kc(w_in[l], OFF["bv"], 512), [8, 512])
            for t in range(NT):
                si = [i for i, (t0, nt) in enumerate(seqs) if t0 <= t < t0 + nt][0]
                pos0 = (t - seqs[si][0]) * 128
                for qi, (w, wv, dst) in enumerate(((wq, wqv, qT), (wk, wkv, kT))):
                    b = proj_tm(w, wv, 0, 512, t)
                    r_ = rq[qi]
                    if lat:
                        x4 = b[:, :].rearrange("p (a c d) -> p a c d", a=8, c=2)
                        o4 = r_[:].rearrange("p (a c d) -> p a c d", a=8, c=2)
                        cosb = K["rope_cos"][:, t, :].unsqueeze(1).to_broadcast([128, 8, 32])
                        sinb = K["rope_sin"][:, t, :].unsqueeze(1).to_broadcast([128, 8, 32])
                        r3 = [x[:].rearrange("p (a d) -> p a d", a=8) for x in rt]
                        tt("dve", r3[0], x4[:, :, 0, :], cosb, ALU.mult, [b, K["rope_cos"]], [rt[0]])
                        tt("dve", r3[1], x4[:, :, 1, :], sinb, ALU.mult, [b, K["rope_sin"]], [rt[1]])
                        tt("pool", o4[:, :, 0, :], r3[0], r3[1], ALU.subtract, [rt[0], rt[1]], [r_])
                        tt("dve", r3[2], x4[:, :, 1, :], cosb, ALU.mult, [b, K["rope_cos"]], [rt[2]])
                        tt("dve", r3[3], x4[:, :, 0, :], sinb, ALU.mult, [b, K["rope_sin"]], [rt[3]])
                        tt("pool", o4[:, :, 1, :], r3[2], r3[3], ALU.add, [rt[2], rt[3]], [r_])
                    else:
                        cp("act", r_[:], b[:, :], [b], [r_])
                        if qi == 1:
                            sg_ = stg[0]
                            cp("dve", sg_[:], b[:, :], [b], [sg_])
                            S.dma("sp", o_ck_b[si, l, :, pos0:pos0 + 128, :].rearrange("h t d -> t h d"),
                                  sg_[:].rearrange("p (h d) -> p h d", h=4), reads=[sg_], is_output=True)
                    bt = bank()
                    bb = bt.ap[:].bitcast(BF16)
                    for h in range(4):
                        tr(bb[:, h * 128:(h + 1) * 128], r_[:, h * 128:(h + 1) * 128], identb[:], [r_, identb], [bt])
                    cp("act", dst[:, :, t * 128:(t + 1) * 128], bb[:, 0:512].rearrange("p (a b) -> p a b", a=4), [bt], [dst])
                b = proj_tm(wv_, wvv, 0, 512, t)
                cp("act", vx[:, t, :, 0:128], b[:, :].rearrange("p (h d) -> p h d", h=4), [b], [vx])
                if not lat:
                    sg_ = stg[1]
                    cp("dve", sg_[:], b[:, :], [b], [sg_])
                    S.dma("sp", o_cv_b[si, l, :, pos0:pos0 + 128, :].rearrange("h t d -> t h d"),
                          sg_[:].rearrange("p (h d) -> p h d", h=4), reads=[sg_], is_output=True)
            ckeys = []
            if lat:
                ckb = AR.view([4, 4, 128], BF16, "b_ckb")
                for h in range(4):
                    S.dma("pool", ckb[:, h, :, :], ck_b[l, h].rearrange("(kt p) d -> p kt d", p=128), writes=[ckb])
                for kt in range(4):
                    S.dma("pool", cvx[:, kt, :, 0:128], cv_b[l][:, kt * 128:(kt + 1) * 128, :].rearrange("h p d -> p h d"), writes=[cvx])
                for kt in range(4):
                    for h in range(4):
                        S.dma("pool", cvx[:, kt, h, 128:130], cdram["ones"][:, 0:2], writes=[cvx])
                for h in range(4):
                    bt = bank()
                    bb = bt.ap[:].bitcast(BF16)
                    for kt in range(4):
                        tr(bb[:, kt * 128:(kt + 1) * 128], ckb[:, h, kt, :], identb[:], [ckb, identb], [bt])
                    cp("act", ckT[:, h, :], bb[:, 0:512], [bt], [ckT])
            AR.release(markb)
            PTb = [AR.view([512], BF16, f"b_PT{i}") for i in range(4)]
            ostg = AR.view([4, 512], F32, "b_ostg")
            rr = AR.view([16], F32, "b_rr")
            ngb = AR.view([128], F32, "b_ngb")
            S.dma("sp", ngb[:], diff_ng[l:l + 1, :].partition_broadcast(128).rearrange("p o d -> p (o d)"), writes=[ngb])
            ts("dve", ngb[:], ngb[:], float(1.0 - lam_init), ALU.mult, [ngb], [ngb])
            gzs = [AR.view([4, 128], F32, f"b_gz{i}") for i in range(2)]
            obs = [AR.view([512], BF16, f"b_ob{i}") for i in range(2)]
            pti = 0
            for (t0, nt) in seqs:
                keys = []
                if lat:
                    for kt in range(4):
                        keys.append(("c", kt))
                for t in range(t0, t0 + nt):
                    keys.append(("l", t))
                for qg0 in range(t0, t0 + nt, 4):
                    nqt = min(4, t0 + nt - qg0)
                    nq = nqt * 128
                    for h in range(4):
                        accs = [[banks[0], banks[1]], [banks[2], banks[3]]]
                        items = [(ki, kind, kx, m) for ki, (kind, kx) in enumerate(keys) for m in range(2)]

                        def b_score(it):
                            ki, kind, kx, m = it
                            pb = slice(64 * m, 64 * m + 64)
                            bsc = bank_r(4, 8)
                            if kind == "c":
                                kTap, kr = ckT[pb, h, kx * 128:(kx + 1) * 128], ckT
                            else:
                                kTap, kr = kT[pb, h, kx * 128:(kx + 1) * 128], kT
                            mm(bsc[:, 0:nq], kTap, qT[pb, h, qg0 * 128:qg0 * 128 + nq], True, True, [kr, qT], [bsc])
                            return bsc

                        def b_rest(it, bsc):
                            nonlocal pti
                            ki, kind, kx, m = it
                            if kind == "c":
                                vap, vr = cvx[:, kx, h, 0:129], cvx
                            else:
                                vap, vr = vx[:, kx, h, 0:129], vx
                            PT_ = PTb[pti % 4]; pti += 1
                            act(PT_[:, 0:nq], bsc[:, 0:nq], AF.Exp, [bsc], [PT_], scale=0.125)
                            for j in range(nqt):
                                ab = accs[m][j // 2]
                                mm(ab[:, (j % 2) * 256:(j % 2) * 256 + 129], PT_[:, j * 128:(j + 1) * 128], vap, ki == 0 and j % 2 == 0,
                                   ki == len(keys) - 1 and (j % 2 == 1 or j == nqt - 1), [PT_, vr], [ab])

                        LA = 3
                        pend = [b_score(it) for it in items[:LA]]
                        for idx, it in enumerate(items):
                            if idx + LA < len(items):
                                pend.append(b_score(items[idx + LA]))
                            b_rest(it, pend.pop(0))
                        for j in range(nqt):
                            a0 = accs[0][j // 2]; a1 = accs[1][j // 2]; c0 = (j % 2) * 256
                            S.op("dve", lambda e, o=rr[:, 2 * j:2 * j + 1], i=a0[:, c0 + 128:c0 + 129]: e.reciprocal(o, i), [a0], [rr])
                            S.op("dve", lambda e, o=rr[:, 2 * j + 1:2 * j + 2], i=a1[:, c0 + 128:c0 + 129]: e.reciprocal(o, i), [a1], [rr])
                            tt("dve", rr[:, 2 * j + 1:2 * j + 2], rr[:, 2 * j + 1:2 * j + 2], lam[:, 3:4], ALU.mult, [rr, lam], [rr])
                            od = ostg[:, j, h * 128:(h + 1) * 128]
                            ts("dve", od, a0[:, c0:c0 + 128], rr[:, 2 * j:2 * j + 1], ALU.mult, [a0, rr], [ostg])
                            stt(od, a1[:, c0:c0 + 128], rr[:, 2 * j + 1:2 * j + 2], od, ALU.mult, ALU.add, [a1, rr, ostg], [ostg])
                    for j in range(nqt):
                        t = qg0 + j
                        st = small(); junk = xn[t % 2]; gz = gzs[t % 2]; ob = obs[t % 2]
                        for h in range(4):
                            act(junk[:, h * 128:(h + 1) * 128], ostg[:, j, h * 128:(h + 1) * 128], AF.Square, [ostg], [junk, st], accum=st[:, h:h + 1])
                        act(st[:, 4:8], st[:, 0:4], AF.Ln, [st, epsc], [st], bias=epsc[:], scale=1.0 / 128)
                        act(st[:, 4:8], st[:, 4:8], AF.Exp, [st], [st], scale=-0.5)
                        o3 = ostg[:, j, :].rearrange("p (a b) -> p a b", a=4)
                        tt("dve", gz[:], o3, st[:, 4:8].unsqueeze(2).to_broadcast([128, 4, 128]), ALU.mult, [ostg, st], [gz])
                        tt("dve", ob[:].rearrange("p (a b) -> p a b", a=4), gz[:], ngb[:].unsqueeze(1).to_broadcast([128, 4, 128]), ALU.mult, [gz, ngb], [ob])
                        bt = bank_r(4, 8)
                        bb = bt.ap[:].bitcast(BF16)
                        for kc in range(4):
                            tr(bb[:, kc * 128:(kc + 1) * 128], ob[:, kc * 128:(kc + 1) * 128], identb[:], [ob, identb], [bt])
                        cp("act", OT[1][:, :, t * 128:(t + 1) * 128], bb[:, 0:512].rearrange("p (a b) -> p a b", a=4), [bt], R_OT[1])

        def mixer_D(l, g):
            NT, T, seqs, lat = g["NT"], g["T"], g["seqs"], g["lat"]
            AR.reset()
            qT = AR.view([4, T], BF16, "d_qT"); kT = AR.view([4, T], BF16, "d_kT")
            vx = AR.view([NT, 8, 66], BF16, "d_vx")
            oall = AR.view([NT, 512], BF16, "d_oall")
            if lat:
                ckT = AR.view([4, 512], BF16, "d_ckT")
                cvx = AR.view([4, 8, 66], BF16, "d_cvx")
            markd = AR.mark()
            ones16 = cdram["ones"][:, 0:16].rearrange("p (h c) -> p h c", c=2)
            for t in range(NT):
                S.dma("pool", vx[:, t, :, 64:66], ones16, writes=[vx])
            stg = [AR.view([512], F32, f"d_stg{i}") for i in range(2)]
            wq, wqv = load_w(w_kc(w_in[l], OFF["dq"], 512), [8, 512])
            wk, wkv = load_w(w_kc(w_in[l], OFF["dk"], 512), [8, 512])
            wv_, wvv = load_w(w_kc(w_in[l], OFF["dv"], 512), [8, 512])
            for c in range(4):
                for tg in range(T // 512):
                    b = proj_fm(wq, wqv, c * 128, tg * 512, 512)
                    cp("act", qT[:, c, tg * 512:(tg + 1) * 512], b[:, :], [b], [qT])
                    b = proj_fm(wk, wkv, c * 128, tg * 512, 512)
                    cp("dve", kT[:, c, tg * 512:(tg + 1) * 512], b[:, :], [b], [kT])
            for t in range(NT):
                si = [i for i, (t0, nt) in enumerate(seqs) if t0 <= t < t0 + nt][0]
                pos0 = (t - seqs[si][0]) * 128
                b = proj_tm(wv_, wvv, 0, 512, t)
                cp("act", vx[:, t, :, 0:64], b[:, :].rearrange("p (h d) -> p h d", h=8), [b], [vx])
                if not lat:
                    sg_ = stg[0]
                    cp("dve", sg_[:], b[:, :], [b], [sg_])
                    S.dma("sp", o_cv_d[si, l, :, pos0:pos0 + 128, :].rearrange("h t d -> t h d"),
                          sg_[:].rearrange("p (h d) -> p h d", h=8), reads=[sg_], is_output=True)
                    b2 = proj_tm(wk, wkv, 0, 512, t)
                    sg2 = stg[1]
                    cp("act", sg2[:], b2[:, :], [b2], [sg2])
                    S.dma("sp", o_ck_d[si, l, :, pos0:pos0 + 128, :].rearrange("h t d -> t h d"),
                          sg2[:].rearrange("p (h d) -> p h d", h=8), reads=[sg2], is_output=True)
            if lat:
                ckd = AR.view([4, 8, 64], BF16, "d_ckd")
                for h in range(8):
                    S.dma("pool", ckd[:, :, h, :], ck_d[l, h].rearrange("(kt p) d -> p kt d", p=128), writes=[ckd])
                for kt in range(4):
                    S.dma("pool", cvx[:, kt, :, 0:64], cv_d[l][:, kt * 128:(kt + 1) * 128, :].rearrange("h p d -> p h d"), writes=[cvx])
                    S.dma("pool", cvx[:, kt, :, 64:66], ones16, writes=[cvx])
                for c in range(4):
                    bt = bank()
                    bb = bt.ap[:].bitcast(BF16)
                    for kt in range(4):
                        tr(bb[:, kt * 128:(kt + 1) * 128], ckd[:, kt, 2 * c:2 * c + 2, :].rearrange("p a b -> p (a b)"), identb[:], [ckd, identb], [bt])
                    cp("act", ckT[:, c, :], bb[:, 0:512], [bt], [ckT])
                R1 = AR.view([31], F32, "d_R1"); fpad = AR.view([127], F32, "d_fpad")
                S.dma("sp", R1[0:120, :], rpb[l], writes=[R1])
                S.dma("sp", fpad[:], zeros_d[:, 0:127], writes=[fpad])
                r1ap = R1[0:120, :]
                rev = bass.AP(r1ap.tensor, r1ap[:, 30:31].offset, [list(r1ap.ap[0]), [-1, 31]])
                cp("dve", fpad[0:120, 48:79], rev, [R1, fpad], [fpad])
                S.dma("sp", rpb_scr, fpad[0:120, :].unsqueeze(1).to_broadcast([120, 64, 127]), reads=[fpad], writes=[self.rpb_scr_res])
            AR.release(markd)
            rr = AR.view([8], F32, "d_rr")
            PTc = [AR.view([512], BF16, f"d_PTc{i}") for i in range(2)]
            if lat:
                toep = [AR.view([7, 128], F32, f"d_toep{i}") for i in range(2)]
                comb = [AR.view([128], BF16, f"d_comb{i}") for i in range(10)]
                PTw = [AR.view([512], BF16, f"d_PTw{i}") for i in range(4)]
                ci_ = 0; pw_ = 0
                units = []
                for h in range(8):
                    tp = toep[h % 2]
                    for j in range(8):
                        units.append((h, j, tp))

                def d_score(u):
                    nonlocal ci_
                    h, j, tp = u

                    def load_toep(hh):
                        tq = toep[hh % 2]
                        for di, dl_ in enumerate(range(-3, 4)):
                            for a in range(2):
                                for b_ in range(2):
                                    dr = 2 * dl_ + a - b_ + 7
                                    assert 0 <= dr <= 14
                                    src = bass.AP(rpb_scr.tensor, (hh * 15 + dr) * 64 * 127 + 63, [[126, 64], [1, 64]])
                                    S.dma("sp", tq[a * 64:(a + 1) * 64, di, b_ * 64:(b_ + 1) * 64], src, reads=[self.rpb_scr_res], writes=[tq])
                    if j == 0 and h == 0:
                        load_toep(0)
                    if j == 2 and h + 1 < 8:
                        load_toep(h + 1)
                    c = h // 2; pb = slice(64 * (h % 2), 64 * (h % 2) + 64)
                    pairs_j = [(pi, i) for pi, (jj, i) in enumerate(NA_PAIRS) if jj == j]
                    bsc = bank_r(2, 8)
                    for kt in range(4):
                        mm(bsc[:, kt * 128:(kt + 1) * 128], ckT[pb, c, kt * 128:(kt + 1) * 128], qT[pb, c, j * 128:(j + 1) * 128], True, True, [ckT, qT], [bsc])
                    nw = len(pairs_j)
                    bws = [bank_r(2, 8)] + ([bank_r(2, 8)] if nw > 4 else [])
                    for idx, (pi, i) in enumerate(pairs_j):
                        cb = comb[ci_ % len(comb)]; ci_ += 1
                        stt(cb[:], tp[:, i - j + 3, :], 8.0, namask[:, pi, :], ALU.mult, ALU.add, [tp, namask], [cb])
                        bw = bws[idx // 4]; col = (idx % 4) * 128
                        mm(bw[:, col:col + 128], kT[pb, c, i * 128:(i + 1) * 128], qT[pb, c, j * 128:(j + 1) * 128], True, False, [kT, qT], [bw])
                        mm(bw[:, col:col + 128], identb[:], cb[:], False, True, [identb, cb], [bw])
                    return (bsc, bws, pairs_j)

                def d_rest(u, sc):
                    nonlocal pw_
                    h, j, tp = u
                    bsc, bws, pairs_j = sc
                    nw = len(pairs_j)
                    acc = banks[(h * 8 + j) % 2]
                    ptc = PTc[j % 2]
                    act(ptc[:], bsc[:, :], AF.Exp, [bsc], [ptc], scale=0.125)
                    pws = []
                    for bi, bw in enumerate(bws):
                        ncol = min(4, nw - 4 * bi) * 128
                        pw = PTw[pw_ % 4]; pw_ += 1
                        act(pw[:, 0:ncol], bw[:, 0:ncol], AF.Exp, [bw], [pw], scale=0.125)
                        pws.append(pw)
                    nk = 4 + nw; kk = 0
                    for kt in range(4):
                        mm(acc[:, 0:65], ptc[:, kt * 128:(kt + 1) * 128], cvx[:, kt, h, 0:65], kk == 0, kk == nk - 1, [ptc, cvx], [acc]); kk += 1
                    for idx, (pi, i) in enumerate(pairs_j):
                        pw = pws[idx // 4]; col = (idx % 4) * 128
                        mm(acc[:, 0:65], pw[:, col:col + 128], vx[:, i, h, 0:65], kk == 0, kk == nk - 1, [pw, vx], [acc]); kk += 1
                    S.op("dve", lambda e, o=rr[:, j:j + 1], i_=acc[:, 64:65]: e.reciprocal(o, i_), [acc], [rr])
                    ts("dve", oall[:, j, h * 64:(h + 1) * 64], acc[:, 0:64], rr[:, j:j + 1], ALU.mult, [acc, rr], [oall])

                prev = d_score(units[0])
                for ui, u in enumerate(units):
                    nxt = d_score(units[ui + 1]) if ui + 1 < len(units) else None
                    d_rest(u, prev)
                    prev = nxt
            else:
                for (t0, nt) in seqs:
                    for h in range(8):
                        c = h // 2; pb = slice(64 * (h % 2), 64 * (h % 2) + 64)
                        bsc = bank_r(2, 8)
                        for kt in range(2):
                            mm(bsc[:, kt * 256:(kt + 1) * 256], kT[pb, c, (t0 + kt) * 128:(t0 + kt + 1) * 128], qT[pb, c, t0 * 128:(t0 + 2) * 128], True, True, [kT, qT], [bsc])
                        ptc = PTc[h % 2]
                        act(ptc[:], bsc[:, :], AF.Exp, [bsc], [ptc], scale=0.125)
                        acc = banks[h % 2]
                        for jq in range(2):
                            for kt in range(2):
                                mm(acc[:, jq * 128:jq * 128 + 65], ptc[:, kt * 256 + jq * 128:kt * 256 + (jq + 1) * 128], vx[:, t0 + kt, h, 0:65], kt == 0, kt == 1, [ptc, vx], [acc])
                        for jq in range(2):
                            S.op("dve", lambda e, o=rr[:, jq:jq + 1], i_=acc[:, jq * 128 + 64:jq * 128 + 65]: e.reciprocal(o, i_), [acc], [rr])
                            ts("dve", oall[:, t0 + jq, h * 64:(h + 1) * 64], acc[:, jq * 128:jq * 128 + 64], rr[:, jq:jq + 1], ALU.mult, [acc, rr], [oall])
            for t in range(NT):
                bt = bank_r(2, 8)
                bb = bt.ap[:].bitcast(BF16)
                for kc in range(4):
                    tr(bb[:, kc * 128:(kc + 1) * 128], oall[:, t, kc * 128:(kc + 1) * 128], identb[:], [oall, identb], [bt])
                cp("act", OT[3][:, :, t * 128:(t + 1) * 128], bb[:, 0:512].rearrange("p (a b) -> p a b", a=4), [bt], R_OT[3])

        def mixers(l, g):
            sel = "ACBD"
            if "A" in sel:
                try:
                    mixer_A(l, g)
                except _Stop:
                    pass
            if "C" in sel:
                mixer_C(l, g)
            if "B" in sel:
                mixer_B(l, g)
            if "D" in sel:
                mixer_D(l, g)
        self.mixers = mixers

        self.h = dict(locals())
        groups = [dict(name="p", NT=4, T=512, seqs=[(0, 2), (2, 2)], lat=False, xd=xp, yd=ys if False else yp, cond=c_ctx),
                  dict(name="s", NT=8, T=1024, seqs=[(0, 8)], lat=True, xd=xs, yd=ys, cond=c_s)]
        if self.stage >= 1:
            compute_mod_all()
        for gi_, g in enumerate(groups):
            NT = g["NT"]
            S.dma("sp", X[:, 0:NT, :], g["xd"].rearrange("(t p) d -> p t d", p=128), writes=[X])
            for l in range(DEPTH):
                if self.stage < 1:
                    break
                self.cm = MOD[l][gi_]
                norm_to_hT(NT, S1[l][gi_], self.cm[:, 0:8], self.cm)
                self.dbg(f"hT_{g['name']}{l}", HT, HT[:, :, 0:g["T"]], [128, 8, g["T"]])
                if self.stage < 2:
                    break
                if self.stage >= 4:
                    self.mixers(l, g)
                    self.dbg(f"OT_{g['name']}{l}", list(RS), OTall.ap[:, :], [128, 16384])
                if self.stage >= 5:
                    merge_out(l, NT, g["T"])
                    self.dbg(f"x1_{g['name']}{l}", X, X[:, 0:NT, :], [128, NT, D])
                norm_to_hT(NT, S2[l][gi_], self.cm[:, 24:32], self.cm)
                ffn(l, NT, g["T"])
                self.dbg(f"x2_{g['name']}{l}", X, X[:, 0:NT, :], [128, NT, D])
            AR.reset()
            final_norm(NT, g["yd"])
            AR.reset()
        S.emit_ctx = nc.allow_non_contiguous_dma(reason="small strided constant loads")
        with S.emit_ctx:
            S.emit()
        return self


def _core_inputs(inp, c, consts, na_mask):
    f = lambda a: np.ascontiguousarray(a, dtype=np.float32)
    m = {
        "xp": f(inp["x_prompt"][2 * c:2 * c + 2].reshape(512, D)), "xs": f(inp["x_sample"][c]),
        "c": f(inp["c"][c:c + 1]), "c_ctx": f(inp["c_ctx"].reshape(1, D)),
        "state_gdn": f(inp["state_gdn"][c]), "state_hgrn": f(inp["state_hgrn"][c]),
        "cache_diff_k": f(inp["cache_diff_k"][c]), "cache_diff_v": f(inp["cache_diff_v"][c]),
        "cache_na_k": f(inp["cache_na_k"][c]), "cache_na_v": f(inp["cache_na_v"][c]),
        "w_ada": f(inp["w_ada"]), "b_ada": f(inp["b_ada"]), "norm1_g": f(inp["norm1_g"]), "norm2_g": f(inp["norm2_g"]),
        "final_norm_g": f(inp["final_norm_g"].reshape(1, D)), "w_in": f(inp["w_in"]), "gdn_conv_w": f(inp["gdn_conv_w"]),
        "gdn_A_log": f(inp["gdn_A_log"].reshape(2, 8)), "gdn_dt_bias": f(inp["gdn_dt_bias"].reshape(2, 8)),
        "gdn_norm_g": f(inp["gdn_norm_g"]), "diff_lambda": f(inp["diff_lambda"].reshape(2, 256)),
        "diff_norm_g": f(inp["diff_norm_g"]), "hgrn_lb_logits": f(inp["hgrn_lb_logits"]), "hgrn_norm_g": f(inp["hgrn_norm_g"]),
        "na_rpb": f(inp["na_rpb"].reshape(2, 120, 31)), "w_branch": f(inp["w_branch"]), "w_out": f(inp["w_out"]),
        "w_ffn_gate": f(inp["w_ffn_gate"]), "w_ffn_up": f(inp["w_ffn_up"]), "w_ffn_down": f(inp["w_ffn_down"]),
        "k_na_mask": na_mask, "k_zeros": np.zeros((128, 2048), np.float32),
    }
    for k, v in consts.items():
        m["k_" + k] = v
    return m


_PROG = {}


def _run(inputs, cores, debug=(), stage=99):
    key = (tuple(sorted(debug)), stage)
    if key not in _PROG:
        _PROG[key] = Prog(debug, stage).build()
    P = _PROG[key]
    consts = _consts()
    na_mask = _na_masks()
    in_maps = [_core_inputs(inputs, c, consts, na_mask) for c in cores]
    res = run_bass_kernel_spmd(P.nc, in_maps, core_ids=list(range(len(cores))))
    return P, res.results


def kernel(**inputs):
    P, rs = _run(inputs, list(range(8)))
    cat = lambda k: np.concatenate([r[k] for r in rs], axis=0)
    y_prompt = cat("yp").reshape(16, 256, D)
    y_sample = np.stack([r["ys"] for r in rs], 0)
    return (y_prompt.astype(np.float32), y_sample.astype(np.float32), cat("o_st_gdn"), cat("o_ck_b"), cat("o_cv_b"),
            cat("o_st_hg"), cat("o_ck_d"), cat("o_cv_d"))
```
